# Optimizing a Trainium2 kernel written in Bass

```python
import math
import jax
import jax.numpy as jnp
from jax import lax
import numpy as np

D_MODEL = 1024
BATCH = 16
SEQ = 4096
DEPTH = 1

ATT_HEADS = 16
ATT_KV_HEADS = 4
ATT_HEAD_DIM = 64
ATT_W = ATT_HEADS * ATT_HEAD_DIM
KV_W = ATT_KV_HEADS * ATT_HEAD_DIM
WINDOW = 128
BLOCK = 128
ROPE_THETA = 10000.0

SSD_HEADS = 8
SSD_HEAD_DIM = 64
SSD_W = SSD_HEADS * SSD_HEAD_DIM
SSD_GROUPS = 2
SSD_HPG = SSD_HEADS // SSD_GROUPS
SSD_STATE = 128
SSD_CONV = 5
CHUNK = 128
N_DIR = 2
XBC_W = SSD_W + 2 * SSD_GROUPS * SSD_STATE

MEM_LEN = 256
MEM_HEADS = 4
MEM_HEAD_DIM = 128
MEM_W = MEM_HEADS * MEM_HEAD_DIM

MIX_W = ATT_W + SSD_W + MEM_W
SPLIT_SIZES = (ATT_W, KV_W, KV_W, ATT_W, SSD_W, XBC_W, N_DIR * SSD_HEADS, MEM_W, MEM_W)
IN_W = ATT_W + 2 * KV_W + ATT_W + SSD_W + XBC_W + N_DIR * SSD_HEADS + 2 * MEM_W
EPS = 1e-6

kernel_name = "hymba_style_bidir_swa_ssd_memxattn_layer"


def _split_points():
    pts, acc = [], 0
    for s in SPLIT_SIZES[:-1]:
        acc += s
        pts.append(acc)
    return pts


def rms_norm(x, w):
    xf = x.astype(jnp.float32)
    y = xf * lax.rsqrt(jnp.mean(xf * xf, axis=-1, keepdims=True) + EPS)
    return (y * w.astype(jnp.float32)).astype(x.dtype)


def gated_group_rmsnorm(y, z, w):
    g = (y * jax.nn.silu(z)).astype(jnp.float32)
    b, l, c = g.shape
    g = g.reshape(b, l, SSD_GROUPS, c // SSD_GROUPS)
    g = g * lax.rsqrt(jnp.mean(g * g, axis=-1, keepdims=True) + EPS)
    return (g.reshape(b, l, c) * w.astype(jnp.float32)).astype(y.dtype)


def rotary(t, cos, sin):
    t1, t2 = jnp.split(t, 2, axis=-1)
    c = cos[None, :, None, :].astype(t.dtype)
    s = sin[None, :, None, :].astype(t.dtype)
    return jnp.concatenate([t1 * c - t2 * s, t1 * s + t2 * c], axis=-1)


def window_attention(q, k, v, sink):
    b, l, h, dh = q.shape
    kvh = k.shape[2]
    r = h // kvh
    nb = l // BLOCK
    span = BLOCK + 2 * WINDOW
    pad = ((0, 0), (WINDOW, WINDOW), (0, 0), (0, 0))
    kp = jnp.pad(k, pad)
    vp = jnp.pad(v, pad)
    qg = q.reshape(b, l, kvh, r, dh)
    scale = dh ** -0.5
    sink_f = sink.astype(jnp.float32).reshape(1, kvh, r, 1, 1)

    def one_block(i):
        start = i * BLOCK
        qb = lax.dynamic_slice_in_dim(qg, start, BLOCK, axis=1)
        kb = lax.dynamic_slice_in_dim(kp, start, span, axis=1)
        vb = lax.dynamic_slice_in_dim(vp, start, span, axis=1)
        s = jnp.einsum('bqkrd,bskd->bkrqs', qb, kb).astype(jnp.float32) * scale
        qpos = start + jnp.arange(BLOCK)
        kpos = start - WINDOW + jnp.arange(span)
        valid = ((jnp.abs(qpos[:, None] - kpos[None, :]) <= WINDOW)
                 & (kpos >= 0)[None, :] & (kpos < l)[None, :])
        s = jnp.where(valid, s, -jnp.inf)
        sink_col = jnp.broadcast_to(sink_f, s.shape[:-1] + (1,))
        p = jax.nn.softmax(jnp.concatenate([s, sink_col], axis=-1), axis=-1)[..., :-1]
        o = jnp.einsum('bkrqs,bskd->bqkrd', p.astype(vb.dtype), vb)
        return o.reshape(b, BLOCK, h, dh)

    out = lax.map(one_block, jnp.arange(nb))
    return out.transpose(1, 0, 2, 3, 4).reshape(b, l, h, dh)


def centred_depthwise_conv(u, w, bias):
    k = w.shape[0]
    y = lax.conv_general_dilated(
        u, w[:, None, :].astype(u.dtype), window_strides=(1,),
        padding=[((k - 1) // 2, k // 2)],
        dimension_numbers=('NWC', 'WIO', 'NWC'),
        feature_group_count=u.shape[-1])
    return y + bias.astype(u.dtype)


def ssd_scan(x, dt, a, bm, cm):
    bsz, l, g, r, p = x.shape
    n = bm.shape[-1]
    nc = l // CHUNK
    xdt = (x * dt[..., None]).reshape(bsz, nc, CHUNK, g, r, p)
    adt = (dt.astype(jnp.float32) * a.astype(jnp.float32)).reshape(bsz, nc, CHUNK, g, r)
    bc = bm.reshape(bsz, nc, CHUNK, g, n)
    cc = cm.reshape(bsz, nc, CHUNK, g, n)
    a_cs = jnp.cumsum(adt, axis=2).transpose(0, 1, 3, 4, 2)
    seg = a_cs[..., :, None] - a_cs[..., None, :]
    lower = jnp.tril(jnp.ones((CHUNK, CHUNK), dtype=bool))
    lmat = jnp.exp(jnp.where(lower, seg, -jnp.inf)).astype(x.dtype)
    cb = jnp.einsum('bclgn,bcsgn->bcgls', cc, bc)
    y_diag = jnp.einsum('bcgls,bcgrls,bcsgrp->bclgrp', cb, lmat, xdt)
    decay_in = jnp.exp(a_cs[..., -1:] - a_cs).astype(x.dtype)
    states = jnp.einsum('bcsgn,bcgrs,bcsgrp->bcgrpn', bc, decay_in, xdt)
    chunk_decay = jnp.exp(a_cs[..., -1]).astype(x.dtype)

    def step(hstate, inp):
        st, dec = inp
        return hstate * dec[..., None, None] + st, hstate

    h0 = jnp.zeros((bsz, g, r, p, n), dtype=states.dtype)
    _, prev = lax.scan(step, h0, (states.transpose(1, 0, 2, 3, 4, 5),
                                  chunk_decay.transpose(1, 0, 2, 3)))
    prev = prev.transpose(1, 0, 2, 3, 4, 5)
    y_off = jnp.einsum('bclgn,bcgrpn,bcgrl->bclgrp', cc, prev,
                       jnp.exp(a_cs).astype(x.dtype))
    return (y_diag + y_off).reshape(bsz, l, g, r, p)


def setup_inputs(seed: int = 0) -> dict:
    key = jax.random.key(seed)
    ks = jax.random.split(key, 20)
    f32 = jnp.float32
    x = jax.random.normal(ks[0], (BATCH, SEQ, D_MODEL), f32)
    mem = jax.random.normal(ks[1], (BATCH, MEM_LEN, D_MODEL), f32)
    norm_mem_w = 1.0 + 0.02 * jax.random.normal(ks[2], (D_MODEL,), f32)
    norm_in_w = 1.0 + 0.02 * jax.random.normal(ks[3], (DEPTH, D_MODEL), f32)
    w_in = jax.random.normal(ks[4], (DEPTH, D_MODEL, IN_W), f32) * D_MODEL ** -0.5
    attn_sink = 0.5 * jax.random.normal(ks[5], (DEPTH, ATT_HEADS), f32)
    conv_w = jax.random.normal(ks[6], (DEPTH, SSD_CONV, XBC_W), f32) * SSD_CONV ** -0.5
    conv_b = 0.01 * jax.random.normal(ks[7], (DEPTH, XBC_W), f32)
    u = jax.random.uniform(ks[8], (DEPTH, N_DIR, SSD_HEADS), f32)
    dt0 = jnp.exp(u * (math.log(0.1) - math.log(0.001)) + math.log(0.001))
    dt_bias = dt0 + jnp.log(-jnp.expm1(-dt0))
    a_log = jnp.log(jax.random.uniform(ks[9], (DEPTH, N_DIR, SSD_HEADS), f32,
                                       minval=1.0, maxval=16.0))
    d_skip = 1.0 + 0.1 * jax.random.normal(ks[10], (DEPTH, SSD_HEADS), f32)
    ssd_norm_w = 1.0 + 0.02 * jax.random.normal(ks[11], (DEPTH, SSD_W), f32)
    w_mem_kv = jax.random.normal(ks[12], (DEPTH, D_MODEL, 2 * MEM_W), f32) * D_MODEL ** -0.5
    w_out = jax.random.normal(ks[13], (DEPTH, MIX_W, D_MODEL), f32) * MIX_W ** -0.5
    norm_out_w = 1.0 + 0.02 * jax.random.normal(ks[14], (D_MODEL,), f32)
    return {"x": x, "mem": mem, "norm_mem_w": norm_mem_w, "norm_in_w": norm_in_w,
            "w_in": w_in, "attn_sink": attn_sink, "conv_w": conv_w, "conv_b": conv_b,
            "dt_bias": dt_bias, "a_log": a_log, "d_skip": d_skip, "ssd_norm_w": ssd_norm_w,
            "w_mem_kv": w_mem_kv, "w_out": w_out, "norm_out_w": norm_out_w}


def reference(x, mem, norm_mem_w, norm_in_w, w_in, attn_sink, conv_w, conv_b,
              dt_bias, a_log, d_skip, ssd_norm_w, w_mem_kv, w_out, norm_out_w):
    bsz, l, _ = x.shape
    pos = jnp.arange(l, dtype=jnp.float32)
    inv_freq = 1.0 / (ROPE_THETA ** (jnp.arange(0, ATT_HEAD_DIM, 2, dtype=jnp.float32) / ATT_HEAD_DIM))
    ang = pos[:, None] * inv_freq[None, :]
    cos, sin = jnp.cos(ang), jnp.sin(ang)
    memn = rms_norm(mem, norm_mem_w)
    splits = _split_points()

    h = x
    for layer in range(DEPTH):
        hn = rms_norm(h, norm_in_w[layer])
        proj = hn @ w_in[layer]
        q, k, v, g_attn, z, xbc, dt_raw, q_mem, g_mem = jnp.split(proj, splits, axis=-1)

        q = rotary(q.reshape(bsz, l, ATT_HEADS, ATT_HEAD_DIM), cos, sin)
        k = rotary(k.reshape(bsz, l, ATT_KV_HEADS, ATT_HEAD_DIM), cos, sin)
        v = v.reshape(bsz, l, ATT_KV_HEADS, ATT_HEAD_DIM)
        att = window_attention(q, k, v, attn_sink[layer]).reshape(bsz, l, ATT_W)
        att = att * jax.nn.silu(g_attn)

        xbc = jax.nn.silu(centred_depthwise_conv(xbc, conv_w[layer], conv_b[layer]))
        xs, bm, cm = jnp.split(xbc, [SSD_W, SSD_W + SSD_GROUPS * SSD_STATE], axis=-1)
        xs = xs.reshape(bsz, l, SSD_GROUPS, SSD_HPG, SSD_HEAD_DIM)
        bm = bm.reshape(bsz, l, SSD_GROUPS, SSD_STATE)
        cm = cm.reshape(bsz, l, SSD_GROUPS, SSD_STATE)
        dt = jax.nn.softplus(dt_raw.reshape(bsz, l, N_DIR, SSD_GROUPS, SSD_HPG)
                             + dt_bias[layer].reshape(N_DIR, SSD_GROUPS, SSD_HPG).astype(dt_raw.dtype))
        a = -jnp.exp(a_log[layer].astype(jnp.float32)).reshape(N_DIR, SSD_GROUPS, SSD_HPG)
        y_fwd = ssd_scan(xs, dt[:, :, 0], a[0], bm, cm)
        y_bwd = ssd_scan(xs[:, ::-1], dt[:, ::-1, 1], a[1], bm[:, ::-1], cm[:, ::-1])[:, ::-1]
        y = y_fwd + y_bwd + d_skip[layer].reshape(SSD_GROUPS, SSD_HPG, 1).astype(xs.dtype) * xs
        ssd = gated_group_rmsnorm(y.reshape(bsz, l, SSD_W).astype(x.dtype), z, ssd_norm_w[layer])

        mkv = memn @ w_mem_kv[layer]
        mk, mv = jnp.split(mkv, 2, axis=-1)
        mk = mk.reshape(bsz, MEM_LEN, MEM_HEADS, MEM_HEAD_DIM)
        mv = mv.reshape(bsz, MEM_LEN, MEM_HEADS, MEM_HEAD_DIM)
        qm = q_mem.reshape(bsz, l, MEM_HEADS, MEM_HEAD_DIM)
        sm = jnp.einsum('blhd,bmhd->bhlm', qm, mk).astype(jnp.float32) * MEM_HEAD_DIM ** -0.5
        pm = jax.nn.softmax(sm, axis=-1).astype(mv.dtype)
        xat = jnp.einsum('bhlm,bmhd->blhd', pm, mv).reshape(bsz, l, MEM_W)
        xat = xat * jax.nn.silu(g_mem)

        mixed = jnp.concatenate([att, ssd.astype(att.dtype), xat], axis=-1)
        h = h + mixed @ w_out[layer]

    return rms_norm(h, norm_out_w)
```

```python
import threading
import numpy as np
from contextlib import ExitStack
import concourse.bass as bass
import concourse.mybir as mybir
from concourse.bass_utils import run_bass_kernel_spmd

F32 = mybir.dt.float32
BF16 = mybir.dt.bfloat16
AF = mybir.ActivationFunctionType
ALU = mybir.AluOpType

D = 1024
NU = 21
SAME_SYNC = True
EPOCH = 30000
NEG = -30000.0
STOP = None
DMA_SCRATCH = 4096
CO_A, CO_B = 1, 1

P_NIN, P_NMEM, P_CW, P_CB, P_DTB, P_ALOG, P_DSK, P_SNW, P_NOW, P_SINK, P_EPS = (
    0, 8, 16, 56, 64, 80, 96, 104, 616, 1640, 1656)
P_NEGH = 1657
PT = 1659
C_SGT, C_SLE, C_SLT, C_SGE, C_ONES = 0, 128, 256, 384, 512
B_ID, B_NF, B_NB, B_RP = 0, 128, 640, 1152


class Reg:
    __slots__ = ("w", "r", "name", "tw", "tr")

    def __init__(self, name=""):
        self.w = None
        self.r = []
        self.name = name
        self.tw = 0.0
        self.tr = 0.0


class Co:
    def __init__(self, fn):
        self.fn = fn
        self.go = threading.Semaphore(0)
        self.back = threading.Semaphore(0)
        self.done = False
        self.exc = None
        self.th = threading.Thread(target=self._run, daemon=True)
        self.th.start()

    def _run(self):
        self.go.acquire()
        try:
            self.fn()
        except BaseException as e:
            self.exc = e
        self.done = True
        self.back.release()

    def step(self):
        self.go.release()
        self.back.acquire()
        if self.exc is not None:
            raise self.exc

    def pause(self):
        self.back.release()
        self.go.acquire()


class Sched:
    def __init__(self, nc, es):
        self.nc = nc
        self.es = es
        self.names = ["pe", "act", "dve", "pool", "sp"]
        self.prog = {k: [] for k in self.names}
        self.cnt = {k: 0 for k in self.names}
        self.epoch = {k: 0 for k in self.names}
        self.esem = {}
        self.seen = {k: {} for k in self.names}
        self.pend = {k: ([], []) for k in self.names}
        self.dsem = {}
        self.nsem = 0
        self.efree = {k: 0.0 for k in self.names}
        self.cur = None
        self.hold = 0
        self.sclock = {}
        for k in self.names:
            self._newepoch(k)

    def _vt(self, e, reads, writes, cost):
        t = self.efree[e]
        for r in reads:
            t = max(t, getattr(r, "tw", 0.0) + 0.15)
        for w in writes:
            t = max(t, getattr(w, "tw", 0.0) + 0.15, getattr(w, "tr", 0.0) + 0.15)
        fin = t + cost
        for r in reads:
            r.tr = max(getattr(r, "tr", 0.0), fin)
        for w in writes:
            w.tw = fin
            w.tr = 0.0
        return t, fin

    def _sem(self, name):
        self.nsem += 1
        return self.es.enter_context(self.nc.semaphore(name))

    def _newepoch(self, e):
        self.epoch[e] += 1
        nm = "%s#%d" % (e, self.epoch[e])
        self.esem[e] = (nm, self._sem("s_%s_%d" % (e, self.epoch[e])))
        self.cnt[e] = 0

    def _waits(self, e, reads, writes):
        waits = {}

        def need(t):
            if t is None:
                return
            nm, h, v = t
            if nm in waits:
                if v > waits[nm][1]:
                    waits[nm] = (h, v)
            else:
                waits[nm] = (h, v)

        for r in reads:
            need(r.w)
        for w in writes:
            need(w.w)
            for t in w.r:
                need(t)
        for nm, (h, v) in waits.items():
            if self.seen[e].get(nm, 0) >= v:
                continue
            if nm.split("#")[0] == e:
                if e == "pe" or not SAME_SYNC:
                    continue
            self.seen[e][nm] = v
            self.prog[e].append(lambda E, h=h, v=v: E.wait_ge(h, v))

    def op(self, e, fn, reads=(), writes=(), inc=True, cost=0.3):
        for k in self.names:
            if k != e:
                assert not self.pend[k][0] and not self.pend[k][1], "pending ops on %s" % k
        self._waits(e, reads, writes)
        t0, fin = self._vt(e, reads, writes, cost)
        self.efree[e] = fin
        if self.cur is not None:
            self.sclock[self.cur] = t0
        if inc:
            self.cnt[e] += 1
            v = self.cnt[e]
            nm, h = self.esem[e]
            self.prog[e].append(lambda E, fn=fn, h=h: fn(E).then_inc(h, 1))
            tok = (nm, h, v)
            pr, pw = self.pend[e]
            for r in list(reads) + pr:
                r.r.append(tok)
            for w in list(writes) + pw:
                w.w = tok
                w.r = []
            self.pend[e] = ([], [])
            if v >= EPOCH:
                self._newepoch(e)
            if self.cur is not None and self.hold == 0:
                self.cur.pause()
        else:
            self.prog[e].append(lambda E, fn=fn: fn(E))
            self.pend[e][0].extend(reads)
            self.pend[e][1].extend(writes)

    def dma(self, q, out, in_, key, reads=(), writes=(), cost=4.5):
        for k in self.names:
            assert not self.pend[k][0] and not self.pend[k][1], "pending ops on %s" % k
        self._waits(q, reads, writes)
        t0, fin = self._vt(q, reads, writes, cost)
        self.efree[q] = t0 + (1.0 if q == "pool" else 0.1)
        if key not in self.dsem:
            self.dsem[key] = ["d:" + key, self._sem("d_" + key), 0]
        ds = self.dsem[key]
        ds[2] += 16
        nm, h, v = ds
        self.prog[q].append(lambda E, h=h, out=out, in_=in_: E.dma_start(out=out, in_=in_).then_inc(h, 16))
        tok = (nm, h, v)
        for r in reads:
            r.r.append(tok)
        for w in writes:
            w.w = tok
            w.r = []
        return tok

    def finish_waits(self, e, toks):
        for nm, h, v in toks:
            self.prog[e].append(lambda E, h=h, v=v: E.wait_ge(h, v))


def build_nc(L, NSEQ, dbg=False):
    NT = L // 128
    NST = L // 512
    nc = bass.Bass("TRN2", target_bir_lowering=False, dynamic_dma_scratch_size=DMA_SCRATCH)
    dx = nc.dram_tensor("x", [NSEQ, L, D], F32, kind="ExternalInput").ap()
    dmem = nc.dram_tensor("mem", [NSEQ, 256, D], F32, kind="ExternalInput").ap()
    dwin = nc.dram_tensor("win_u", [NU, 128, 2048], F32, kind="ExternalInput").ap()
    dwdt = nc.dram_tensor("wdt", [128, 8 * 16], F32, kind="ExternalInput").ap()
    dwout = nc.dram_tensor("wout_p", [128, 16 * 1024], F32, kind="ExternalInput").ap()
    dwmem = nc.dram_tensor("wmem_p", [128, 8 * 1024], F32, kind="ExternalInput").ap()
    dptab = nc.dram_tensor("ptab", [128, PT], F32, kind="ExternalInput").ap()
    dcf = nc.dram_tensor("cf32", [128, 640], F32, kind="ExternalInput").ap()
    dcb = nc.dram_tensor("cb16", [128, 1280], F32, kind="ExternalInput").ap()
    dcos = nc.dram_tensor("cosT", [128, L + 256], F32, kind="ExternalInput").ap()
    dsin = nc.dram_tensor("sinT", [128, L + 256], F32, kind="ExternalInput").ap()
    dout = nc.dram_tensor("out", [NSEQ, L, D], F32, kind="ExternalOutput").ap()
    dwbf = nc.dram_tensor("wbf", [NU, 128, 2048], BF16).ap()
    dprev = nc.dram_tensor("prevb", [NT, 128, 512], BF16).ap()

    es = ExitStack()
    with es:
        S = Sched(nc, es)

        def sb(name, shape, dt):
            return es.enter_context(nc.sbuf_tensor("s_" + name, shape, dt))

        def psum(name, shape, dt):
            return es.enter_context(nc.psum_tensor("p_" + name, shape, dt))

        ptab = sb("ptab", [128, PT], F32); r_ptab = Reg()
        cf = sb("cf", [128, 640], F32); r_cf = Reg()
        cb = sb("cb", [128, 1280], BF16); r_cb = Reg()
        wout = sb("wout", [128, 16, 1024], BF16); r_wout = Reg()
        wdt = sb("wdtb", [128, 8, 16], BF16); r_wdt = Reg()
        NW = 6
        wring = [sb("wr%d" % i, [128, 8, 256], BF16) for i in range(NW)]
        r_wring = [Reg() for _ in range(NW)]
        mkT = sb("mkT", [128, 4, NSEQ * 256], BF16); r_mkT = Reg()
        mvaug = sb("mvaug", [128, NSEQ * 2, 4, 130], BF16); r_mv = Reg()
        xin = [sb("xin%d" % i, [128, D], F32) for i in range(2)]; r_xin = [Reg() for _ in range(2)]
        xres = [sb("xres%d" % i, [128, D], F32) for i in range(2)]; r_xres = [Reg() for _ in range(2)]
        hnT = sb("hnT", [128, 8, 768], BF16); r_hn = [Reg() for _ in range(6)]
        cosb = sb("cosb", [128, 768], F32); sinb = sb("sinb", [128, 768], F32); r_cs = Reg()
        qT = [sb("qT%d" % i, [128, 2, 512], BF16) for i in range(2)]; r_qT = [[Reg(), Reg()] for _ in range(2)]
        kTlo = sb("kTlo", [128, 4, 768], BF16); kThi = sb("kThi", [128, 4, 768], BF16); r_kT = [Reg() for _ in range(4)]
        vaug = sb("vaug", [128, 6, 4, 66], BF16); r_v = [Reg() for _ in range(6)]
        rp0 = sb("rp0", [128, 512], BF16); r_rp0 = Reg()
        rp1 = sb("rp1", [128, 512], F32); r_rp1 = Reg()
        rp2 = sb("rp2", [128, 512], F32); r_rp2 = Reg()
        rq0 = sb("rq0", [128, 512], BF16); r_rq0 = Reg()
        rq1 = sb("rq1", [128, 512], F32); r_rq1 = Reg()
        rq2 = sb("rq2", [128, 512], F32); r_rq2 = Reg()
        xsb = [rp1[:].bitcast(BF16), rp2[:].bitcast(BF16)]; r_xsb = [r_rp1, r_rp2]
        PTb = [sb("PT%d" % i, [128, 3, 512], BF16) for i in range(2)]; r_PT = [Reg(), Reg()]
        gs = [sb("gs%d" % i, [128, 4, 256], BF16) for i in range(2)]; r_gs = [[Reg() for _ in range(4)] for _ in range(2)]
        r_dn = [Reg(), Reg()]; r_ssd_ss = Reg(); r_mem_dn = Reg(); r_out_ss = Reg()
        sm = sb("sm", [128, 64], F32); r_sm = Reg()
        gr = [sb("gr%d" % i, [128, 256], F32) for i in range(2)]; r_gr = [Reg(), Reg()]
        attg = [sb("attg%d" % i, [128, 256], BF16) for i in range(2)]; r_attg = [Reg(), Reg()]
        grm = gr; r_grm = r_gr; attgm = attg; r_attgm = r_attg
        mixT = sb("mixT", [128, 16, 512], BF16); r_mix = [[Reg() for _ in range(4)] for _ in range(16)]
        dts = sb("dts", [128, 4, 16], F32); r_dts = Reg()
        adt = sb("adt", [128, 4, 16], F32); r_adt = Reg()
        exs = sb("exs", [128, 4, 48], F32); r_exs = Reg()
        wfb = sb("wfb", [128, 4, 16], F32); r_wfb = Reg()
        zs = sb("zs", [128, 4, 512], BF16); r_zs = [Reg() for _ in range(4)]
        cacc = [sb("cacc%d" % i, [128, 256], F32) for i in range(3)]; r_cacc = [Reg() for _ in range(3)]
        cth = [sb("cth%d" % i, [128, 256], F32) for i in range(3)]; r_cth = [Reg() for _ in range(3)]
        cbs = cacc[0]; r_cbs = r_cacc[0]
        xsTt = [sb("xsTt%d" % i, [128, 256], BF16) for i in range(3)]; r_xsTt = [Reg() for _ in range(3)]
        bcT = sb("bcT", [128, 4, 512], BF16); r_bcT = [[Reg() for _ in range(4)] for _ in range(4)]
        xs_tm = sb("xs_tm", [128, 4, 512], BF16); r_xs = [Reg() for _ in range(4)]
        bm_tm = sb("bm_tm", [128, 4, 256], BF16); r_bm = [Reg() for _ in range(4)]
        Rb = [sb("Rb%d" % i, [128, 4, 128], F32) for i in range(2)]; r_Rb = [Reg(), Reg()]
        Lt = [sb("Lt%d" % i, [128, 4, 128], BF16) for i in range(2)]; r_Lt = [Reg(), Reg()]
        Mt = [sb("Mt%d" % i, [128, 4, 128], BF16) for i in range(4)]; r_Mt = [Reg() for _ in range(4)]
        xsc = [sb("xsc%d" % i, [128, 512], BF16) for i in range(5)]; r_xsc = [Reg() for _ in range(5)]
        Sst = [sb("Sst%d" % i, [128, 512], F32) for i in range(2)]; r_Sst = [Reg(), Reg()]
        Sbf = [sb("Sbf%d" % i, [128, 512], BF16) for i in range(2)]; r_Sbf = [Reg(), Reg()]
        pbuf = [sb("pbuf%d" % i, [128, 512], BF16) for i in range(2)]; r_pbuf = [Reg(), Reg()]
        y1 = sb("y1", [128, 512], F32); r_y1 = Reg()
        y2 = sb("y2", [128, 512], F32); r_y2 = Reg()
        junk = y2[:].bitcast(BF16); r_junk = r_y2
        yn = sb("yn", [128, 512], BF16); r_yn = Reg()
        qmT = zs; r_qm = r_zs
        gms = xs_tm; r_gms = r_xs
        PTm = PTb; r_PTm = r_PT
        wmem = mixT[:].rearrange("p (a b) c -> p a (b c)", a=8); r_wmem = Reg()
        memT = hnT[:, :, 0:NSEQ * 256]; r_memT = Reg()
        r_dwbf = [Reg() for _ in range(NU)]
        r_dprev = [Reg() for _ in range(NT)]

        pf = [psum("pf%d" % i, [128, 512], F32) for i in range(6)]; r_pf = [Reg() for _ in range(6)]
        pb = [psum("pb%d" % i, [128, 1024], BF16) for i in range(2)]; r_pb = [Reg(), Reg()]
        cnt = {"pf": 0, "pb": 0, "x": 0, "xr": 0, "xs": 0, "pt": 0, "ca": 0, "rb": 0, "ptm": 0, "w": 0, "pbuf": 0, "sbf": 0}

        def nxt(k, n):
            i = cnt[k] % n
            cnt[k] += 1
            return i

        pools = {"A1": ([0, 1], [0]), "A2": ([2, 3], [0]), "B": ([4, 5], [1]),
                 "X": ([0, 1], [0]), "Y": ([2, 3], [1]), "Z": ([4], [0]), "D": ([5], [1]),
                 "O1": ([0, 1, 2], [0]), "O2": ([3, 4, 5], [1]), "H1": ([0], [0]), "H2": ([3], [1]), None: ([0, 1, 2, 3, 4, 5], [0, 1])}
        pcnt = {}

        def PF():
            k = getattr(S.cur, "pool", None)
            lst = pools[k][0]
            n = pcnt.get(("f", k), 0)
            pcnt[("f", k)] = n + 1
            i = lst[n % len(lst)]
            return pf[i], r_pf[i]

        def PB():
            k = getattr(S.cur, "pool", None)
            lst = pools[k][1]
            n = pcnt.get(("b", k), 0)
            pcnt[("b", k)] = n + 1
            i = lst[n % len(lst)]
            return pb[i], r_pb[i]

        def fsz(ap):
            return ap.free_size()

        def mm(out, lhsT, rhs, start, stop, reads, writes, inc):
            c = max(64, fsz(rhs)) / 1200.0 * (4.0 if rhs.dtype == F32 else 1.0)
            S.op("pe", lambda E: E.matmul(out, lhsT, rhs, start=start, stop=stop), reads, writes, inc, cost=c)

        def tp(out, in_, reads, writes, inc):
            S.op("pe", lambda E: E.transpose(out, in_, cb[:, B_ID:B_ID + 128]), list(reads) + [r_cb], writes, inc, cost=0.11)

        ECOST = {"act": (0.2, 0.00085), "dve": (0.12, 0.0011), "pool": (0.15, 0.0022)}

        def ecost(e, out):
            a, b = ECOST[e]
            return a + b * fsz(out)

        def act(out, in_, func, reads, writes, bias=None, scale=None, accum=None):
            kw = {}
            if bias is not None:
                kw["bias"] = bias
            if scale is not None:
                kw["scale"] = scale
            if accum is not None:
                kw["accum_out"] = accum
            S.op("act", lambda E: E.activation(out, in_, func, **kw), reads, writes,
                 cost=ecost("act", in_) + (0.1 if accum is not None else 0.0))

        def tt(e, out, in0, in1, op, reads, writes):
            S.op(e, lambda E: E.tensor_tensor(out, in0, in1, op), reads, writes, cost=ecost(e, out))

        def ts(e, out, in0, s1, s2, op0, op1, reads, writes):
            if op1 is None:
                S.op(e, lambda E: E.tensor_scalar(out, in0, s1, None, op0), reads, writes, cost=ecost(e, out))
            else:
                S.op(e, lambda E: E.tensor_scalar(out, in0, s1, s2, op0, op1), reads, writes, cost=ecost(e, out))

        def stt(out, in0, sc, in1, op0, op1, reads, writes):
            S.op("dve", lambda E: E.scalar_tensor_tensor(out, in0, sc, in1, op0, op1), reads, writes, cost=ecost("dve", out))

        def cp(e, out, in_, reads, writes):
            if e == "act":
                S.op("act", lambda E: E.copy(out, in_), reads, writes, cost=ecost("act", out))
            else:
                S.op(e, lambda E: E.tensor_copy(out, in_), reads, writes, cost=ecost(e, out))

        def bc(ap, shape, axis):
            return ap.unsqueeze(axis).to_broadcast(shape)

        S.dma("sp", ptab[:], dptab[:, :], "ptab", writes=[r_ptab])
        S.dma("sp", cf[:], dcf[:, :], "cf", writes=[r_cf])
        S.dma("pool", cb[:], dcb[:, :], "cb", writes=[r_cb])
        S.dma("pool", wdt[:].rearrange("p a b -> p (a b)"), dwdt[:, :], "wdt", writes=[r_wdt])
        for kc in range(8):
            S.dma("pool", wmem[:, kc, :], dwmem[:, kc * 1024:(kc + 1) * 1024], "wmem", writes=[r_wmem])
        A_ap = ptab[:, P_ALOG:P_ALOG + 16]
        act(A_ap, A_ap, AF.Exp, [r_ptab], [r_ptab])
        ts("dve", A_ap, A_ap, -1.0, None, ALU.mult, None, [r_ptab], [r_ptab])
        act(ptab[:, P_SINK:P_SINK + 16], ptab[:, P_SINK:P_SINK + 16], AF.Exp, [r_ptab], [r_ptab])
        eps_ap = ptab[:, P_EPS:P_EPS + 1]
        ts("dve", ptab[:, P_CW:P_CW + 48], ptab[:, P_CW:P_CW + 48], 0.5, None, ALU.mult, None, [r_ptab], [r_ptab])
        S.op("pool", lambda E: E.memset(vaug[:], 1.0), [], r_v)
        S.op("pool", lambda E: E.memset(kTlo[:], 0.0), [], r_kT)
        S.op("pool", lambda E: E.memset(kThi[:], 0.0), [], r_kT)
        S.op("pool", lambda E: E.memset(mvaug[:], 1.0), [], [r_mv])

        def rstd_from_ss(ss_ap, n, scale, reads_writes):
            ts("dve", ss_ap, ss_ap, float(scale), 1e-6, ALU.mult, ALU.add, reads_writes, reads_writes)
            tt("pool", ss_ap, ss_ap, ptab[:, P_NEGH:P_NEGH + n], ALU.pow, reads_writes + [r_ptab], reads_writes)

        r_hnss = [r_sm, Reg()]
        junk1 = y1[:].bitcast(BF16)

        def norm_transpose(src_ap, xslot, r_x, wcol, dst_fn, r_dst, h=None):
            hh = 0 if h is None else h
            ss = sm[:, hh:hh + 1]
            jk, rjk = (junk, r_junk) if hh == 0 else (junk1, r_y1)
            act(jk[:], src_ap, AF.Square, [r_x], [rjk, r_hnss[hh]], accum=ss)
            rstd_from_ss(ss, 1, 1.0 / D, [r_hnss[hh]])
            i = nxt("xs", 2) if h is None else h
            act(xsb[i][:], src_ap, AF.Identity, [r_x, r_hnss[hh]], [r_xsb[i]], scale=ss)
            pbt, r_pbt = PB()
            for c in range(8):
                tp(pbt[:, c * 128:(c + 1) * 128], xsb[i][:, c * 128:(c + 1) * 128], [r_xsb[i]], [r_pbt], c == 7)
            tt("dve", dst_fn(), pbt[:].rearrange("p (c t) -> p c t", c=8),
               bc(ptab[:, wcol:wcol + 8], [128, 8, 128], 2), ALU.mult, [r_pbt, r_ptab], r_dst)

        for sq in range(NSEQ):
            for mt in range(2):
                i = nxt("x", 2)
                S.dma("sp", xin[i][:], dmem[sq, mt * 128:(mt + 1) * 128, :], "xin%d" % i, writes=[r_xin[i]])
                c0 = sq * 256 + mt * 128
                norm_transpose(xin[i][:], i, r_xin[i], P_NMEM, lambda c0=c0: memT[:, :, c0:c0 + 128], [r_memT])
        for sq in range(NSEQ):
            for hm in range(4):
                p, rp = PF()
                for kc in range(8):
                    mm(p[:, 0:256], wmem[:, kc, hm * 128:(hm + 1) * 128], memT[:, kc, sq * 256:(sq + 1) * 256],
                       kc == 0, kc == 7, [r_wmem, r_memT], [rp], kc == 7)
                cp("act", mkT[:, hm, sq * 256:(sq + 1) * 256], p[:, 0:256], [rp], [r_mkT])
            for mt in range(2):
                p, rp = PF()
                for kc in range(8):
                    mm(p[:, :], memT[:, kc, sq * 256 + mt * 128: sq * 256 + (mt + 1) * 128], wmem[:, kc, 512:1024],
                       kc == 0, kc == 7, [r_wmem, r_memT], [rp], kc == 7)
                cp("dve", mvaug[:, sq * 2 + mt, :, 0:128], p[:, :].rearrange("p (h d) -> p h d", h=4), [rp], [r_mv])
        for rr in r_hn:
            rr.w = r_memT.w; rr.r = list(r_memT.r)
        for c in range(16):
            for t in range(4):
                r_mix[c][t].w = r_wmem.w; r_mix[c][t].r = list(r_wmem.r)
        last = None
        for u in range(NU):
            i = u % NW
            S.dma("pool", wring[i][:].rearrange("p a b -> p (a b)"), dwin[u, :, :], "wst%d" % i, writes=[r_wring[i]])
            last = S.dma("pool", dwbf[u, :, :], wring[i][:].rearrange("p a b -> p (a b)"), "wsv%d" % i,
                         reads=[r_wring[i]], writes=[r_dwbf[u]])
        for kc in range(16):
            S.dma("pool", wout[:, kc, :], dwout[:, kc * 1024:(kc + 1) * 1024], "wout", writes=[r_wout])
        r_wout.w = (S.dsem["wout"][0], S.dsem["wout"][1], S.dsem["wout"][2])

        def wload(u, i):
            S.dma("sp", wring[i][:].rearrange("p a b -> p (a b)"), dwbf[u, :, :], "wld%d" % i,
                  reads=[r_dwbf[u]], writes=[r_wring[i]])
            return wring[i], r_wring[i]

        class WStream:
            def __init__(self, units, base):
                self.units = list(units)
                self.base = base
                self.k = 0
                self.loaded = []

            def prefetch(self):
                while len(self.loaded) < min(self.k + 2, len(self.units)):
                    n = len(self.loaded)
                    self.loaded.append(wload(self.units[n], self.base + n % 2))

            def get(self):
                self.prefetch()
                r = self.loaded[self.k]
                self.k += 1
                return r

        wpre = {}
        xpre = set()

        def hn_first(st, prev_st, h):
            reuse = ()
            if prev_st is not None and prev_st == st - 1:
                reuse = (0, 1)
            elif prev_st is not None and prev_st == st + 1:
                reuse = (4, 5)
            todo = [j for j in range(6) if j not in reuse]
            j = todo[h]
            g = st * 4 - 1 + j
            return g if 0 <= g < NT else None

        def prefetch_x_hn(sq, st, prev_st):
            for h in range(2):
                g = hn_first(st, prev_st, h)
                if g is not None:
                    S.dma("sp", xin[h][:], dx[sq, g * 128:(g + 1) * 128, :], "xin%d" % h, writes=[r_xin[h]])
                    xpre.add(("xin", sq, st, h, g))

        def prefetch_x_out(sq, st):
            for h in range(2):
                g = st * 4 + 2 * h
                S.dma("sp", xres[h][:], dx[sq, g * 128:(g + 1) * 128, :], "xr%d" % h, writes=[r_xres[h]])
                xpre.add(("xr", sq, st, h, g))

        def mkws(key, units, base):
            if key in wpre:
                return wpre.pop(key)
            return WStream(units, base)

        def prefetch_ws(key, units, base):
            w = WStream(units, base)
            w.prefetch()
            wpre[key] = w

        ident_b = cb[:, B_ID:B_ID + 128]
        negf4 = cb[:, B_NF:B_NF + 512]
        negb4 = cb[:, B_NB:B_NB + 512]
        rperm = cb[:, B_RP:B_RP + 128]

        def hn_stage(sq, st, prev_st=None, h=None):
            reuse = {}
            if prev_st is not None and prev_st == st - 1:
                reuse = {0: 4, 1: 5}
            elif prev_st is not None and prev_st == st + 1:
                reuse = {4: 0, 5: 1}
            if h is None or h == "halo":
                for j in sorted(reuse, reverse=(4 in reuse)):
                    sj = reuse[j]
                    cp("dve", hnT[:, :, j * 128:(j + 1) * 128], hnT[:, :, sj * 128:(sj + 1) * 128], [r_hn[sj]], [r_hn[j]])
            if h == "halo":
                return
            todo = [j for j in range(6) if j not in reuse]
            for n, j in enumerate(todo):
                if h is not None and n % 2 != h:
                    continue
                g = st * 4 - 1 + j
                dst = hnT[:, :, j * 128:(j + 1) * 128]
                if g < 0 or g >= NT:
                    S.op("pool", lambda E, dst=dst: E.memset(dst, 0.0), [], [r_hn[j]])
                    continue
                i = nxt("x", 2) if h is None else h
                if ("xin", sq, st, h, g) in xpre:
                    xpre.discard(("xin", sq, st, h, g))
                else:
                    S.dma("sp", xin[i][:], dx[sq, g * 128:(g + 1) * 128, :], "xin%d" % i, writes=[r_xin[i]])
                norm_transpose(xin[i][:], i, r_xin[i], P_NIN, lambda dst=dst: dst, [r_hn[j]], h)

        def hn_regs(lo, hi):
            return [r_hn[j] for j in range(lo // 128, (hi - 1) // 128 + 1)]

        def proj_fm(w, rw, col0, lo, hi, pbank, rp, pcol0=0):
            for kc in range(8):
                mm(pbank[:, pcol0:pcol0 + hi - lo], w[:, kc, col0:col0 + 128], hnT[:, kc, lo:hi],
                   kc == 0, kc == 7, [rw] + hn_regs(lo, hi), [rp], kc == 7)

        def proj_tm(w, rw, ncol, tcol, pbank, rp, wcol0=0, pcol0=0):
            for kc in range(8):
                mm(pbank[:, pcol0:pcol0 + ncol], hnT[:, kc, tcol:tcol + 128], w[:, kc, wcol0:wcol0 + ncol],
                   kc == 0, kc == 7, [rw] + hn_regs(tcol, tcol + 128), [rp], kc == 7)

        def rope(p, rp, n, csl, dst, r_dst, sx=0):
            a0, ra0 = (rp0, r_rp0) if sx == 0 else (rq0, r_rq0)
            a1, ra1 = (rp1, r_rp1) if sx == 0 else (rq1, r_rq1)
            a2, ra2 = (rp2, r_rp2) if sx == 0 else (rq2, r_rq2)
            cp("act", a0[:, 0:n], p[:, 0:n], [rp], [ra0])
            p2, rp2_ = PF()
            mm(p2[:, 0:n], rperm, a0[:, 0:n], True, True, [r_cb, ra0], [rp2_], True)
            tt("dve", a1[:, 0:n], p2[:, 0:n], sinb[:, csl:csl + n], ALU.mult, [rp2_, r_cs], [ra1])
            tt("dve", a2[:, 0:n], p[:, 0:n], cosb[:, csl:csl + n], ALU.mult, [rp, r_cs], [ra2])
            tt("pool", dst, a1[:, 0:n], a2[:, 0:n], ALU.add, [ra1, ra2], r_dst)

        def rope_k(p, rp, n, csl, j, lo, hi):
            cp("act", rp0[:, 0:n], p[:, 0:n], [rp], [r_rp0])
            p2, rp2_ = PF()
            mm(p2[:, 0:n], rperm, rp0[:, 0:n], True, True, [r_cb, r_rp0], [rp2_], True)
            tt("dve", rp1[:, 0:n], p2[:, 0:n], sinb[:, csl:csl + n], ALU.mult, [rp2_, r_cs], [r_rp1])
            tt("dve", rp2[:, 0:n], p[:, 0:n], cosb[:, csl:csl + n], ALU.mult, [rp, r_cs], [r_rp2])
            tt("pool", kTlo[0:64, j, lo:hi], rp1[0:64, 0:n], rp2[0:64, 0:n], ALU.add, [r_rp1, r_rp2], [r_kT[j]])
            tt("pool", kThi[64:128, j, lo:hi], rp1[64:128, 0:n], rp2[64:128, 0:n], ALU.add, [r_rp1, r_rp2], [r_kT[j]])

        def att_cossin(sq, st):
            S0 = st * 512
            S.dma("sp", cosb[:], dcos[:, S0:S0 + 768], "cos", writes=[r_cs])
            S.dma("sp", sinb[:], dsin[:, S0:S0 + 768], "sin", writes=[r_cs])

        def att_prologue_k(sq, st):
            ws = mkws(("K", sq, st), [4, 5], 0)
            for u in range(2):
                w, rw = ws.get()
                for jj in range(2):
                    j = u * 2 + jj
                    for (lo, hi) in ((0, 512), (512, 768)):
                        p, rp = PF()
                        proj_fm(w, rw, jj * 128, lo, hi, p, rp)
                        rope_k(p, rp, hi - lo, lo, j, lo, hi)

        def att_prologue_v(sq, st):
            ws = mkws(("V", sq, st), [6], 2)
            w, rw = ws.get()
            for j6 in range(6):
                p, rp = PF()
                proj_tm(w, rw, 256, j6 * 128, p, rp)
                cp("act", vaug[:, j6, :, 0:64], p[:, 0:256].rearrange("p (k d) -> p k d", k=4), [rp], [r_v[j6]])

        kvflag = {}

        def att_stream(sq, st, sx):
            if sx == 0:
                att_prologue_k(sq, st)
                kvflag[(sq, st, 0)] = True
            else:
                att_prologue_v(sq, st)
                kvflag[(sq, st, 1)] = True
            ws = WStream([sx, 7 + sx, sx + 2, 9 + sx, 17 + sx, 19 + sx], 2 * sx)
            for j in (sx, sx + 2):
                wq, rwq = ws.get()
                for c in range(2):
                    p, rp = PF()
                    proj_fm(wq, rwq, c * 128, 128, 640, p, rp)
                    rope(p, rp, 512, 128, qT[sx][:, c, :], [r_qT[sx][c]], sx)
                wg, rwg = ws.get()
                for t in range(4):
                    p, rp = PF()
                    proj_tm(wg, rwg, 256, 128 + t * 128, p, rp)
                    act(gr[sx][:], p[:, 0:256], AF.Tanh, [rp], [r_gr[sx]], scale=0.5)
                    stt(gs[sx][:, t, :], gr[sx][:], 1.0, p[:, 0:256], ALU.add, ALU.mult, [r_gr[sx], rp], [r_gs[sx][t]])
                while not (kvflag.get((sq, st, 0)) and kvflag.get((sq, st, 1))):
                    S.sclock[S.cur] = max(S.sclock.values()) + 1e-3
                    S.cur.pause()
                for qb in range(4):
                    gq = st * 4 + qb
                    kbs = [kb for kb in (-1, 0, 1) if 0 <= gq + kb < NT]
                    for kb in kbs:
                        kcol = (qb + 1 + kb) * 128
                        p, rp = PF()
                        first = True
                        if kb != 0:
                            mm(p[:, :], ident_b, negb4 if kb < 0 else negf4, True, False, [r_cb], [rp], False)
                            first = False
                        for r in range(4):
                            c, half = r // 2, r % 2
                            kk = kTlo if half == 0 else kThi
                            mm(p[:, r * 128:(r + 1) * 128], kk[:, j, kcol:kcol + 128], qT[sx][:, c, qb * 128:(qb + 1) * 128],
                               first, r == 3, [r_kT[j], r_qT[sx][c]], [rp], r == 3)
                            first = False
                        act(PTb[sx][:, kb + 1, :], p[:, :], AF.Exp, [rp], [r_PT[sx]], scale=0.125)
                    pv, rpv = PF()
                    first = True
                    for r in range(4):
                        for kb in kbs:
                            lastmm = (r == 3 and kb == kbs[-1])
                            mm(pv[:, r * 128:r * 128 + 65], PTb[sx][:, kb + 1, r * 128:(r + 1) * 128],
                               vaug[:, qb + 1 + kb, j, 0:65], first, lastmm, [r_PT[sx], r_v[qb + 1 + kb]], [rpv], lastmm)
                            first = False
                    pv3 = pv[:, :].rearrange("p (r d) -> p r d", r=4)
                    dn = sm[:, 8 + 4 * sx:12 + 4 * sx]
                    tt("dve", dn, pv3[:, :, 64], ptab[:, P_SINK + 4 * j:P_SINK + 4 * j + 4], ALU.add,
                       [rpv, r_ptab], [r_dn[sx]])
                    S.op("dve", lambda E, dn=dn: E.reciprocal(dn, dn), [r_dn[sx]], [r_dn[sx]])
                    tt("pool", gr[sx][:].rearrange("p (r d) -> p r d", r=4), gs[sx][:, qb, :].rearrange("p (r d) -> p r d", r=4),
                       bc(dn, [128, 4, 64], 2), ALU.mult, [r_gs[sx][qb], r_dn[sx]], [r_gr[sx]])
                    stt(attg[sx][:].rearrange("p (r d) -> p r d", r=4), pv3[:, :, 0:64], 0.5,
                        gr[sx][:].rearrange("p (r d) -> p r d", r=4), ALU.mult, ALU.mult, [rpv, r_gr[sx]], [r_attg[sx]])
                    S.hold += 1
                    pbt, rpbt = PB()
                    for c in range(2):
                        tp(pbt[:, c * 128:(c + 1) * 128], attg[sx][:, c * 128:(c + 1) * 128], [r_attg[sx]], [rpbt], c == 1)
                    for c in range(2):
                        if c == 1:
                            S.hold -= 1
                        cp("act", mixT[:, 2 * j + c, qb * 128:(qb + 1) * 128], pbt[:, c * 128:(c + 1) * 128],
                           [rpbt], [r_mix[2 * j + c][qb]])
            w, rw = ws.get()
            for cc in range(2):
                p, rp = PF()
                proj_fm(w, rw, cc * 128, 128, 640, p, rp)
                cp("act", qT[sx][:, cc, :], p[:, :], [rp], [r_qT[sx][cc]])
            w, rw = ws.get()
            for t in range(4):
                p, rp = PF()
                proj_tm(w, rw, 256, 128 + t * 128, p, rp)
                act(gr[sx][:], p[:, 0:256], AF.Tanh, [rp], [r_gr[sx]], scale=0.5)
                stt(gs[sx][:, t, :], gr[sx][:], 1.0, p[:, 0:256], ALU.add, ALU.mult, [r_gr[sx], rp], [r_gs[sx][t]])
            sc = float(1.0 / np.sqrt(128.0))
            for cc in range(2):
                hm = 2 * sx + cc
                for mb in range(2):
                    p, rp = PF()
                    m0 = sq * 256 + mb * 128
                    mm(p[:, :], mkT[:, hm, m0:m0 + 128], qT[sx][:, cc, :], True, True, [r_mkT, r_qT[sx][cc]], [rp], True)
                    act(PTb[sx][:, mb, :], p[:, :], AF.Exp, [rp], [r_PT[sx]], scale=sc)
                for tp2 in range(2):
                    pv, rpv = PF()
                    first = True
                    for tl in range(2):
                        t = tp2 * 2 + tl
                        for mb in range(2):
                            lastmm = (tl == 1 and mb == 1)
                            mm(pv[:, tl * 256:tl * 256 + 129], PTb[sx][:, mb, t * 128:(t + 1) * 128],
                               mvaug[:, sq * 2 + mb, hm, 0:129], first, lastmm, [r_PT[sx], r_mv], [rpv], lastmm)
                            first = False
                    pv3 = pv[:, :].rearrange("p (t d) -> p t d", t=2)
                    dn = sm[:, 24 + 2 * sx:26 + 2 * sx]
                    S.op("dve", lambda E, dn=dn, pv3=pv3: E.reciprocal(dn, pv3[:, :, 128]), [rpv], [r_dn[sx]])
                    for tl in range(2):
                        t = tp2 * 2 + tl
                        ts("pool", gr[sx][:, 0:128], gs[sx][:, t, cc * 128:(cc + 1) * 128], dn[:, tl:tl + 1], 0.5, ALU.mult, ALU.mult,
                           [r_gs[sx][t], r_dn[sx]], [r_gr[sx]])
                        tt("dve", attg[sx][:, 0:128], pv[:, tl * 256:tl * 256 + 128], gr[sx][:, 0:128], ALU.mult,
                           [rpv, r_gr[sx]], [r_attg[sx]])
                        S.hold += 1
                        pbt, rpbt = PB()
                        tp(pbt[:, 0:128], attg[sx][:, 0:128], [r_attg[sx]], [rpbt], True)
                        S.hold -= 1
                        cp("act", mixT[:, 12 + hm, t * 128:(t + 1) * 128], pbt[:, 0:128], [rpbt], [r_mix[12 + hm][t]])
            yield

        def ssd_dt(sq, st):
            p, rp = PF()
            for t in range(4):
                for kc in range(8):
                    mm(p[:, t * 16:(t + 1) * 16], hnT[:, kc, 128 + t * 128:256 + t * 128], wdt[:, kc, :],
                       t == 0 and kc == 0, t == 3 and kc == 7, [r_wdt] + hn_regs(128, 640), [rp], t == 3 and kc == 7)
            d3 = dts[:]
            tt("dve", d3, p[:, 0:64].rearrange("p (t h) -> p t h", t=4), bc(ptab[:, P_DTB:P_DTB + 16], [128, 4, 16], 1),
               ALU.add, [rp, r_ptab], [r_dts])
            act(d3, d3, AF.Exp, [r_dts], [r_dts])
            act(d3, d3, AF.Ln, [r_dts], [r_dts], bias=1.0)
            tt("dve", adt[:], d3, bc(ptab[:, P_ALOG:P_ALOG + 16], [128, 4, 16], 1), ALU.mult, [r_dts, r_ptab], [r_adt])
            p, rp = PF()
            first = True
            for t in range(4):
                for (col, mat, a0, n) in ((0, C_SGT, 0, 8), (8, C_SLE, 0, 8), (16, C_SLT, 8, 8), (24, C_SGE, 8, 8),
                                          (32, C_ONES, 0, 16)):
                    lastmm = (t == 3 and col == 32)
                    mm(p[:, t * 48 + col:t * 48 + col + n], cf[:, mat:mat + 128], adt[:, t, a0:a0 + n], first, lastmm,
                       [r_cf, r_adt], [rp], lastmm)
                    first = False
            act(exs[:], p[:, 0:192].rearrange("p (t c) -> p t c", t=4), AF.Exp, [rp], [r_exs])
            tt("dve", wfb[:, :, 0:8], dts[:, :, 0:8], exs[:, :, 0:8], ALU.mult, [r_dts, r_exs], [r_wfb])
            tt("dve", wfb[:, :, 8:16], dts[:, :, 8:16], exs[:, :, 16:24], ALU.mult, [r_dts, r_exs], [r_wfb])

        def ssd_z(sq, st, ws):
            for u in range(2):
                w, rw = ws.get()
                for t in range(4):
                    p, rp = PF()
                    proj_tm(w, rw, 256, 128 + t * 128, p, rp)
                    act(cth[0][:], p[:, 0:256], AF.Tanh, [rp], [r_cth[0]], scale=0.5)
                    stt(zs[:, t, u * 256:(u + 1) * 256], cth[0][:], 1.0, p[:, 0:256], ALU.add, ALU.mult, [r_cth[0], rp], [r_zs[t]])
            yield

        def ssd_conv(sq, st, us, ws, ia):
            for u in us:
                w, rw = ws.get()
                for cc in range(2):
                    ch = u * 2 + cc
                    for hf in range(2):
                        lo = 126 + hf * 256
                        p, rp = PF()
                        proj_fm(w, rw, cc * 128, lo, lo + 260, p, rp)
                        acc = cacc[ia]
                        act(acc[:], p[:, 0:256], AF.Identity, [rp, r_ptab], [r_cacc[ia]],
                            bias=ptab[:, P_CB + ch:P_CB + ch + 1], scale=ptab[:, P_CW + ch * 5:P_CW + ch * 5 + 1])
                        for k in range(1, 5):
                            stt(acc[:], p[:, k:k + 256], ptab[:, P_CW + ch * 5 + k:P_CW + ch * 5 + k + 1], acc[:],
                                ALU.mult, ALU.add, [rp, r_ptab, r_cacc[ia]], [r_cacc[ia]])
                        if ch < 4:
                            act(cth[ia][:], acc[:], AF.Tanh, [r_cacc[ia]], [r_cth[ia]])
                            stt(xsTt[ia][:], cth[ia][:], 1.0, acc[:], ALU.add, ALU.mult, [r_cth[ia], r_cacc[ia]], [r_xsTt[ia]])
                            S.hold += 1
                            pbt, rpbt = PB()
                            for tl in range(2):
                                tp(pbt[:, tl * 128:(tl + 1) * 128], xsTt[ia][:, tl * 128:(tl + 1) * 128], [r_xsTt[ia]],
                                   [rpbt], tl == 1)
                            for tl in range(2):
                                if tl == 1:
                                    S.hold -= 1
                                t = hf * 2 + tl
                                cp("act", xs_tm[:, t, ch * 128:(ch + 1) * 128], pbt[:, tl * 128:(tl + 1) * 128],
                                   [rpbt], [r_xs[t]])
                        else:
                            q = ch - 4
                            act(cth[ia][:], acc[:], AF.Tanh, [r_cacc[ia]], [r_cth[ia]])
                            stt(bcT[:, q, hf * 256:(hf + 1) * 256], cth[ia][:], 1.0, acc[:], ALU.add, ALU.mult,
                                [r_cth[ia], r_cacc[ia]], [r_bcT[q][hf * 2], r_bcT[q][hf * 2 + 1]])
                            if q < 2:
                                S.hold += 1
                                pbt, rpbt = PB()
                                for tl in range(2):
                                    t = hf * 2 + tl
                                    tp(pbt[:, tl * 128:(tl + 1) * 128], bcT[:, q, t * 128:(t + 1) * 128], [r_bcT[q][t]],
                                       [rpbt], tl == 1)
                                for tl in range(2):
                                    if tl == 1:
                                        S.hold -= 1
                                    t = hf * 2 + tl
                                    cp("act", bm_tm[:, t, q * 128:(q + 1) * 128], pbt[:, tl * 128:(tl + 1) * 128],
                                       [rpbt], [r_bm[t]])
                        yield

        def xscale(dst_i, t, col_ap, r_col):
            tt("pool", xsc[dst_i][:].rearrange("p (h d) -> p h d", h=8), xs_tm[:, t, :].rearrange("p (h d) -> p h d", h=8),
               bc(col_ap, [128, 8, 64], 2), ALU.mult, [r_xs[t], r_col], [r_xsc[dst_i]])

        def state_update(t, d, xdd_i):
            p, rp = PF()
            for g in range(2):
                mm(p[:, g * 256:(g + 1) * 256], bm_tm[:, t, g * 128:(g + 1) * 128], xsc[xdd_i][:, g * 256:(g + 1) * 256],
                   g == 0, g == 1, [r_bm[t], r_xsc[xdd_i]], [rp], g == 1)
            cd = exs[:, t, 32 + d * 8:40 + d * 8]
            S3 = Sst[d][:].rearrange("p (h d) -> p h d", h=8)
            tt("dve", S3, S3, bc(cd, [128, 8, 64], 2), ALU.mult, [r_Sst[d], r_exs], [r_Sst[d]])
            tt("dve", Sst[d][:], Sst[d][:], p[:, :], ALU.add, [r_Sst[d], rp], [r_Sst[d]])

        def pass1_chunk(sq, st, t):
            g = st * 4 + t
            i = nxt("sbf", 2)
            cp("act", Sbf[i][:], Sst[1][:], [r_Sst[1]], [r_Sbf[i]])
            S.dma("pool", dprev[g, :, :], Sbf[i][:], "pst%d" % i, reads=[r_Sbf[i]], writes=[r_dprev[g]])
            xscale(3, t, wfb[:, t, 8:16], r_wfb)
            state_update(t, 1, 3)

        def ssd_chunk(sq, st, t):
            g = st * 4 + t
            tc0 = t * 128
            ip = nxt("pbuf", 2)
            S.dma("sp", pbuf[ip][:], dprev[g, :, :], "pld%d" % ip, reads=[r_dprev[g]], writes=[r_pbuf[ip]])
            pcb, rpcb = PF()
            for gI in range(2):
                mm(pcb[:, gI * 128:(gI + 1) * 128], bcT[:, gI, tc0:tc0 + 128], bcT[:, 2 + gI, tc0:tc0 + 128], gI == 0, gI == 1,
                   [r_bcT[gI][t], r_bcT[2 + gI][t]], [rpcb], gI == 1)
            cp("act", cbs[:], pcb[:, 0:256], [rpcb], [r_cbs])
            for d in range(2):
                for gI in range(2):
                    ir = nxt("rb", 2)
                    U = cf[:, C_SLE:C_SLE + 128] if d == 0 else cf[:, C_SGE:C_SGE + 128]
                    LT = cf[:, C_SGT:C_SGT + 128] if d == 0 else cf[:, C_SLT:C_SLT + 128]
                    tt("pool", Rb[ir][:], bc(U, [128, 4, 128], 1),
                       bc(adt[:, t, d * 8 + gI * 4:d * 8 + gI * 4 + 4], [128, 4, 128], 2), ALU.mult, [r_cf, r_adt], [r_Rb[ir]])
                    p, rp = PF()
                    mm(p[:, :], ident_b, negf4 if d == 0 else negb4, True, False, [r_cb], [rp], False)
                    mm(p[:, :], LT, Rb[ir][:].rearrange("p h l -> p (h l)"), False, True, [r_cf, r_Rb[ir]], [rp], True)
                    act(Lt[ir][:].rearrange("p h l -> p (h l)"), p[:, :], AF.Exp, [rp], [r_Lt[ir]])
                    mi = d * 2 + gI
                    tt("dve", Mt[mi][:], Lt[ir][:], bc(cbs[:, gI * 128:(gI + 1) * 128], [128, 4, 128], 1), ALU.mult,
                       [r_Lt[ir], r_cbs], [r_Mt[mi]])
                    yield
            xscale(0, t, dts[:, t, 0:8], r_dts)
            xscale(1, t, dts[:, t, 8:16], r_dts)
            xscale(2, t, wfb[:, t, 0:8], r_wfb)
            xscale(4, t, ptab[:, P_DSK:P_DSK + 8], r_ptab)
            isb = (cnt["sbf"] - 1) % 2 if cnt["sbf"] > 0 else 0
            e_f = exs[:, t, 8:16]
            e_b = exs[:, t, 24:32]
            v3 = lambda ap: ap.rearrange("p (h d) -> p h d", h=8)
            for d in range(2):
                po, rpo = PF()
                prev_ap, r_prev = (Sbf[isb], r_Sbf[isb]) if d == 0 else (pbuf[ip], r_pbuf[ip])
                for gI in range(2):
                    mm(po[:, gI * 256:(gI + 1) * 256], bcT[:, 2 + gI, tc0:tc0 + 128], prev_ap[:, gI * 256:(gI + 1) * 256],
                       gI == 0, gI == 1, [r_bcT[2 + gI][t], r_prev], [rpo], gI == 1)
                if d == 0:
                    tt("dve", v3(y1[:]), v3(po[:, :]), bc(e_f, [128, 8, 64], 2), ALU.mult, [rpo, r_exs], [r_y1])
                else:
                    tt("dve", v3(y2[:]), v3(po[:, :]), bc(e_b, [128, 8, 64], 2), ALU.mult, [rpo, r_exs], [r_y2])
            py, rpy = PF()
            mm(py[:, :], ident_b, xsc[4][:], True, False, [r_cb, r_xsc[4]], [rpy], False)
            for h in range(8):
                gI, r = h // 4, h % 4
                for d in range(2):
                    lastmm = (h == 7 and d == 1)
                    mm(py[:, h * 64:(h + 1) * 64], Mt[d * 2 + gI][:, r, :], xsc[d][:, h * 64:(h + 1) * 64], False, lastmm,
                       [r_Mt[d * 2 + gI], r_xsc[d]], [rpy], lastmm)
            tt("dve", y1[:], y1[:], y2[:], ALU.add, [r_y1, r_y2], [r_y1])
            tt("dve", y2[:], py[:, :], y1[:], ALU.add, [rpy, r_y1], [r_y2])
            stt(y1[:], y2[:], 0.5, zs[:, t, :], ALU.mult, ALU.mult, [r_y2, r_zs[t]], [r_y1])
            yield
            ss = sm[:, 16:18]
            for gI in range(2):
                act(junk[:, 0:256], y1[:, gI * 256:(gI + 1) * 256], AF.Square, [r_y1], [r_junk, r_ssd_ss], accum=ss[:, gI:gI + 1])
            rstd_from_ss(ss, 2, 1.0 / 256, [r_ssd_ss])
            for gI in range(2):
                stt(yn[:, gI * 256:(gI + 1) * 256], y1[:, gI * 256:(gI + 1) * 256], ss[:, gI:gI + 1],
                    ptab[:, P_SNW + gI * 256:P_SNW + (gI + 1) * 256], ALU.mult, ALU.mult, [r_y1, r_ssd_ss, r_ptab], [r_yn])
            pbt, rpbt = PB()
            for c in range(4):
                tp(pbt[:, c * 128:(c + 1) * 128], yn[:, c * 128:(c + 1) * 128], [r_yn], [rpbt], c == 3)
            for c in range(4):
                cp("act", mixT[:, 8 + c, tc0:tc0 + 128], pbt[:, c * 128:(c + 1) * 128], [rpbt], [r_mix[8 + c][t]])
            yield
            state_update(t, 0, 2)
            i = nxt("sbf", 2)
            cp("act", Sbf[i][:], Sst[0][:], [r_Sst[0]], [r_Sbf[i]])

        def mem_attn(sq, st, ws):
            for u in range(2):
                w, rw = ws.get()
                for cc in range(2):
                    hm = u * 2 + cc
                    p, rp = PF()
                    proj_fm(w, rw, cc * 128, 128, 640, p, rp)
                    cp("act", qmT[:, hm, :], p[:, :], [rp], [r_qm[hm]])
                    yield
            for u in range(2):
                w, rw = ws.get()
                for t in range(4):
                    p, rp = PF()
                    proj_tm(w, rw, 256, 128 + t * 128, p, rp)
                    act(gms[:, t, u * 256:(u + 1) * 256], p[:, 0:256], AF.Silu, [rp], [r_gms[t]])
                    yield
            sc = 1.0 / np.sqrt(128.0)
            for hm in range(4):
                ip = nxt("ptm", 2)
                for mb in range(2):
                    p, rp = PF()
                    m0 = sq * 256 + mb * 128
                    mm(p[:, :], mkT[:, hm, m0:m0 + 128], qmT[:, hm, :], True, True, [r_mkT, r_qm[hm]], [rp], True)
                    act(PTm[ip][:, mb, :], p[:, :], AF.Exp, [rp], [r_PTm[ip]], scale=float(sc))
                yield
                for tp2 in range(2):
                    pv, rpv = PF()
                    first = True
                    for tl in range(2):
                        t = tp2 * 2 + tl
                        for mb in range(2):
                            lastmm = (tl == 1 and mb == 1)
                            mm(pv[:, tl * 256:tl * 256 + 129], PTm[ip][:, mb, t * 128:(t + 1) * 128],
                               mvaug[:, sq * 2 + mb, hm, 0:129], first, lastmm, [r_PTm[ip], r_mv], [rpv], lastmm)
                            first = False
                    pv3 = pv[:, :].rearrange("p (t d) -> p t d", t=2)
                    dn = sm[:, 24:26]
                    S.op("dve", lambda E, dn=dn, pv3=pv3: E.reciprocal(dn, pv3[:, :, 128]), [rpv], [r_mem_dn])
                    for tl in range(2):
                        t = tp2 * 2 + tl
                        ts("pool", grm[tl][:, 0:128], gms[:, t, hm * 128:(hm + 1) * 128], dn[:, tl:tl + 1], None, ALU.mult, None,
                           [r_gms[t], r_mem_dn], [r_grm[tl]])
                        tt("dve", attgm[tl][:, 0:128], pv[:, tl * 256:tl * 256 + 128], grm[tl][:, 0:128], ALU.mult, [rpv, r_grm[tl]], [r_attgm[tl]])
                        pbt, rpbt = PB()
                        tp(pbt[:, 0:128], attgm[tl][:, 0:128], [r_attgm[tl]], [rpbt], True)
                        cp("act", mixT[:, 12 + hm, t * 128:(t + 1) * 128], pbt[:, 0:128], [rpbt], [r_mix[12 + hm][t]])
                    yield

        out_toks = []

        r_oss = [r_out_ss, Reg()]

        def out_proj(sq, st, h=None):
            hh = 0 if h is None else h
            for t in (range(4) if h is None else (2 * h, 2 * h + 1)):
                g = st * 4 + t
                i = nxt("xr", 2) if h is None else h
                if ("xr", sq, st, h, g) in xpre:
                    xpre.discard(("xr", sq, st, h, g))
                else:
                    S.dma("sp", xres[i][:], dx[sq, g * 128:(g + 1) * 128, :], "xr%d" % i, writes=[r_xres[i]])
                for nb in range(2):
                    p, rp = PF()
                    for c in range(16):
                        mm(p[:, :], mixT[:, c, t * 128:(t + 1) * 128], wout[:, c, nb * 512:(nb + 1) * 512], c == 0, c == 15,
                           [r_mix[c][t], r_wout], [rp], c == 15)
                    tt("dve", xres[i][:, nb * 512:(nb + 1) * 512], xres[i][:, nb * 512:(nb + 1) * 512], p[:, :], ALU.add,
                       [r_xres[i], rp], [r_xres[i]])
                ss = sm[:, 32 + hh:33 + hh]
                jk, rjk = (junk, r_junk) if hh == 0 else (junk1, r_y1)
                act(jk[:], xres[i][:], AF.Square, [r_xres[i]], [rjk, r_oss[hh]], accum=ss)
                rstd_from_ss(ss, 1, 1.0 / D, [r_oss[hh]])
                stt(xres[i][:], xres[i][:], ss, ptab[:, P_NOW:P_NOW + 1024], ALU.mult, ALU.mult,
                    [r_xres[i], r_oss[hh], r_ptab], [r_xres[i]])
                tok = S.dma("pool", dout[sq, g * 128:(g + 1) * 128, :], xres[i][:], "ost%d" % i, reads=[r_xres[i]])
                out_toks.append(tok)

        class _Stop(Exception):
            pass

        def stage(n):
            if STOP is not None and n >= STOP:
                raise _Stop()

        try:
          stage(0)
          def run(g):
              for _ in g:
                  pass

          def corun(*streams):
              cos = []
              base = min(S.efree[e] for e in ("pe", "act", "dve", "pool"))
              for g, pool in streams:
                  c = Co(lambda g=g: run(g))
                  c.pool = pool
                  S.sclock[c] = base
                  cos.append(c)
              alive = list(cos)
              while alive:
                  c = min(alive, key=lambda c: S.sclock[c])
                  S.cur = c
                  c.step()
                  S.cur = None
                  if c.done:
                      alive.remove(c)

          def gen(fn, *a):
              fn(*a)
              yield

          def ssd_stream(sq, st):
              ws = mkws(("B", sq, st), [11, 12, 13, 14, 15, 16], 4)
              ssd_dt(sq, st)
              yield from ssd_z(sq, st, ws)
              yield from ssd_conv(sq, st, [0, 1, 2, 3], ws, 0)
              for t in range(4):
                  yield from ssd_chunk(sq, st, t)

          for sq in range(NSEQ):
            S.op("pool", lambda E: E.memset(Sst[1][:], 0.0), [], [r_Sst[1]])
            def upd(sq, st):
                for t in reversed(range(4)):
                    pass1_chunk(sq, st, t)

            hn_stage(sq, NST - 1, None, "halo")
            corun((gen(hn_stage, sq, NST - 1, None, 0), "H1"), (gen(hn_stage, sq, NST - 1, None, 1), "H2"))
            for st in reversed(range(NST)):
                stage(1)
                if st > 0:
                    prefetch_x_hn(sq, st - 1, st)
                corun((ssd_conv(sq, st, [0], mkws(("X", sq, st), [13], 0), 0), "X"),
                      (ssd_conv(sq, st, [1], mkws(("Y", sq, st), [14], 2), 1), "Y"),
                      (ssd_conv(sq, st, [2], mkws(("Z", sq, st), [15], 4), 2), "Z"),
                      (gen(ssd_dt, sq, st), "D"))
                stage(2)
                if st > 0:
                    prefetch_ws(("X", sq, st - 1), [13], 0)
                    prefetch_ws(("Y", sq, st - 1), [14], 2)
                    prefetch_ws(("Z", sq, st - 1), [15], 4)
                    hn_stage(sq, st - 1, st, "halo")
                    corun((gen(upd, sq, st), "X"),
                          (gen(hn_stage, sq, st - 1, st, 0), "H1"), (gen(hn_stage, sq, st - 1, st, 1), "H2"))
                else:
                    upd(sq, st)
                stage(3)
            S.op("pool", lambda E: E.memset(Sst[0][:], 0.0), [], [r_Sst[0]])
            i = nxt("sbf", 2)
            S.op("pool", lambda E, i=i: E.memset(Sbf[i][:], 0.0), [], [r_Sbf[i]])
            for st in range(NST):
                if st == 0:
                    att_cossin(sq, st)
                if st + 1 < NST:
                    prefetch_x_out(sq, st)
                    prefetch_x_hn(sq, st + 1, st)
                corun((att_stream(sq, st, 0), "A1"), (att_stream(sq, st, 1), "A2"), (ssd_stream(sq, st), "B"))
                if st + 1 < NST:
                    prefetch_ws(("K", sq, st + 1), [4, 5], 0)
                    prefetch_ws(("V", sq, st + 1), [6], 2)
                    prefetch_ws(("B", sq, st + 1), [11, 12, 13, 14, 15, 16], 4)
                    att_cossin(sq, st + 1)
                    hn_stage(sq, st + 1, st, "halo")
                    corun((gen(out_proj, sq, st, 0), "O1"), (gen(out_proj, sq, st, 1), "O2"),
                          (gen(hn_stage, sq, st + 1, st, 0), "H1"), (gen(hn_stage, sq, st + 1, st, 1), "H2"))
                else:
                    out_proj(sq, st)
        except _Stop:
            pass
        for e in ("pe", "act", "dve", "pool"):
            if S.cnt[e] > 0:
                nm, h = S.esem[e]
                S.finish_waits("pool", [(nm, h, S.cnt[e])])
        for k, (nm, h, v) in S.dsem.items():
            S.finish_waits("pool", [(nm, h, v)])
        fin = {}
        for nm, h, v in out_toks:
            if nm not in fin or fin[nm][2] < v:
                fin[nm] = (nm, h, v)
        S.finish_waits("pool", list(fin.values()))

        assert not xpre and not wpre, (xpre, list(wpre))
        block = es.enter_context(nc.Block())

        @block.tensor
        def _(E):
            for f in S.prog["pe"]:
                f(E)

        @block.scalar
        def _(E):
            for f in S.prog["act"]:
                f(E)

        @block.vector
        def _(E):
            for f in S.prog["dve"]:
                f(E)

        @block.gpsimd
        def _(E):
            for f in S.prog["pool"]:
                f(E)

        @block.sync
        def _(E):
            for f in S.prog["sp"]:
                f(E)
    return nc


def _tables(L):
    j = np.arange(128)[:, None]
    s = np.arange(128)[None, :]
    cf = np.concatenate([(j > s), (j <= s), (j < s), (j >= s), np.ones((128, 128), bool)], axis=1).astype(np.float32)
    ident = np.eye(128, dtype=np.float32)
    negf = np.where(j > s, NEG, 0.0).astype(np.float32)
    negb = np.where(j < s, NEG, 0.0).astype(np.float32)
    rp = np.zeros((128, 128), np.float32)
    for m in range(128):
        if (m % 64) < 32:
            rp[m + 32, m] = -1.0
        else:
            rp[m - 32, m] = 1.0
    cb = np.concatenate([ident, np.tile(negf, (1, 4)), np.tile(negb, (1, 4)), rp], axis=1).astype(np.float32)
    inv_freq = (1.0 / (np.float32(10000.0) ** (np.arange(0, 64, 2, dtype=np.float32) / np.float32(64.0)))).astype(np.float32)
    pos = np.arange(L, dtype=np.float32)
    ang = (pos[None, :] * inv_freq[np.arange(128) % 32][:, None]).astype(np.float32)
    cosT = np.zeros((128, L + 256), np.float32)
    sinT = np.zeros((128, L + 256), np.float32)
    cosT[:, 128:128 + L] = np.cos(ang)
    sinT[:, 128:128 + L] = np.sin(ang)
    return cf, cb, cosT, sinT


def _prep_shared(inp, L):
    w_in = np.asarray(inp["w_in"], np.float32)[0]
    cols = []
    for j in range(4):
        cols.append(np.arange(256 * j, 256 * j + 256))
    kb = 1024
    for u in range(2):
        a = kb + (2 * u) * 64 + np.arange(64)
        b = kb + (2 * u + 1) * 64 + np.arange(64)
        cols.append(np.concatenate([a, a, b, b]))
    cols.append(np.arange(1280, 1536))
    for j in range(4):
        cols.append(np.arange(1536 + 256 * j, 1536 + 256 * j + 256))
    for j in range(2):
        cols.append(np.arange(2560 + 256 * j, 2560 + 256 * j + 256))
    for j in range(4):
        cols.append(np.arange(3072 + 256 * j, 3072 + 256 * j + 256))
    for j in range(2):
        cols.append(np.arange(4112 + 256 * j, 4112 + 256 * j + 256))
    for j in range(2):
        cols.append(np.arange(4624 + 256 * j, 4624 + 256 * j + 256))
    assert len(cols) == NU
    w3 = w_in.reshape(8, 128, -1)
    win_u = np.stack([w3[:, :, c].transpose(1, 0, 2).reshape(128, 2048) for c in cols], axis=0)
    wdt = w3[:, :, 4096:4112].transpose(1, 0, 2).reshape(128, 128)
    wout_p = np.asarray(inp["w_out"], np.float32)[0].reshape(16, 128, 1024).transpose(1, 0, 2).reshape(128, 16 * 1024)
    wmem_p = np.asarray(inp["w_mem_kv"], np.float32)[0].reshape(8, 128, 1024).transpose(1, 0, 2).reshape(128, 8 * 1024)
    pt = np.zeros((128, PT), np.float32)
    pt[:, P_NIN:P_NIN + 8] = np.asarray(inp["norm_in_w"], np.float32)[0].reshape(8, 128).T
    pt[:, P_NMEM:P_NMEM + 8] = np.asarray(inp["norm_mem_w"], np.float32).reshape(8, 128).T
    cw = np.asarray(inp["conv_w"], np.float32)[0]
    pt[:, P_CW:P_CW + 40] = cw.reshape(5, 8, 128).transpose(2, 1, 0).reshape(128, 40)
    pt[:, P_CB:P_CB + 8] = np.asarray(inp["conv_b"], np.float32)[0].reshape(8, 128).T
    pt[:, P_DTB:P_DTB + 16] = np.asarray(inp["dt_bias"], np.float32)[0].reshape(1, 16)
    pt[:, P_ALOG:P_ALOG + 16] = np.asarray(inp["a_log"], np.float32)[0].reshape(1, 16)
    pt[:, P_DSK:P_DSK + 8] = np.asarray(inp["d_skip"], np.float32)[0].reshape(1, 8)
    pt[:, P_SNW:P_SNW + 512] = np.asarray(inp["ssd_norm_w"], np.float32)[0].reshape(1, 512)
    pt[:, P_NOW:P_NOW + 1024] = np.asarray(inp["norm_out_w"], np.float32).reshape(1, 1024)
    pt[:, P_SINK:P_SINK + 16] = np.asarray(inp["attn_sink"], np.float32)[0].reshape(1, 16)
    pt[:, P_EPS] = 1e-6
    pt[:, P_NEGH:P_NEGH + 2] = -0.5
    cf, cb, cosT, sinT = _tables(L)
    return {"win_u": np.ascontiguousarray(win_u), "wdt": np.ascontiguousarray(wdt),
            "wout_p": np.ascontiguousarray(wout_p), "wmem_p": np.ascontiguousarray(wmem_p), "ptab": pt,
            "cf32": cf, "cb16": cb, "cosT": cosT, "sinT": sinT}


def run(inp, n_cores):
    x = np.asarray(inp["x"], np.float32)
    mem = np.asarray(inp["mem"], np.float32)
    B, L, _ = x.shape
    assert B % n_cores == 0
    nseq = B // n_cores
    shared = _prep_shared(inp, L)
    nc = build_nc(L, nseq)
    in_maps = []
    for i in range(n_cores):
        m = dict(shared)
        m["x"] = np.ascontiguousarray(x[i * nseq:(i + 1) * nseq])
        m["mem"] = np.ascontiguousarray(mem[i * nseq:(i + 1) * nseq])
        in_maps.append(m)
    res = run_bass_kernel_spmd(nc, in_maps, core_ids=list(range(n_cores)))
    return np.concatenate([np.asarray(r["out"], np.float32) for r in res.results], axis=0)


def kernel(**inputs):
    return run(inputs, 8)
```

```python
import threading
import numpy as np
from contextlib import ExitStack
import concourse.bass as bass
import concourse.mybir as mybir
from concourse.bass_utils import run_bass_kernel_spmd

F32 = mybir.dt.float32
BF16 = mybir.dt.bfloat16
AF = mybir.ActivationFunctionType
ALU = mybir.AluOpType

D = 1024
NU = 21
SAME_SYNC = True
EPOCH = 30000
NEG = -30000.0
STOP = None
DMA_SCRATCH = 4096
CO_A, CO_B = 1, 1

P_NIN, P_NMEM, P_CW, P_CB, P_DTB, P_ALOG, P_DSK, P_SNW, P_NOW, P_SINK, P_EPS = (
    0, 8, 16, 56, 64, 80, 96, 104, 616, 1640, 1656)
P_NEGH = 1657
PT = 1659
C_SGT, C_SLE, C_SLT, C_SGE, C_ONES = 0, 128, 256, 384, 512
B_ID, B_NF, B_NB, B_RP = 0, 128, 640, 1152


class Reg:
    __slots__ = ("w", "r", "name", "tw", "tr")

    def __init__(self, name=""):
        self.w = None
        self.r = []
        self.name = name
        self.tw = 0.0
        self.tr = 0.0


class Co:
    def __init__(self, fn):
        self.fn = fn
        self.go = threading.Semaphore(0)
        self.back = threading.Semaphore(0)
        self.done = False
        self.exc = None
        self.th = threading.Thread(target=self._run, daemon=True)
        self.th.start()

    def _run(self):
        self.go.acquire()
        try:
            self.fn()
        except BaseException as e:
            self.exc = e
        self.done = True
        self.back.release()

    def step(self):
        self.go.release()
        self.back.acquire()
        if self.exc is not None:
            raise self.exc

    def pause(self):
        self.back.release()
        self.go.acquire()


class Sched:
    def __init__(self, nc, es):
        self.nc = nc
        self.es = es
        self.names = ["pe", "act", "dve", "pool", "sp"]
        self.prog = {k: [] for k in self.names}
        self.cnt = {k: 0 for k in self.names}
        self.epoch = {k: 0 for k in self.names}
        self.esem = {}
        self.seen = {k: {} for k in self.names}
        self.pend = {k: ([], []) for k in self.names}
        self.dsem = {}
        self.nsem = 0
        self.efree = {k: 0.0 for k in self.names}
        self.cur = None
        self.hold = 0
        self.sclock = {}
        for k in self.names:
            self._newepoch(k)

    def _vt(self, e, reads, writes, cost):
        t = self.efree[e]
        for r in reads:
            t = max(t, getattr(r, "tw", 0.0) + 0.15)
        for w in writes:
            t = max(t, getattr(w, "tw", 0.0) + 0.15, getattr(w, "tr", 0.0) + 0.15)
        fin = t + cost
        for r in reads:
            r.tr = max(getattr(r, "tr", 0.0), fin)
        for w in writes:
            w.tw = fin
            w.tr = 0.0
        return t, fin

    def _sem(self, name):
        self.nsem += 1
        return self.es.enter_context(self.nc.semaphore(name))

    def _newepoch(self, e):
        self.epoch[e] += 1
        nm = "%s#%d" % (e, self.epoch[e])
        self.esem[e] = (nm, self._sem("s_%s_%d" % (e, self.epoch[e])))
        self.cnt[e] = 0

    def _waits(self, e, reads, writes):
        waits = {}

        def need(t):
            if t is None:
                return
            nm, h, v = t
            if nm in waits:
                if v > waits[nm][1]:
                    waits[nm] = (h, v)
            else:
                waits[nm] = (h, v)

        for r in reads:
            need(r.w)
        for w in writes:
            need(w.w)
            for t in w.r:
                need(t)
        for nm, (h, v) in waits.items():
            if self.seen[e].get(nm, 0) >= v:
                continue
            if nm.split("#")[0] == e:
                if e == "pe" or not SAME_SYNC:
                    continue
            self.seen[e][nm] = v
            self.prog[e].append(lambda E, h=h, v=v: E.wait_ge(h, v))

    def op(self, e, fn, reads=(), writes=(), inc=True, cost=0.3):
        for k in self.names:
            if k != e:
                assert not self.pend[k][0] and not self.pend[k][1], "pending ops on %s" % k
        self._waits(e, reads, writes)
        t0, fin = self._vt(e, reads, writes, cost)
        self.efree[e] = fin
        if self.cur is not None:
            self.sclock[self.cur] = t0
        if inc:
            self.cnt[e] += 1
            v = self.cnt[e]
            nm, h = self.esem[e]
            self.prog[e].append(lambda E, fn=fn, h=h: fn(E).then_inc(h, 1))
            tok = (nm, h, v)
            pr, pw = self.pend[e]
            for r in list(reads) + pr:
                r.r.append(tok)
            for w in list(writes) + pw:
                w.w = tok
                w.r = []
            self.pend[e] = ([], [])
            if v >= EPOCH:
                self._newepoch(e)
            if self.cur is not None and self.hold == 0:
                self.cur.pause()
        else:
            self.prog[e].append(lambda E, fn=fn: fn(E))
            self.pend[e][0].extend(reads)
            self.pend[e][1].extend(writes)

    def dma(self, q, out, in_, key, reads=(), writes=(), cost=4.5):
        for k in self.names:
            assert not self.pend[k][0] and not self.pend[k][1], "pending ops on %s" % k
        self._waits(q, reads, writes)
        t0, fin = self._vt(q, reads, writes, cost)
        self.efree[q] = t0 + (1.0 if q == "pool" else 0.1)
        if key not in self.dsem:
            self.dsem[key] = ["d:" + key, self._sem("d_" + key), 0]
        ds = self.dsem[key]
        ds[2] += 16
        nm, h, v = ds
        self.prog[q].append(lambda E, h=h, out=out, in_=in_: E.dma_start(out=out, in_=in_).then_inc(h, 16))
        tok = (nm, h, v)
        for r in reads:
            r.r.append(tok)
        for w in writes:
            w.w = tok
            w.r = []
        return tok

    def finish_waits(self, e, toks):
        for nm, h, v in toks:
            self.prog[e].append(lambda E, h=h, v=v: E.wait_ge(h, v))


def build_nc(L, NSEQ, dbg=False):
    NT = L // 128
    NST = L // 512
    nc = bass.Bass("TRN2", target_bir_lowering=False, dynamic_dma_scratch_size=DMA_SCRATCH)
    dx = nc.dram_tensor("x", [NSEQ, L, D], F32, kind="ExternalInput").ap()
    dmem = nc.dram_tensor("mem", [NSEQ, 256, D], F32, kind="ExternalInput").ap()
    dwin = nc.dram_tensor("win_u", [NU, 128, 2048], F32, kind="ExternalInput").ap()
    dwdt = nc.dram_tensor("wdt", [128, 8 * 16], F32, kind="ExternalInput").ap()
    dwout = nc.dram_tensor("wout_p", [128, 16 * 1024], F32, kind="ExternalInput").ap()
    dwmem = nc.dram_tensor("wmem_p", [128, 8 * 1024], F32, kind="ExternalInput").ap()
    dptab = nc.dram_tensor("ptab", [128, PT], F32, kind="ExternalInput").ap()
    dcf = nc.dram_tensor("cf32", [128, 640], F32, kind="ExternalInput").ap()
    dcb = nc.dram_tensor("cb16", [128, 1280], F32, kind="ExternalInput").ap()
    dcos = nc.dram_tensor("cosT", [128, L + 256], F32, kind="ExternalInput").ap()
    dsin = nc.dram_tensor("sinT", [128, L + 256], F32, kind="ExternalInput").ap()
    dout = nc.dram_tensor("out", [NSEQ, L, D], F32, kind="ExternalOutput").ap()
    dwbf = nc.dram_tensor("wbf", [NU, 128, 2048], BF16).ap()
    dprev = nc.dram_tensor("prevb", [NT, 128, 512], BF16).ap()

    es = ExitStack()
    with es:
        S = Sched(nc, es)

        def sb(name, shape, dt):
            return es.enter_context(nc.sbuf_tensor("s_" + name, shape, dt))

        def psum(name, shape, dt):
            return es.enter_context(nc.psum_tensor("p_" + name, shape, dt))

        ptab = sb("ptab", [128, PT], F32); r_ptab = Reg()
        cf = sb("cf", [128, 640], F32); r_cf = Reg()
        cb = sb("cb", [128, 1280], BF16); r_cb = Reg()
        wout = sb("wout", [128, 16, 1024], BF16); r_wout = Reg()
        wdt = sb("wdtb", [128, 8, 16], BF16); r_wdt = Reg()
        NW = 6
        wring = [sb("wr%d" % i, [128, 8, 256], BF16) for i in range(NW)]
        r_wring = [Reg() for _ in range(NW)]
        mkT = sb("mkT", [128, 4, NSEQ * 256], BF16); r_mkT = Reg()
        mvaug = sb("mvaug", [128, NSEQ * 2, 4, 130], BF16); r_mv = Reg()
        xin = [sb("xin%d" % i, [128, D], F32) for i in range(2)]; r_xin = [Reg() for _ in range(2)]
        xres = [sb("xres%d" % i, [128, D], F32) for i in range(2)]; r_xres = [Reg() for _ in range(2)]
        hnT = sb("hnT", [128, 8, 768], BF16); r_hn = [Reg() for _ in range(6)]
        cosb = sb("cosb", [128, 768], F32); sinb = sb("sinb", [128, 768], F32); r_cs = Reg()
        qT = [sb("qT%d" % i, [128, 2, 512], BF16) for i in range(2)]; r_qT = [[Reg(), Reg()] for _ in range(2)]
        kTlo = sb("kTlo", [128, 4, 768], BF16); kThi = sb("kThi", [128, 4, 768], BF16); r_kT = [Reg() for _ in range(4)]
        vaug = sb("vaug", [128, 6, 4, 66], BF16); r_v = [Reg() for _ in range(6)]
        rp0 = sb("rp0", [128, 512], BF16); r_rp0 = Reg()
        rp1 = sb("rp1", [128, 512], F32); r_rp1 = Reg()
        rp2 = sb("rp2", [128, 512], F32); r_rp2 = Reg()
        rq0 = sb("rq0", [128, 512], BF16); r_rq0 = Reg()
        rq1 = sb("rq1", [128, 512], F32); r_rq1 = Reg()
        rq2 = sb("rq2", [128, 512], F32); r_rq2 = Reg()
        xsb = [rp1[:].bitcast(BF16), rp2[:].bitcast(BF16)]; r_xsb = [r_rp1, r_rp2]
        PTb = [sb("PT%d" % i, [128, 3, 512], BF16) for i in range(2)]; r_PT = [Reg(), Reg()]
        gs = [sb("gs%d" % i, [128, 4, 256], BF16) for i in range(2)]; r_gs = [[Reg() for _ in range(4)] for _ in range(2)]
        r_dn = [Reg(), Reg()]; r_ssd_ss = Reg(); r_mem_dn = Reg(); r_out_ss = Reg()
        sm = sb("sm", [128, 64], F32); r_sm = Reg()
        gr = [sb("gr%d" % i, [128, 256], F32) for i in range(2)]; r_gr = [Reg(), Reg()]
        attg = [sb("attg%d" % i, [128, 256], BF16) for i in range(2)]; r_attg = [Reg(), Reg()]
        grm = gr; r_grm = r_gr; attgm = attg; r_attgm = r_attg
        mixT = sb("mixT", [128, 16, 512], BF16); r_mix = [[Reg() for _ in range(4)] for _ in range(16)]
        dts = sb("dts", [128, 4, 16], F32); r_dts = Reg()
        adt = sb("adt", [128, 4, 16], F32); r_adt = Reg()
        exs = sb("exs", [128, 4, 48], F32); r_exs = Reg()
        wfb = sb("wfb", [128, 4, 16], F32); r_wfb = Reg()
        zs = sb("zs", [128, 4, 512], BF16); r_zs = [Reg() for _ in range(4)]
        cacc = [sb("cacc%d" % i, [128, 256], F32) for i in range(3)]; r_cacc = [Reg() for _ in range(3)]
        cth = [sb("cth%d" % i, [128, 256], F32) for i in range(3)]; r_cth = [Reg() for _ in range(3)]
        cbs = cacc[0]; r_cbs = r_cacc[0]
        xsTt = [sb("xsTt%d" % i, [128, 256], BF16) for i in range(3)]; r_xsTt = [Reg() for _ in range(3)]
        bcT = sb("bcT", [128, 4, 512], BF16); r_bcT = [[Reg() for _ in range(4)] for _ in range(4)]
        xs_tm = sb("xs_tm", [128, 4, 512], BF16); r_xs = [Reg() for _ in range(4)]
        bm_tm = sb("bm_tm", [128, 4, 256], BF16); r_bm = [Reg() for _ in range(4)]
        Rb = [sb("Rb%d" % i, [128, 4, 128], F32) for i in range(2)]; r_Rb = [Reg(), Reg()]
        Lt = [sb("Lt%d" % i, [128, 4, 128], BF16) for i in range(2)]; r_Lt = [Reg(), Reg()]
        Mt = [sb("Mt%d" % i, [128, 4, 128], BF16) for i in range(4)]; r_Mt = [Reg() for _ in range(4)]
        xsc = [sb("xsc%d" % i, [128, 512], BF16) for i in range(5)]; r_xsc = [Reg() for _ in range(5)]
        Sst = [sb("Sst%d" % i, [128, 512], F32) for i in range(2)]; r_Sst = [Reg(), Reg()]
        Sbf = [sb("Sbf%d" % i, [128, 512], BF16) for i in range(2)]; r_Sbf = [Reg(), Reg()]
        pbuf = [sb("pbuf%d" % i, [128, 512], BF16) for i in range(2)]; r_pbuf = [Reg(), Reg()]
        y1 = sb("y1", [128, 512], F32); r_y1 = Reg()
        y2 = sb("y2", [128, 512], F32); r_y2 = Reg()
        junk = y2[:].bitcast(BF16); r_junk = r_y2
        yn = sb("yn", [128, 512], BF16); r_yn = Reg()
        qmT = zs; r_qm = r_zs
        gms = xs_tm; r_gms = r_xs
        PTm = PTb; r_PTm = r_PT
        wmem = mixT[:].rearrange("p (a b) c -> p a (b c)", a=8); r_wmem = Reg()
        memT = hnT[:, :, 0:NSEQ * 256]; r_memT = Reg()
        r_dwbf = [Reg() for _ in range(NU)]
        r_dprev = [Reg() for _ in range(NT)]

        pf = [psum("pf%d" % i, [128, 512], F32) for i in range(6)]; r_pf = [Reg() for _ in range(6)]
        pb = [psum("pb%d" % i, [128, 1024], BF16) for i in range(2)]; r_pb = [Reg(), Reg()]
        cnt = {"pf": 0, "pb": 0, "x": 0, "xr": 0, "xs": 0, "pt": 0, "ca": 0, "rb": 0, "ptm": 0, "w": 0, "pbuf": 0, "sbf": 0}

        def nxt(k, n):
            i = cnt[k] % n
            cnt[k] += 1
            return i

        pools = {"A1": ([0, 1], [0]), "A2": ([2, 3], [0]), "B": ([4, 5], [1]),
                 "X": ([0, 1], [0]), "Y": ([2, 3], [1]), "Z": ([4], [0]), "D": ([5], [1]),
                 "O1": ([0, 1, 2], [0]), "O2": ([3, 4, 5], [1]), "H1": ([0], [0]), "H2": ([3], [1]), None: ([0, 1, 2, 3, 4, 5], [0, 1])}
        pcnt = {}

        def PF():
            k = getattr(S.cur, "pool", None)
            lst = pools[k][0]
            n = pcnt.get(("f", k), 0)
            pcnt[("f", k)] = n + 1
            i = lst[n % len(lst)]
            return pf[i], r_pf[i]

        def PB():
            k = getattr(S.cur, "pool", None)
            lst = pools[k][1]
            n = pcnt.get(("b", k), 0)
            pcnt[("b", k)] = n + 1
            i = lst[n % len(lst)]
            return pb[i], r_pb[i]

        def fsz(ap):
            return ap.free_size()

        def mm(out, lhsT, rhs, start, stop, reads, writes, inc):
            c = max(64, fsz(rhs)) / 1800.0 * (4.0 if rhs.dtype == F32 else 1.0)
            S.op("pe", lambda E: E.matmul(out, lhsT, rhs, start=start, stop=stop), reads, writes, inc, cost=c)

        def tp(out, in_, reads, writes, inc):
            S.op("pe", lambda E: E.transpose(out, in_, cb[:, B_ID:B_ID + 128]), list(reads) + [r_cb], writes, inc, cost=0.11)

        ECOST = {"act": (0.2, 0.00085), "dve": (0.12, 0.0011), "pool": (0.15, 0.0022)}

        def ecost(e, out):
            a, b = ECOST[e]
            return a + b * fsz(out)

        def act(out, in_, func, reads, writes, bias=None, scale=None, accum=None):
            kw = {}
            if bias is not None:
                kw["bias"] = bias
            if scale is not None:
                kw["scale"] = scale
            if accum is not None:
                kw["accum_out"] = accum
            S.op("act", lambda E: E.activation(out, in_, func, **kw), reads, writes,
                 cost=ecost("act", in_) + (0.1 if accum is not None else 0.0))

        def tt(e, out, in0, in1, op, reads, writes):
            S.op(e, lambda E: E.tensor_tensor(out, in0, in1, op), reads, writes, cost=ecost(e, out))

        def ts(e, out, in0, s1, s2, op0, op1, reads, writes):
            if op1 is None:
                S.op(e, lambda E: E.tensor_scalar(out, in0, s1, None, op0), reads, writes, cost=ecost(e, out))
            else:
                S.op(e, lambda E: E.tensor_scalar(out, in0, s1, s2, op0, op1), reads, writes, cost=ecost(e, out))

        def stt(out, in0, sc, in1, op0, op1, reads, writes):
            S.op("dve", lambda E: E.scalar_tensor_tensor(out, in0, sc, in1, op0, op1), reads, writes, cost=ecost("dve", out))

        def cp(e, out, in_, reads, writes):
            if e == "act":
                S.op("act", lambda E: E.copy(out, in_), reads, writes, cost=ecost("act", out))
            else:
                S.op(e, lambda E: E.tensor_copy(out, in_), reads, writes, cost=ecost(e, out))

        def bc(ap, shape, axis):
            return ap.unsqueeze(axis).to_broadcast(shape)

        S.dma("sp", ptab[:], dptab[:, :], "ptab", writes=[r_ptab])
        S.dma("sp", cf[:], dcf[:, :], "cf", writes=[r_cf])
        S.dma("pool", cb[:], dcb[:, :], "cb", writes=[r_cb])
        S.dma("pool", wdt[:].rearrange("p a b -> p (a b)"), dwdt[:, :], "wdt", writes=[r_wdt])
        for kc in range(8):
            S.dma("pool", wmem[:, kc, :], dwmem[:, kc * 1024:(kc + 1) * 1024], "wmem", writes=[r_wmem])
        A_ap = ptab[:, P_ALOG:P_ALOG + 16]
        act(A_ap, A_ap, AF.Exp, [r_ptab], [r_ptab])
        ts("dve", A_ap, A_ap, -1.0, None, ALU.mult, None, [r_ptab], [r_ptab])
        act(ptab[:, P_SINK:P_SINK + 16], ptab[:, P_SINK:P_SINK + 16], AF.Exp, [r_ptab], [r_ptab])
        eps_ap = ptab[:, P_EPS:P_EPS + 1]
        ts("dve", ptab[:, P_CW:P_CW + 48], ptab[:, P_CW:P_CW + 48], 0.5, None, ALU.mult, None, [r_ptab], [r_ptab])
        S.op("pool", lambda E: E.memset(vaug[:], 1.0), [], r_v)
        S.op("pool", lambda E: E.memset(kTlo[:], 0.0), [], r_kT)
        S.op("pool", lambda E: E.memset(kThi[:], 0.0), [], r_kT)
        S.op("pool", lambda E: E.memset(mvaug[:], 1.0), [], [r_mv])

        def rstd_from_ss(ss_ap, n, scale, reads_writes):
            ts("dve", ss_ap, ss_ap, float(scale), 1e-6, ALU.mult, ALU.add, reads_writes, reads_writes)
            tt("pool", ss_ap, ss_ap, ptab[:, P_NEGH:P_NEGH + n], ALU.pow, reads_writes + [r_ptab], reads_writes)

        r_hnss = [r_sm, Reg()]
        junk1 = y1[:].bitcast(BF16)

        def norm_transpose(src_ap, xslot, r_x, wcol, dst_fn, r_dst, h=None):
            hh = 0 if h is None else h
            ss = sm[:, hh:hh + 1]
            jk, rjk = (junk, r_junk) if hh == 0 else (junk1, r_y1)
            act(jk[:], src_ap, AF.Square, [r_x], [rjk, r_hnss[hh]], accum=ss)
            rstd_from_ss(ss, 1, 1.0 / D, [r_hnss[hh]])
            i = nxt("xs", 2) if h is None else h
            act(xsb[i][:], src_ap, AF.Identity, [r_x, r_hnss[hh]], [r_xsb[i]], scale=ss)
            pbt, r_pbt = PB()
            for c in range(8):
                tp(pbt[:, c * 128:(c + 1) * 128], xsb[i][:, c * 128:(c + 1) * 128], [r_xsb[i]], [r_pbt], c == 7)
            tt("dve", dst_fn(), pbt[:].rearrange("p (c t) -> p c t", c=8),
               bc(ptab[:, wcol:wcol + 8], [128, 8, 128], 2), ALU.mult, [r_pbt, r_ptab], r_dst)

        for sq in range(NSEQ):
            for mt in range(2):
                i = nxt("x", 2)
                S.dma("sp", xin[i][:], dmem[sq, mt * 128:(mt + 1) * 128, :], "xin%d" % i, writes=[r_xin[i]])
                c0 = sq * 256 + mt * 128
                norm_transpose(xin[i][:], i, r_xin[i], P_NMEM, lambda c0=c0: memT[:, :, c0:c0 + 128], [r_memT])
        for sq in range(NSEQ):
            for hm in range(4):
                p, rp = PF()
                for kc in range(8):
                    mm(p[:, 0:256], wmem[:, kc, hm * 128:(hm + 1) * 128], memT[:, kc, sq * 256:(sq + 1) * 256],
                       kc == 0, kc == 7, [r_wmem, r_memT], [rp], kc == 7)
                cp("act", mkT[:, hm, sq * 256:(sq + 1) * 256], p[:, 0:256], [rp], [r_mkT])
            for mt in range(2):
                p, rp = PF()
                for kc in range(8):
                    mm(p[:, :], memT[:, kc, sq * 256 + mt * 128: sq * 256 + (mt + 1) * 128], wmem[:, kc, 512:1024],
                       kc == 0, kc == 7, [r_wmem, r_memT], [rp], kc == 7)
                cp("dve", mvaug[:, sq * 2 + mt, :, 0:128], p[:, :].rearrange("p (h d) -> p h d", h=4), [rp], [r_mv])
        for rr in r_hn:
            rr.w = r_memT.w; rr.r = list(r_memT.r)
        for c in range(16):
            for t in range(4):
                r_mix[c][t].w = r_wmem.w; r_mix[c][t].r = list(r_wmem.r)
        last = None
        for u in range(NU):
            i = u % NW
            S.dma("pool", wring[i][:].rearrange("p a b -> p (a b)"), dwin[u, :, :], "wst%d" % i, writes=[r_wring[i]])
            last = S.dma("pool", dwbf[u, :, :], wring[i][:].rearrange("p a b -> p (a b)"), "wsv%d" % i,
                         reads=[r_wring[i]], writes=[r_dwbf[u]])
        for kc in range(16):
            S.dma("pool", wout[:, kc, :], dwout[:, kc * 1024:(kc + 1) * 1024], "wout", writes=[r_wout])
        r_wout.w = (S.dsem["wout"][0], S.dsem["wout"][1], S.dsem["wout"][2])

        def wload(u, i):
            S.dma("sp", wring[i][:].rearrange("p a b -> p (a b)"), dwbf[u, :, :], "wld%d" % i,
                  reads=[r_dwbf[u]], writes=[r_wring[i]])
            return wring[i], r_wring[i]

        class WStream:
            def __init__(self, units, base):
                self.units = list(units)
                self.base = base
                self.k = 0
                self.loaded = []

            def prefetch(self):
                while len(self.loaded) < min(self.k + 2, len(self.units)):
                    n = len(self.loaded)
                    self.loaded.append(wload(self.units[n], self.base + n % 2))

            def get(self):
                self.prefetch()
                r = self.loaded[self.k]
                self.k += 1
                return r

        wpre = {}

        def mkws(key, units, base):
            if key in wpre:
                return wpre.pop(key)
            return WStream(units, base)

        def prefetch_ws(key, units, base):
            w = WStream(units, base)
            w.prefetch()
            wpre[key] = w

        ident_b = cb[:, B_ID:B_ID + 128]
        negf4 = cb[:, B_NF:B_NF + 512]
        negb4 = cb[:, B_NB:B_NB + 512]
        rperm = cb[:, B_RP:B_RP + 128]

        def hn_stage(sq, st, prev_st=None, h=None):
            reuse = {}
            if prev_st is not None and prev_st == st - 1:
                reuse = {0: 4, 1: 5}
            elif prev_st is not None and prev_st == st + 1:
                reuse = {4: 0, 5: 1}
            if h is None or h == "halo":
                for j in sorted(reuse, reverse=(4 in reuse)):
                    sj = reuse[j]
                    cp("dve", hnT[:, :, j * 128:(j + 1) * 128], hnT[:, :, sj * 128:(sj + 1) * 128], [r_hn[sj]], [r_hn[j]])
            if h == "halo":
                return
            todo = [j for j in range(6) if j not in reuse]
            for n, j in enumerate(todo):
                if h is not None and n % 2 != h:
                    continue
                g = st * 4 - 1 + j
                dst = hnT[:, :, j * 128:(j + 1) * 128]
                if g < 0 or g >= NT:
                    S.op("pool", lambda E, dst=dst: E.memset(dst, 0.0), [], [r_hn[j]])
                    continue
                i = nxt("x", 2) if h is None else h
                S.dma("sp", xin[i][:], dx[sq, g * 128:(g + 1) * 128, :], "xin%d" % i, writes=[r_xin[i]])
                norm_transpose(xin[i][:], i, r_xin[i], P_NIN, lambda dst=dst: dst, [r_hn[j]], h)

        def hn_regs(lo, hi):
            return [r_hn[j] for j in range(lo // 128, (hi - 1) // 128 + 1)]

        def proj_fm(w, rw, col0, lo, hi, pbank, rp, pcol0=0):
            for kc in range(8):
                mm(pbank[:, pcol0:pcol0 + hi - lo], w[:, kc, col0:col0 + 128], hnT[:, kc, lo:hi],
                   kc == 0, kc == 7, [rw] + hn_regs(lo, hi), [rp], kc == 7)

        def proj_tm(w, rw, ncol, tcol, pbank, rp, wcol0=0, pcol0=0):
            for kc in range(8):
                mm(pbank[:, pcol0:pcol0 + ncol], hnT[:, kc, tcol:tcol + 128], w[:, kc, wcol0:wcol0 + ncol],
                   kc == 0, kc == 7, [rw] + hn_regs(tcol, tcol + 128), [rp], kc == 7)

        def rope(p, rp, n, csl, dst, r_dst, sx=0):
            a0, ra0 = (rp0, r_rp0) if sx == 0 else (rq0, r_rq0)
            a1, ra1 = (rp1, r_rp1) if sx == 0 else (rq1, r_rq1)
            a2, ra2 = (rp2, r_rp2) if sx == 0 else (rq2, r_rq2)
            cp("act", a0[:, 0:n], p[:, 0:n], [rp], [ra0])
            p2, rp2_ = PF()
            mm(p2[:, 0:n], rperm, a0[:, 0:n], True, True, [r_cb, ra0], [rp2_], True)
            tt("dve", a1[:, 0:n], p2[:, 0:n], sinb[:, csl:csl + n], ALU.mult, [rp2_, r_cs], [ra1])
            tt("dve", a2[:, 0:n], p[:, 0:n], cosb[:, csl:csl + n], ALU.mult, [rp, r_cs], [ra2])
            tt("pool", dst, a1[:, 0:n], a2[:, 0:n], ALU.add, [ra1, ra2], r_dst)

        def rope_k(p, rp, n, csl, j, lo, hi):
            cp("act", rp0[:, 0:n], p[:, 0:n], [rp], [r_rp0])
            p2, rp2_ = PF()
            mm(p2[:, 0:n], rperm, rp0[:, 0:n], True, True, [r_cb, r_rp0], [rp2_], True)
            tt("dve", rp1[:, 0:n], p2[:, 0:n], sinb[:, csl:csl + n], ALU.mult, [rp2_, r_cs], [r_rp1])
            tt("dve", rp2[:, 0:n], p[:, 0:n], cosb[:, csl:csl + n], ALU.mult, [rp, r_cs], [r_rp2])
            tt("pool", kTlo[0:64, j, lo:hi], rp1[0:64, 0:n], rp2[0:64, 0:n], ALU.add, [r_rp1, r_rp2], [r_kT[j]])
            tt("pool", kThi[64:128, j, lo:hi], rp1[64:128, 0:n], rp2[64:128, 0:n], ALU.add, [r_rp1, r_rp2], [r_kT[j]])

        def att_cossin(sq, st):
            S0 = st * 512
            S.dma("sp", cosb[:], dcos[:, S0:S0 + 768], "cos", writes=[r_cs])
            S.dma("sp", sinb[:], dsin[:, S0:S0 + 768], "sin", writes=[r_cs])

        def att_prologue_k(sq, st):
            ws = mkws(("K", sq, st), [4, 5], 0)
            for u in range(2):
                w, rw = ws.get()
                for jj in range(2):
                    j = u * 2 + jj
                    for (lo, hi) in ((0, 512), (512, 768)):
                        p, rp = PF()
                        proj_fm(w, rw, jj * 128, lo, hi, p, rp)
                        rope_k(p, rp, hi - lo, lo, j, lo, hi)

        def att_prologue_v(sq, st):
            ws = mkws(("V", sq, st), [6], 2)
            w, rw = ws.get()
            for j6 in range(6):
                p, rp = PF()
                proj_tm(w, rw, 256, j6 * 128, p, rp)
                cp("act", vaug[:, j6, :, 0:64], p[:, 0:256].rearrange("p (k d) -> p k d", k=4), [rp], [r_v[j6]])

        kvflag = {}

        def att_stream(sq, st, sx):
            if sx == 0:
                att_prologue_k(sq, st)
                kvflag[(sq, st, 0)] = True
            else:
                att_prologue_v(sq, st)
                kvflag[(sq, st, 1)] = True
            ws = WStream([sx, 7 + sx, sx + 2, 9 + sx, 17 + sx, 19 + sx], 2 * sx)
            for j in (sx, sx + 2):
                wq, rwq = ws.get()
                for c in range(2):
                    p, rp = PF()
                    proj_fm(wq, rwq, c * 128, 128, 640, p, rp)
                    rope(p, rp, 512, 128, qT[sx][:, c, :], [r_qT[sx][c]], sx)
                wg, rwg = ws.get()
                for t in range(4):
                    p, rp = PF()
                    proj_tm(wg, rwg, 256, 128 + t * 128, p, rp)
                    act(gr[sx][:], p[:, 0:256], AF.Tanh, [rp], [r_gr[sx]], scale=0.5)
                    stt(gs[sx][:, t, :], gr[sx][:], 1.0, p[:, 0:256], ALU.add, ALU.mult, [r_gr[sx], rp], [r_gs[sx][t]])
                while not (kvflag.get((sq, st, 0)) and kvflag.get((sq, st, 1))):
                    S.sclock[S.cur] = max(S.sclock.values()) + 1e-3
                    S.cur.pause()
                for qb in range(4):
                    gq = st * 4 + qb
                    kbs = [kb for kb in (-1, 0, 1) if 0 <= gq + kb < NT]
                    for kb in kbs:
                        kcol = (qb + 1 + kb) * 128
                        p, rp = PF()
                        first = True
                        if kb != 0:
                            mm(p[:, :], ident_b, negb4 if kb < 0 else negf4, True, False, [r_cb], [rp], False)
                            first = False
                        for r in range(4):
                            c, half = r // 2, r % 2
                            kk = kTlo if half == 0 else kThi
                            mm(p[:, r * 128:(r + 1) * 128], kk[:, j, kcol:kcol + 128], qT[sx][:, c, qb * 128:(qb + 1) * 128],
                               first, r == 3, [r_kT[j], r_qT[sx][c]], [rp], r == 3)
                            first = False
                        act(PTb[sx][:, kb + 1, :], p[:, :], AF.Exp, [rp], [r_PT[sx]], scale=0.125)
                    pv, rpv = PF()
                    first = True
                    for r in range(4):
                        for kb in kbs:
                            lastmm = (r == 3 and kb == kbs[-1])
                            mm(pv[:, r * 128:r * 128 + 65], PTb[sx][:, kb + 1, r * 128:(r + 1) * 128],
                               vaug[:, qb + 1 + kb, j, 0:65], first, lastmm, [r_PT[sx], r_v[qb + 1 + kb]], [rpv], lastmm)
                            first = False
                    pv3 = pv[:, :].rearrange("p (r d) -> p r d", r=4)
                    dn = sm[:, 8 + 4 * sx:12 + 4 * sx]
                    tt("dve", dn, pv3[:, :, 64], ptab[:, P_SINK + 4 * j:P_SINK + 4 * j + 4], ALU.add,
                       [rpv, r_ptab], [r_dn[sx]])
                    S.op("dve", lambda E, dn=dn: E.reciprocal(dn, dn), [r_dn[sx]], [r_dn[sx]])
                    tt("pool", gr[sx][:].rearrange("p (r d) -> p r d", r=4), gs[sx][:, qb, :].rearrange("p (r d) -> p r d", r=4),
                       bc(dn, [128, 4, 64], 2), ALU.mult, [r_gs[sx][qb], r_dn[sx]], [r_gr[sx]])
                    stt(attg[sx][:].rearrange("p (r d) -> p r d", r=4), pv3[:, :, 0:64], 0.5,
                        gr[sx][:].rearrange("p (r d) -> p r d", r=4), ALU.mult, ALU.mult, [rpv, r_gr[sx]], [r_attg[sx]])
                    S.hold += 1
                    pbt, rpbt = PB()
                    for c in range(2):
                        tp(pbt[:, c * 128:(c + 1) * 128], attg[sx][:, c * 128:(c + 1) * 128], [r_attg[sx]], [rpbt], c == 1)
                    for c in range(2):
                        if c == 1:
                            S.hold -= 1
                        cp("act", mixT[:, 2 * j + c, qb * 128:(qb + 1) * 128], pbt[:, c * 128:(c + 1) * 128],
                           [rpbt], [r_mix[2 * j + c][qb]])
            w, rw = ws.get()
            for cc in range(2):
                p, rp = PF()
                proj_fm(w, rw, cc * 128, 128, 640, p, rp)
                cp("act", qT[sx][:, cc, :], p[:, :], [rp], [r_qT[sx][cc]])
            w, rw = ws.get()
            for t in range(4):
                p, rp = PF()
                proj_tm(w, rw, 256, 128 + t * 128, p, rp)
                act(gr[sx][:], p[:, 0:256], AF.Tanh, [rp], [r_gr[sx]], scale=0.5)
                stt(gs[sx][:, t, :], gr[sx][:], 1.0, p[:, 0:256], ALU.add, ALU.mult, [r_gr[sx], rp], [r_gs[sx][t]])
            sc = float(1.0 / np.sqrt(128.0))
            for cc in range(2):
                hm = 2 * sx + cc
                for mb in range(2):
                    p, rp = PF()
                    m0 = sq * 256 + mb * 128
                    mm(p[:, :], mkT[:, hm, m0:m0 + 128], qT[sx][:, cc, :], True, True, [r_mkT, r_qT[sx][cc]], [rp], True)
                    act(PTb[sx][:, mb, :], p[:, :], AF.Exp, [rp], [r_PT[sx]], scale=sc)
                for tp2 in range(2):
                    pv, rpv = PF()
                    first = True
                    for tl in range(2):
                        t = tp2 * 2 + tl
                        for mb in range(2):
                            lastmm = (tl == 1 and mb == 1)
                            mm(pv[:, tl * 256:tl * 256 + 129], PTb[sx][:, mb, t * 128:(t + 1) * 128],
                               mvaug[:, sq * 2 + mb, hm, 0:129], first, lastmm, [r_PT[sx], r_mv], [rpv], lastmm)
                            first = False
                    pv3 = pv[:, :].rearrange("p (t d) -> p t d", t=2)
                    dn = sm[:, 24 + 2 * sx:26 + 2 * sx]
                    S.op("dve", lambda E, dn=dn, pv3=pv3: E.reciprocal(dn, pv3[:, :, 128]), [rpv], [r_dn[sx]])
                    for tl in range(2):
                        t = tp2 * 2 + tl
                        ts("pool", gr[sx][:, 0:128], gs[sx][:, t, cc * 128:(cc + 1) * 128], dn[:, tl:tl + 1], 0.5, ALU.mult, ALU.mult,
                           [r_gs[sx][t], r_dn[sx]], [r_gr[sx]])
                        tt("dve", attg[sx][:, 0:128], pv[:, tl * 256:tl * 256 + 128], gr[sx][:, 0:128], ALU.mult,
                           [rpv, r_gr[sx]], [r_attg[sx]])
                        S.hold += 1
                        pbt, rpbt = PB()
                        tp(pbt[:, 0:128], attg[sx][:, 0:128], [r_attg[sx]], [rpbt], True)
                        S.hold -= 1
                        cp("act", mixT[:, 12 + hm, t * 128:(t + 1) * 128], pbt[:, 0:128], [rpbt], [r_mix[12 + hm][t]])
            yield

        def ssd_dt(sq, st):
            p, rp = PF()
            for t in range(4):
                for kc in range(8):
                    mm(p[:, t * 16:(t + 1) * 16], hnT[:, kc, 128 + t * 128:256 + t * 128], wdt[:, kc, :],
                       t == 0 and kc == 0, t == 3 and kc == 7, [r_wdt] + hn_regs(128, 640), [rp], t == 3 and kc == 7)
            d3 = dts[:]
            tt("dve", d3, p[:, 0:64].rearrange("p (t h) -> p t h", t=4), bc(ptab[:, P_DTB:P_DTB + 16], [128, 4, 16], 1),
               ALU.add, [rp, r_ptab], [r_dts])
            act(d3, d3, AF.Exp, [r_dts], [r_dts])
            act(d3, d3, AF.Ln, [r_dts], [r_dts], bias=1.0)
            tt("dve", adt[:], d3, bc(ptab[:, P_ALOG:P_ALOG + 16], [128, 4, 16], 1), ALU.mult, [r_dts, r_ptab], [r_adt])
            p, rp = PF()
            first = True
            for t in range(4):
                for (col, mat, a0, n) in ((0, C_SGT, 0, 8), (8, C_SLE, 0, 8), (16, C_SLT, 8, 8), (24, C_SGE, 8, 8),
                                          (32, C_ONES, 0, 16)):
                    lastmm = (t == 3 and col == 32)
                    mm(p[:, t * 48 + col:t * 48 + col + n], cf[:, mat:mat + 128], adt[:, t, a0:a0 + n], first, lastmm,
                       [r_cf, r_adt], [rp], lastmm)
                    first = False
            act(exs[:], p[:, 0:192].rearrange("p (t c) -> p t c", t=4), AF.Exp, [rp], [r_exs])
            tt("dve", wfb[:, :, 0:8], dts[:, :, 0:8], exs[:, :, 0:8], ALU.mult, [r_dts, r_exs], [r_wfb])
            tt("dve", wfb[:, :, 8:16], dts[:, :, 8:16], exs[:, :, 16:24], ALU.mult, [r_dts, r_exs], [r_wfb])

        def ssd_z(sq, st, ws):
            for u in range(2):
                w, rw = ws.get()
                for t in range(4):
                    p, rp = PF()
                    proj_tm(w, rw, 256, 128 + t * 128, p, rp)
                    act(cth[0][:], p[:, 0:256], AF.Tanh, [rp], [r_cth[0]], scale=0.5)
                    stt(zs[:, t, u * 256:(u + 1) * 256], cth[0][:], 1.0, p[:, 0:256], ALU.add, ALU.mult, [r_cth[0], rp], [r_zs[t]])
            yield

        def ssd_conv(sq, st, us, ws, ia):
            for u in us:
                w, rw = ws.get()
                for cc in range(2):
                    ch = u * 2 + cc
                    for hf in range(2):
                        lo = 126 + hf * 256
                        p, rp = PF()
                        proj_fm(w, rw, cc * 128, lo, lo + 260, p, rp)
                        acc = cacc[ia]
                        act(acc[:], p[:, 0:256], AF.Identity, [rp, r_ptab], [r_cacc[ia]],
                            bias=ptab[:, P_CB + ch:P_CB + ch + 1], scale=ptab[:, P_CW + ch * 5:P_CW + ch * 5 + 1])
                        for k in range(1, 5):
                            stt(acc[:], p[:, k:k + 256], ptab[:, P_CW + ch * 5 + k:P_CW + ch * 5 + k + 1], acc[:],
                                ALU.mult, ALU.add, [rp, r_ptab, r_cacc[ia]], [r_cacc[ia]])
                        if ch < 4:
                            act(cth[ia][:], acc[:], AF.Tanh, [r_cacc[ia]], [r_cth[ia]])
                            stt(xsTt[ia][:], cth[ia][:], 1.0, acc[:], ALU.add, ALU.mult, [r_cth[ia], r_cacc[ia]], [r_xsTt[ia]])
                            S.hold += 1
                            pbt, rpbt = PB()
                            for tl in range(2):
                                tp(pbt[:, tl * 128:(tl + 1) * 128], xsTt[ia][:, tl * 128:(tl + 1) * 128], [r_xsTt[ia]],
                                   [rpbt], tl == 1)
                            for tl in range(2):
                                if tl == 1:
                                    S.hold -= 1
                                t = hf * 2 + tl
                                cp("act", xs_tm[:, t, ch * 128:(ch + 1) * 128], pbt[:, tl * 128:(tl + 1) * 128],
                                   [rpbt], [r_xs[t]])
                        else:
                            q = ch - 4
                            act(cth[ia][:], acc[:], AF.Tanh, [r_cacc[ia]], [r_cth[ia]])
                            stt(bcT[:, q, hf * 256:(hf + 1) * 256], cth[ia][:], 1.0, acc[:], ALU.add, ALU.mult,
                                [r_cth[ia], r_cacc[ia]], [r_bcT[q][hf * 2], r_bcT[q][hf * 2 + 1]])
                            if q < 2:
                                S.hold += 1
                                pbt, rpbt = PB()
                                for tl in range(2):
                                    t = hf * 2 + tl
                                    tp(pbt[:, tl * 128:(tl + 1) * 128], bcT[:, q, t * 128:(t + 1) * 128], [r_bcT[q][t]],
                                       [rpbt], tl == 1)
                                for tl in range(2):
                                    if tl == 1:
                                        S.hold -= 1
                                    t = hf * 2 + tl
                                    cp("act", bm_tm[:, t, q * 128:(q + 1) * 128], pbt[:, tl * 128:(tl + 1) * 128],
                                       [rpbt], [r_bm[t]])
                        yield

        def xscale(dst_i, t, col_ap, r_col):
            tt("pool", xsc[dst_i][:].rearrange("p (h d) -> p h d", h=8), xs_tm[:, t, :].rearrange("p (h d) -> p h d", h=8),
               bc(col_ap, [128, 8, 64], 2), ALU.mult, [r_xs[t], r_col], [r_xsc[dst_i]])

        def state_update(t, d, xdd_i):
            p, rp = PF()
            for g in range(2):
                mm(p[:, g * 256:(g + 1) * 256], bm_tm[:, t, g * 128:(g + 1) * 128], xsc[xdd_i][:, g * 256:(g + 1) * 256],
                   g == 0, g == 1, [r_bm[t], r_xsc[xdd_i]], [rp], g == 1)
            cd = exs[:, t, 32 + d * 8:40 + d * 8]
            S3 = Sst[d][:].rearrange("p (h d) -> p h d", h=8)
            tt("dve", S3, S3, bc(cd, [128, 8, 64], 2), ALU.mult, [r_Sst[d], r_exs], [r_Sst[d]])
            tt("dve", Sst[d][:], Sst[d][:], p[:, :], ALU.add, [r_Sst[d], rp], [r_Sst[d]])

        def pass1_chunk(sq, st, t):
            g = st * 4 + t
            i = nxt("sbf", 2)
            cp("act", Sbf[i][:], Sst[1][:], [r_Sst[1]], [r_Sbf[i]])
            S.dma("pool", dprev[g, :, :], Sbf[i][:], "pst%d" % i, reads=[r_Sbf[i]], writes=[r_dprev[g]])
            xscale(3, t, wfb[:, t, 8:16], r_wfb)
            state_update(t, 1, 3)

        def ssd_chunk(sq, st, t):
            g = st * 4 + t
            tc0 = t * 128
            ip = nxt("pbuf", 2)
            S.dma("sp", pbuf[ip][:], dprev[g, :, :], "pld%d" % ip, reads=[r_dprev[g]], writes=[r_pbuf[ip]])
            pcb, rpcb = PF()
            for gI in range(2):
                mm(pcb[:, gI * 128:(gI + 1) * 128], bcT[:, gI, tc0:tc0 + 128], bcT[:, 2 + gI, tc0:tc0 + 128], gI == 0, gI == 1,
                   [r_bcT[gI][t], r_bcT[2 + gI][t]], [rpcb], gI == 1)
            cp("act", cbs[:], pcb[:, 0:256], [rpcb], [r_cbs])
            for d in range(2):
                for gI in range(2):
                    ir = nxt("rb", 2)
                    U = cf[:, C_SLE:C_SLE + 128] if d == 0 else cf[:, C_SGE:C_SGE + 128]
                    LT = cf[:, C_SGT:C_SGT + 128] if d == 0 else cf[:, C_SLT:C_SLT + 128]
                    tt("pool", Rb[ir][:], bc(U, [128, 4, 128], 1),
                       bc(adt[:, t, d * 8 + gI * 4:d * 8 + gI * 4 + 4], [128, 4, 128], 2), ALU.mult, [r_cf, r_adt], [r_Rb[ir]])
                    p, rp = PF()
                    mm(p[:, :], ident_b, negf4 if d == 0 else negb4, True, False, [r_cb], [rp], False)
                    mm(p[:, :], LT, Rb[ir][:].rearrange("p h l -> p (h l)"), False, True, [r_cf, r_Rb[ir]], [rp], True)
                    act(Lt[ir][:].rearrange("p h l -> p (h l)"), p[:, :], AF.Exp, [rp], [r_Lt[ir]])
                    mi = d * 2 + gI
                    tt("dve", Mt[mi][:], Lt[ir][:], bc(cbs[:, gI * 128:(gI + 1) * 128], [128, 4, 128], 1), ALU.mult,
                       [r_Lt[ir], r_cbs], [r_Mt[mi]])
                    yield
            xscale(0, t, dts[:, t, 0:8], r_dts)
            xscale(1, t, dts[:, t, 8:16], r_dts)
            xscale(2, t, wfb[:, t, 0:8], r_wfb)
            xscale(4, t, ptab[:, P_DSK:P_DSK + 8], r_ptab)
            isb = (cnt["sbf"] - 1) % 2 if cnt["sbf"] > 0 else 0
            e_f = exs[:, t, 8:16]
            e_b = exs[:, t, 24:32]
            v3 = lambda ap: ap.rearrange("p (h d) -> p h d", h=8)
            for d in range(2):
                po, rpo = PF()
                prev_ap, r_prev = (Sbf[isb], r_Sbf[isb]) if d == 0 else (pbuf[ip], r_pbuf[ip])
                for gI in range(2):
                    mm(po[:, gI * 256:(gI + 1) * 256], bcT[:, 2 + gI, tc0:tc0 + 128], prev_ap[:, gI * 256:(gI + 1) * 256],
                       gI == 0, gI == 1, [r_bcT[2 + gI][t], r_prev], [rpo], gI == 1)
                if d == 0:
                    tt("dve", v3(y1[:]), v3(po[:, :]), bc(e_f, [128, 8, 64], 2), ALU.mult, [rpo, r_exs], [r_y1])
                else:
                    tt("dve", v3(y2[:]), v3(po[:, :]), bc(e_b, [128, 8, 64], 2), ALU.mult, [rpo, r_exs], [r_y2])
            py, rpy = PF()
            mm(py[:, :], ident_b, xsc[4][:], True, False, [r_cb, r_xsc[4]], [rpy], False)
            for h in range(8):
                gI, r = h // 4, h % 4
                for d in range(2):
                    lastmm = (h == 7 and d == 1)
                    mm(py[:, h * 64:(h + 1) * 64], Mt[d * 2 + gI][:, r, :], xsc[d][:, h * 64:(h + 1) * 64], False, lastmm,
                       [r_Mt[d * 2 + gI], r_xsc[d]], [rpy], lastmm)
            tt("dve", y1[:], y1[:], y2[:], ALU.add, [r_y1, r_y2], [r_y1])
            tt("dve", y2[:], py[:, :], y1[:], ALU.add, [rpy, r_y1], [r_y2])
            stt(y1[:], y2[:], 0.5, zs[:, t, :], ALU.mult, ALU.mult, [r_y2, r_zs[t]], [r_y1])
            yield
            ss = sm[:, 16:18]
            for gI in range(2):
                act(junk[:, 0:256], y1[:, gI * 256:(gI + 1) * 256], AF.Square, [r_y1], [r_junk, r_ssd_ss], accum=ss[:, gI:gI + 1])
            rstd_from_ss(ss, 2, 1.0 / 256, [r_ssd_ss])
            for gI in range(2):
                stt(yn[:, gI * 256:(gI + 1) * 256], y1[:, gI * 256:(gI + 1) * 256], ss[:, gI:gI + 1],
                    ptab[:, P_SNW + gI * 256:P_SNW + (gI + 1) * 256], ALU.mult, ALU.mult, [r_y1, r_ssd_ss, r_ptab], [r_yn])
            pbt, rpbt = PB()
            for c in range(4):
                tp(pbt[:, c * 128:(c + 1) * 128], yn[:, c * 128:(c + 1) * 128], [r_yn], [rpbt], c == 3)
            for c in range(4):
                cp("act", mixT[:, 8 + c, tc0:tc0 + 128], pbt[:, c * 128:(c + 1) * 128], [rpbt], [r_mix[8 + c][t]])
            yield
            state_update(t, 0, 2)
            i = nxt("sbf", 2)
            cp("act", Sbf[i][:], Sst[0][:], [r_Sst[0]], [r_Sbf[i]])

        def mem_attn(sq, st, ws):
            for u in range(2):
                w, rw = ws.get()
                for cc in range(2):
                    hm = u * 2 + cc
                    p, rp = PF()
                    proj_fm(w, rw, cc * 128, 128, 640, p, rp)
                    cp("act", qmT[:, hm, :], p[:, :], [rp], [r_qm[hm]])
                    yield
            for u in range(2):
                w, rw = ws.get()
                for t in range(4):
                    p, rp = PF()
                    proj_tm(w, rw, 256, 128 + t * 128, p, rp)
                    act(gms[:, t, u * 256:(u + 1) * 256], p[:, 0:256], AF.Silu, [rp], [r_gms[t]])
                    yield
            sc = 1.0 / np.sqrt(128.0)
            for hm in range(4):
                ip = nxt("ptm", 2)
                for mb in range(2):
                    p, rp = PF()
                    m0 = sq * 256 + mb * 128
                    mm(p[:, :], mkT[:, hm, m0:m0 + 128], qmT[:, hm, :], True, True, [r_mkT, r_qm[hm]], [rp], True)
                    act(PTm[ip][:, mb, :], p[:, :], AF.Exp, [rp], [r_PTm[ip]], scale=float(sc))
                yield
                for tp2 in range(2):
                    pv, rpv = PF()
                    first = True
                    for tl in range(2):
                        t = tp2 * 2 + tl
                        for mb in range(2):
                            lastmm = (tl == 1 and mb == 1)
                            mm(pv[:, tl * 256:tl * 256 + 129], PTm[ip][:, mb, t * 128:(t + 1) * 128],
                               mvaug[:, sq * 2 + mb, hm, 0:129], first, lastmm, [r_PTm[ip], r_mv], [rpv], lastmm)
                            first = False
                    pv3 = pv[:, :].rearrange("p (t d) -> p t d", t=2)
                    dn = sm[:, 24:26]
                    S.op("dve", lambda E, dn=dn, pv3=pv3: E.reciprocal(dn, pv3[:, :, 128]), [rpv], [r_mem_dn])
                    for tl in range(2):
                        t = tp2 * 2 + tl
                        ts("pool", grm[tl][:, 0:128], gms[:, t, hm * 128:(hm + 1) * 128], dn[:, tl:tl + 1], None, ALU.mult, None,
                           [r_gms[t], r_mem_dn], [r_grm[tl]])
                        tt("dve", attgm[tl][:, 0:128], pv[:, tl * 256:tl * 256 + 128], grm[tl][:, 0:128], ALU.mult, [rpv, r_grm[tl]], [r_attgm[tl]])
                        pbt, rpbt = PB()
                        tp(pbt[:, 0:128], attgm[tl][:, 0:128], [r_attgm[tl]], [rpbt], True)
                        cp("act", mixT[:, 12 + hm, t * 128:(t + 1) * 128], pbt[:, 0:128], [rpbt], [r_mix[12 + hm][t]])
                    yield

        out_toks = []

        r_oss = [r_out_ss, Reg()]

        def out_proj(sq, st, h=None):
            hh = 0 if h is None else h
            for t in (range(4) if h is None else (2 * h, 2 * h + 1)):
                g = st * 4 + t
                i = nxt("xr", 2) if h is None else h
                S.dma("sp", xres[i][:], dx[sq, g * 128:(g + 1) * 128, :], "xr%d" % i, writes=[r_xres[i]])
                for nb in range(2):
                    p, rp = PF()
                    for c in range(16):
                        mm(p[:, :], mixT[:, c, t * 128:(t + 1) * 128], wout[:, c, nb * 512:(nb + 1) * 512], c == 0, c == 15,
                           [r_mix[c][t], r_wout], [rp], c == 15)
                    tt("dve", xres[i][:, nb * 512:(nb + 1) * 512], xres[i][:, nb * 512:(nb + 1) * 512], p[:, :], ALU.add,
                       [r_xres[i], rp], [r_xres[i]])
                ss = sm[:, 32 + hh:33 + hh]
                jk, rjk = (junk, r_junk) if hh == 0 else (junk1, r_y1)
                act(jk[:], xres[i][:], AF.Square, [r_xres[i]], [rjk, r_oss[hh]], accum=ss)
                rstd_from_ss(ss, 1, 1.0 / D, [r_oss[hh]])
                stt(xres[i][:], xres[i][:], ss, ptab[:, P_NOW:P_NOW + 1024], ALU.mult, ALU.mult,
                    [r_xres[i], r_oss[hh], r_ptab], [r_xres[i]])
                tok = S.dma("pool", dout[sq, g * 128:(g + 1) * 128, :], xres[i][:], "ost%d" % i, reads=[r_xres[i]])
                out_toks.append(tok)

        class _Stop(Exception):
            pass

        def stage(n):
            if STOP is not None and n >= STOP:
                raise _Stop()

        try:
          stage(0)
          def run(g):
              for _ in g:
                  pass

          def corun(*streams):
              cos = []
              base = min(S.efree[e] for e in ("pe", "act", "dve", "pool"))
              for g, pool in streams:
                  c = Co(lambda g=g: run(g))
                  c.pool = pool
                  S.sclock[c] = base
                  cos.append(c)
              alive = list(cos)
              while alive:
                  c = min(alive, key=lambda c: S.sclock[c])
                  S.cur = c
                  c.step()
                  S.cur = None
                  if c.done:
                      alive.remove(c)

          def gen(fn, *a):
              fn(*a)
              yield

          def ssd_stream(sq, st):
              ws = mkws(("B", sq, st), [11, 12, 13, 14, 15, 16], 4)
              ssd_dt(sq, st)
              yield from ssd_z(sq, st, ws)
              yield from ssd_conv(sq, st, [0, 1, 2, 3], ws, 0)
              for t in range(4):
                  yield from ssd_chunk(sq, st, t)

          for sq in range(NSEQ):
            S.op("pool", lambda E: E.memset(Sst[1][:], 0.0), [], [r_Sst[1]])
            def upd(sq, st):
                for t in reversed(range(4)):
                    pass1_chunk(sq, st, t)

            hn_stage(sq, NST - 1, None, "halo")
            corun((gen(hn_stage, sq, NST - 1, None, 0), "H1"), (gen(hn_stage, sq, NST - 1, None, 1), "H2"))
            for st in reversed(range(NST)):
                stage(1)
                corun((ssd_conv(sq, st, [0], mkws(("X", sq, st), [13], 0), 0), "X"),
                      (ssd_conv(sq, st, [1], mkws(("Y", sq, st), [14], 2), 1), "Y"),
                      (ssd_conv(sq, st, [2], mkws(("Z", sq, st), [15], 4), 2), "Z"),
                      (gen(ssd_dt, sq, st), "D"))
                stage(2)
                if st > 0:
                    prefetch_ws(("X", sq, st - 1), [13], 0)
                    prefetch_ws(("Y", sq, st - 1), [14], 2)
                    prefetch_ws(("Z", sq, st - 1), [15], 4)
                    hn_stage(sq, st - 1, st, "halo")
                    corun((gen(upd, sq, st), "X"),
                          (gen(hn_stage, sq, st - 1, st, 0), "H1"), (gen(hn_stage, sq, st - 1, st, 1), "H2"))
                else:
                    upd(sq, st)
                stage(3)
            S.op("pool", lambda E: E.memset(Sst[0][:], 0.0), [], [r_Sst[0]])
            i = nxt("sbf", 2)
            S.op("pool", lambda E, i=i: E.memset(Sbf[i][:], 0.0), [], [r_Sbf[i]])
            for st in range(NST):
                if st == 0:
                    att_cossin(sq, st)
                corun((att_stream(sq, st, 0), "A1"), (att_stream(sq, st, 1), "A2"), (ssd_stream(sq, st), "B"))
                if st + 1 < NST:
                    prefetch_ws(("K", sq, st + 1), [4, 5], 0)
                    prefetch_ws(("V", sq, st + 1), [6], 2)
                    prefetch_ws(("B", sq, st + 1), [11, 12, 13, 14, 15, 16], 4)
                    att_cossin(sq, st + 1)
                    hn_stage(sq, st + 1, st, "halo")
                    corun((gen(out_proj, sq, st, 0), "O1"), (gen(out_proj, sq, st, 1), "O2"),
                          (gen(hn_stage, sq, st + 1, st, 0), "H1"), (gen(hn_stage, sq, st + 1, st, 1), "H2"))
                else:
                    out_proj(sq, st)
        except _Stop:
            pass
        for e in ("pe", "act", "dve", "pool"):
            if S.cnt[e] > 0:
                nm, h = S.esem[e]
                S.finish_waits("pool", [(nm, h, S.cnt[e])])
        for k, (nm, h, v) in S.dsem.items():
            S.finish_waits("pool", [(nm, h, v)])
        fin = {}
        for nm, h, v in out_toks:
            if nm not in fin or fin[nm][2] < v:
                fin[nm] = (nm, h, v)
        S.finish_waits("pool", list(fin.values()))

        block = es.enter_context(nc.Block())

        @block.tensor
        def _(E):
            for f in S.prog["pe"]:
                f(E)

        @block.scalar
        def _(E):
            for f in S.prog["act"]:
                f(E)

        @block.vector
        def _(E):
            for f in S.prog["dve"]:
                f(E)

        @block.gpsimd
        def _(E):
            for f in S.prog["pool"]:
                f(E)

        @block.sync
        def _(E):
            for f in S.prog["sp"]:
                f(E)
    return nc


def _tables(L):
    j = np.arange(128)[:, None]
    s = np.arange(128)[None, :]
    cf = np.concatenate([(j > s), (j <= s), (j < s), (j >= s), np.ones((128, 128), bool)], axis=1).astype(np.float32)
    ident = np.eye(128, dtype=np.float32)
    negf = np.where(j > s, NEG, 0.0).astype(np.float32)
    negb = np.where(j < s, NEG, 0.0).astype(np.float32)
    rp = np.zeros((128, 128), np.float32)
    for m in range(128):
        if (m % 64) < 32:
            rp[m + 32, m] = -1.0
        else:
            rp[m - 32, m] = 1.0
    cb = np.concatenate([ident, np.tile(negf, (1, 4)), np.tile(negb, (1, 4)), rp], axis=1).astype(np.float32)
    inv_freq = (1.0 / (np.float32(10000.0) ** (np.arange(0, 64, 2, dtype=np.float32) / np.float32(64.0)))).astype(np.float32)
    pos = np.arange(L, dtype=np.float32)
    ang = (pos[None, :] * inv_freq[np.arange(128) % 32][:, None]).astype(np.float32)
    cosT = np.zeros((128, L + 256), np.float32)
    sinT = np.zeros((128, L + 256), np.float32)
    cosT[:, 128:128 + L] = np.cos(ang)
    sinT[:, 128:128 + L] = np.sin(ang)
    return cf, cb, cosT, sinT


def _prep_shared(inp, L):
    w_in = np.asarray(inp["w_in"], np.float32)[0]
    cols = []
    for j in range(4):
        cols.append(np.arange(256 * j, 256 * j + 256))
    kb = 1024
    for u in range(2):
        a = kb + (2 * u) * 64 + np.arange(64)
        b = kb + (2 * u + 1) * 64 + np.arange(64)
        cols.append(np.concatenate([a, a, b, b]))
    cols.append(np.arange(1280, 1536))
    for j in range(4):
        cols.append(np.arange(1536 + 256 * j, 1536 + 256 * j + 256))
    for j in range(2):
        cols.append(np.arange(2560 + 256 * j, 2560 + 256 * j + 256))
    for j in range(4):
        cols.append(np.arange(3072 + 256 * j, 3072 + 256 * j + 256))
    for j in range(2):
        cols.append(np.arange(4112 + 256 * j, 4112 + 256 * j + 256))
    for j in range(2):
        cols.append(np.arange(4624 + 256 * j, 4624 + 256 * j + 256))
    assert len(cols) == NU
    w3 = w_in.reshape(8, 128, -1)
    win_u = np.stack([w3[:, :, c].transpose(1, 0, 2).reshape(128, 2048) for c in cols], axis=0)
    wdt = w3[:, :, 4096:4112].transpose(1, 0, 2).reshape(128, 128)
    wout_p = np.asarray(inp["w_out"], np.float32)[0].reshape(16, 128, 1024).transpose(1, 0, 2).reshape(128, 16 * 1024)
    wmem_p = np.asarray(inp["w_mem_kv"], np.float32)[0].reshape(8, 128, 1024).transpose(1, 0, 2).reshape(128, 8 * 1024)
    pt = np.zeros((128, PT), np.float32)
    pt[:, P_NIN:P_NIN + 8] = np.asarray(inp["norm_in_w"], np.float32)[0].reshape(8, 128).T
    pt[:, P_NMEM:P_NMEM + 8] = np.asarray(inp["norm_mem_w"], np.float32).reshape(8, 128).T
    cw = np.asarray(inp["conv_w"], np.float32)[0]
    pt[:, P_CW:P_CW + 40] = cw.reshape(5, 8, 128).transpose(2, 1, 0).reshape(128, 40)
    pt[:, P_CB:P_CB + 8] = np.asarray(inp["conv_b"], np.float32)[0].reshape(8, 128).T
    pt[:, P_DTB:P_DTB + 16] = np.asarray(inp["dt_bias"], np.float32)[0].reshape(1, 16)
    pt[:, P_ALOG:P_ALOG + 16] = np.asarray(inp["a_log"], np.float32)[0].reshape(1, 16)
    pt[:, P_DSK:P_DSK + 8] = np.asarray(inp["d_skip"], np.float32)[0].reshape(1, 8)
    pt[:, P_SNW:P_SNW + 512] = np.asarray(inp["ssd_norm_w"], np.float32)[0].reshape(1, 512)
    pt[:, P_NOW:P_NOW + 1024] = np.asarray(inp["norm_out_w"], np.float32).reshape(1, 1024)
    pt[:, P_SINK:P_SINK + 16] = np.asarray(inp["attn_sink"], np.float32)[0].reshape(1, 16)
    pt[:, P_EPS] = 1e-6
    pt[:, P_NEGH:P_NEGH + 2] = -0.5
    cf, cb, cosT, sinT = _tables(L)
    return {"win_u": np.ascontiguousarray(win_u), "wdt": np.ascontiguousarray(wdt),
            "wout_p": np.ascontiguousarray(wout_p), "wmem_p": np.ascontiguousarray(wmem_p), "ptab": pt,
            "cf32": cf, "cb16": cb, "cosT": cosT, "sinT": sinT}


def run(inp, n_cores):
    x = np.asarray(inp["x"], np.float32)
    mem = np.asarray(inp["mem"], np.float32)
    B, L, _ = x.shape
    assert B % n_cores == 0
    nseq = B // n_cores
    shared = _prep_shared(inp, L)
    nc = build_nc(L, nseq)
    in_maps = []
    for i in range(n_cores):
        m = dict(shared)
        m["x"] = np.ascontiguousarray(x[i * nseq:(i + 1) * nseq])
        m["mem"] = np.ascontiguousarray(mem[i * nseq:(i + 1) * nseq])
        in_maps.append(m)
    res = run_bass_kernel_spmd(nc, in_maps, core_ids=list(range(n_cores)))
    return np.concatenate([np.asarray(r["out"], np.float32) for r in res.results], axis=0)


def kernel(**inputs):
    return run(inputs, 8)
```

```python
import threading
import numpy as np
from contextlib import ExitStack
import concourse.bass as bass
import concourse.mybir as mybir
from concourse.bass_utils import run_bass_kernel_spmd

F32 = mybir.dt.float32
BF16 = mybir.dt.bfloat16
AF = mybir.ActivationFunctionType
ALU = mybir.AluOpType

D = 1024
NU = 21
SAME_SYNC = True
EPOCH = 30000
NEG = -30000.0
STOP = None
DMA_SCRATCH = 4096
CO_A, CO_B = 1, 1

P_NIN, P_NMEM, P_CW, P_CB, P_DTB, P_ALOG, P_DSK, P_SNW, P_NOW, P_SINK, P_EPS = (
    0, 8, 16, 56, 64, 80, 96, 104, 616, 1640, 1656)
P_NEGH = 1657
PT = 1659
C_SGT, C_SLE, C_SLT, C_SGE, C_ONES = 0, 128, 256, 384, 512
B_ID, B_NF, B_NB, B_RP = 0, 128, 640, 1152


class Reg:
    __slots__ = ("w", "r", "name", "tw", "tr")

    def __init__(self, name=""):
        self.w = None
        self.r = []
        self.name = name
        self.tw = 0.0
        self.tr = 0.0


class Co:
    def __init__(self, fn):
        self.fn = fn
        self.go = threading.Semaphore(0)
        self.back = threading.Semaphore(0)
        self.done = False
        self.exc = None
        self.th = threading.Thread(target=self._run, daemon=True)
        self.th.start()

    def _run(self):
        self.go.acquire()
        try:
            self.fn()
        except BaseException as e:
            self.exc = e
        self.done = True
        self.back.release()

    def step(self):
        self.go.release()
        self.back.acquire()
        if self.exc is not None:
            raise self.exc

    def pause(self):
        self.back.release()
        self.go.acquire()


class Sched:
    def __init__(self, nc, es):
        self.nc = nc
        self.es = es
        self.names = ["pe", "act", "dve", "pool", "sp"]
        self.prog = {k: [] for k in self.names}
        self.cnt = {k: 0 for k in self.names}
        self.epoch = {k: 0 for k in self.names}
        self.esem = {}
        self.seen = {k: {} for k in self.names}
        self.pend = {k: ([], []) for k in self.names}
        self.dsem = {}
        self.nsem = 0
        self.efree = {k: 0.0 for k in self.names}
        self.cur = None
        self.hold = 0
        self.sclock = {}
        for k in self.names:
            self._newepoch(k)

    def _vt(self, e, reads, writes, cost):
        t = self.efree[e]
        for r in reads:
            t = max(t, getattr(r, "tw", 0.0) + 0.15)
        for w in writes:
            t = max(t, getattr(w, "tw", 0.0) + 0.15, getattr(w, "tr", 0.0) + 0.15)
        fin = t + cost
        for r in reads:
            r.tr = max(getattr(r, "tr", 0.0), fin)
        for w in writes:
            w.tw = fin
            w.tr = 0.0
        return t, fin

    def _sem(self, name):
        self.nsem += 1
        return self.es.enter_context(self.nc.semaphore(name))

    def _newepoch(self, e):
        self.epoch[e] += 1
        nm = "%s#%d" % (e, self.epoch[e])
        self.esem[e] = (nm, self._sem("s_%s_%d" % (e, self.epoch[e])))
        self.cnt[e] = 0

    def _waits(self, e, reads, writes):
        waits = {}

        def need(t):
            if t is None:
                return
            nm, h, v = t
            if nm in waits:
                if v > waits[nm][1]:
                    waits[nm] = (h, v)
            else:
                waits[nm] = (h, v)

        for r in reads:
            need(r.w)
        for w in writes:
            need(w.w)
            for t in w.r:
                need(t)
        for nm, (h, v) in waits.items():
            if self.seen[e].get(nm, 0) >= v:
                continue
            if nm.split("#")[0] == e:
                if e == "pe" or not SAME_SYNC:
                    continue
            self.seen[e][nm] = v
            self.prog[e].append(lambda E, h=h, v=v: E.wait_ge(h, v))

    def op(self, e, fn, reads=(), writes=(), inc=True, cost=0.3):
        for k in self.names:
            if k != e:
                assert not self.pend[k][0] and not self.pend[k][1], "pending ops on %s" % k
        self._waits(e, reads, writes)
        t0, fin = self._vt(e, reads, writes, cost)
        self.efree[e] = fin
        if self.cur is not None:
            self.sclock[self.cur] = t0
        if inc:
            self.cnt[e] += 1
            v = self.cnt[e]
            nm, h = self.esem[e]
            self.prog[e].append(lambda E, fn=fn, h=h: fn(E).then_inc(h, 1))
            tok = (nm, h, v)
            pr, pw = self.pend[e]
            for r in list(reads) + pr:
                r.r.append(tok)
            for w in list(writes) + pw:
                w.w = tok
                w.r = []
            self.pend[e] = ([], [])
            if v >= EPOCH:
                self._newepoch(e)
            if self.cur is not None and self.hold == 0:
                self.cur.pause()
        else:
            self.prog[e].append(lambda E, fn=fn: fn(E))
            self.pend[e][0].extend(reads)
            self.pend[e][1].extend(writes)

    def dma(self, q, out, in_, key, reads=(), writes=(), cost=4.5):
        for k in self.names:
            assert not self.pend[k][0] and not self.pend[k][1], "pending ops on %s" % k
        self._waits(q, reads, writes)
        t0, fin = self._vt(q, reads, writes, cost)
        self.efree[q] = t0 + (1.0 if q == "pool" else 0.1)
        if key not in self.dsem:
            self.dsem[key] = ["d:" + key, self._sem("d_" + key), 0]
        ds = self.dsem[key]
        ds[2] += 16
        nm, h, v = ds
        self.prog[q].append(lambda E, h=h, out=out, in_=in_: E.dma_start(out=out, in_=in_).then_inc(h, 16))
        tok = (nm, h, v)
        for r in reads:
            r.r.append(tok)
        for w in writes:
            w.w = tok
            w.r = []
        return tok

    def finish_waits(self, e, toks):
        for nm, h, v in toks:
            self.prog[e].append(lambda E, h=h, v=v: E.wait_ge(h, v))


def build_nc(L, NSEQ, dbg=False):
    NT = L // 128
    NST = L // 512
    nc = bass.Bass("TRN2", target_bir_lowering=False, dynamic_dma_scratch_size=DMA_SCRATCH)
    dx = nc.dram_tensor("x", [NSEQ, L, D], F32, kind="ExternalInput").ap()
    dmem = nc.dram_tensor("mem", [NSEQ, 256, D], F32, kind="ExternalInput").ap()
    dwin = nc.dram_tensor("win_u", [NU, 128, 2048], F32, kind="ExternalInput").ap()
    dwdt = nc.dram_tensor("wdt", [128, 8 * 16], F32, kind="ExternalInput").ap()
    dwout = nc.dram_tensor("wout_p", [128, 16 * 1024], F32, kind="ExternalInput").ap()
    dwmem = nc.dram_tensor("wmem_p", [128, 8 * 1024], F32, kind="ExternalInput").ap()
    dptab = nc.dram_tensor("ptab", [128, PT], F32, kind="ExternalInput").ap()
    dcf = nc.dram_tensor("cf32", [128, 640], F32, kind="ExternalInput").ap()
    dcb = nc.dram_tensor("cb16", [128, 1280], F32, kind="ExternalInput").ap()
    dcos = nc.dram_tensor("cosT", [128, L + 256], F32, kind="ExternalInput").ap()
    dsin = nc.dram_tensor("sinT", [128, L + 256], F32, kind="ExternalInput").ap()
    dout = nc.dram_tensor("out", [NSEQ, L, D], F32, kind="ExternalOutput").ap()
    dwbf = nc.dram_tensor("wbf", [NU, 128, 2048], BF16).ap()
    dprev = nc.dram_tensor("prevb", [NT, 128, 512], BF16).ap()

    es = ExitStack()
    with es:
        S = Sched(nc, es)

        def sb(name, shape, dt):
            return es.enter_context(nc.sbuf_tensor("s_" + name, shape, dt))

        def psum(name, shape, dt):
            return es.enter_context(nc.psum_tensor("p_" + name, shape, dt))

        ptab = sb("ptab", [128, PT], F32); r_ptab = Reg()
        cf = sb("cf", [128, 640], F32); r_cf = Reg()
        cb = sb("cb", [128, 1280], BF16); r_cb = Reg()
        wout = sb("wout", [128, 16, 1024], BF16); r_wout = Reg()
        wdt = sb("wdtb", [128, 8, 16], BF16); r_wdt = Reg()
        NW = 6
        wring = [sb("wr%d" % i, [128, 8, 256], BF16) for i in range(NW)]
        r_wring = [Reg() for _ in range(NW)]
        mkT = sb("mkT", [128, 4, NSEQ * 256], BF16); r_mkT = Reg()
        mvaug = sb("mvaug", [128, NSEQ * 2, 4, 130], BF16); r_mv = Reg()
        xin = [sb("xin%d" % i, [128, D], F32) for i in range(2)]; r_xin = [Reg() for _ in range(2)]
        xres = [sb("xres%d" % i, [128, D], F32) for i in range(2)]; r_xres = [Reg() for _ in range(2)]
        hnT = sb("hnT", [128, 8, 768], BF16); r_hn = [Reg() for _ in range(6)]
        cosb = sb("cosb", [128, 768], F32); sinb = sb("sinb", [128, 768], F32); r_cs = Reg()
        qT = [sb("qT%d" % i, [128, 2, 512], BF16) for i in range(2)]; r_qT = [[Reg(), Reg()] for _ in range(2)]
        kTlo = sb("kTlo", [128, 4, 768], BF16); kThi = sb("kThi", [128, 4, 768], BF16); r_kT = [Reg() for _ in range(4)]
        vaug = sb("vaug", [128, 6, 4, 66], BF16); r_v = [Reg() for _ in range(6)]
        rp0 = sb("rp0", [128, 512], BF16); r_rp0 = Reg()
        rp1 = sb("rp1", [128, 512], F32); r_rp1 = Reg()
        rp2 = sb("rp2", [128, 512], F32); r_rp2 = Reg()
        rq0 = sb("rq0", [128, 512], BF16); r_rq0 = Reg()
        rq1 = sb("rq1", [128, 512], F32); r_rq1 = Reg()
        rq2 = sb("rq2", [128, 512], F32); r_rq2 = Reg()
        xsb = [rp1[:].bitcast(BF16), rp2[:].bitcast(BF16)]; r_xsb = [r_rp1, r_rp2]
        PTb = [sb("PT%d" % i, [128, 3, 512], BF16) for i in range(2)]; r_PT = [Reg(), Reg()]
        gs = [sb("gs%d" % i, [128, 4, 256], BF16) for i in range(2)]; r_gs = [[Reg() for _ in range(4)] for _ in range(2)]
        r_dn = [Reg(), Reg()]; r_ssd_ss = Reg(); r_mem_dn = Reg(); r_out_ss = Reg()
        sm = sb("sm", [128, 64], F32); r_sm = Reg()
        gr = [sb("gr%d" % i, [128, 256], F32) for i in range(2)]; r_gr = [Reg(), Reg()]
        attg = [sb("attg%d" % i, [128, 256], BF16) for i in range(2)]; r_attg = [Reg(), Reg()]
        grm = gr; r_grm = r_gr; attgm = attg; r_attgm = r_attg
        mixT = sb("mixT", [128, 16, 512], BF16); r_mix = [[Reg() for _ in range(4)] for _ in range(16)]
        dts = sb("dts", [128, 4, 16], F32); r_dts = Reg()
        adt = sb("adt", [128, 4, 16], F32); r_adt = Reg()
        exs = sb("exs", [128, 4, 48], F32); r_exs = Reg()
        wfb = sb("wfb", [128, 4, 16], F32); r_wfb = Reg()
        zs = sb("zs", [128, 4, 512], BF16); r_zs = [Reg() for _ in range(4)]
        cacc = [sb("cacc%d" % i, [128, 256], F32) for i in range(3)]; r_cacc = [Reg() for _ in range(3)]
        cth = [sb("cth%d" % i, [128, 256], F32) for i in range(3)]; r_cth = [Reg() for _ in range(3)]
        cbs = cacc[0]; r_cbs = r_cacc[0]
        xsTt = [sb("xsTt%d" % i, [128, 256], BF16) for i in range(3)]; r_xsTt = [Reg() for _ in range(3)]
        bcT = sb("bcT", [128, 4, 512], BF16); r_bcT = [[Reg() for _ in range(4)] for _ in range(4)]
        xs_tm = sb("xs_tm", [128, 4, 512], BF16); r_xs = [Reg() for _ in range(4)]
        bm_tm = sb("bm_tm", [128, 4, 256], BF16); r_bm = [Reg() for _ in range(4)]
        Rb = [sb("Rb%d" % i, [128, 4, 128], F32) for i in range(2)]; r_Rb = [Reg(), Reg()]
        Lt = [sb("Lt%d" % i, [128, 4, 128], BF16) for i in range(2)]; r_Lt = [Reg(), Reg()]
        Mt = [sb("Mt%d" % i, [128, 4, 128], BF16) for i in range(4)]; r_Mt = [Reg() for _ in range(4)]
        xsc = [sb("xsc%d" % i, [128, 512], BF16) for i in range(5)]; r_xsc = [Reg() for _ in range(5)]
        Sst = [sb("Sst%d" % i, [128, 512], F32) for i in range(2)]; r_Sst = [Reg(), Reg()]
        Sbf = [sb("Sbf%d" % i, [128, 512], BF16) for i in range(2)]; r_Sbf = [Reg(), Reg()]
        pbuf = [sb("pbuf%d" % i, [128, 512], BF16) for i in range(2)]; r_pbuf = [Reg(), Reg()]
        y1 = sb("y1", [128, 512], F32); r_y1 = Reg()
        y2 = sb("y2", [128, 512], F32); r_y2 = Reg()
        junk = y2[:].bitcast(BF16); r_junk = r_y2
        yn = sb("yn", [128, 512], BF16); r_yn = Reg()
        qmT = zs; r_qm = r_zs
        gms = xs_tm; r_gms = r_xs
        PTm = PTb; r_PTm = r_PT
        wmem = mixT[:].rearrange("p (a b) c -> p a (b c)", a=8); r_wmem = Reg()
        memT = hnT[:, :, 0:NSEQ * 256]; r_memT = Reg()
        r_dwbf = [Reg() for _ in range(NU)]
        r_dprev = [Reg() for _ in range(NT)]

        pf = [psum("pf%d" % i, [128, 512], F32) for i in range(6)]; r_pf = [Reg() for _ in range(6)]
        pb = [psum("pb%d" % i, [128, 1024], BF16) for i in range(2)]; r_pb = [Reg(), Reg()]
        cnt = {"pf": 0, "pb": 0, "x": 0, "xr": 0, "xs": 0, "pt": 0, "ca": 0, "rb": 0, "ptm": 0, "w": 0, "pbuf": 0, "sbf": 0}

        def nxt(k, n):
            i = cnt[k] % n
            cnt[k] += 1
            return i

        pools = {"A1": ([0, 1], [0]), "A2": ([2, 3], [0]), "B": ([4, 5], [1]),
                 "X": ([0, 1], [0]), "Y": ([2, 3], [1]), "Z": ([4], [0]), "D": ([5], [1]),
                 "O1": ([0, 1, 2], [0]), "O2": ([3, 4, 5], [1]), "H1": ([0], [0]), "H2": ([3], [1]), None: ([0, 1, 2, 3, 4, 5], [0, 1])}
        pcnt = {}

        def PF():
            k = getattr(S.cur, "pool", None)
            lst = pools[k][0]
            n = pcnt.get(("f", k), 0)
            pcnt[("f", k)] = n + 1
            i = lst[n % len(lst)]
            return pf[i], r_pf[i]

        def PB():
            k = getattr(S.cur, "pool", None)
            lst = pools[k][1]
            n = pcnt.get(("b", k), 0)
            pcnt[("b", k)] = n + 1
            i = lst[n % len(lst)]
            return pb[i], r_pb[i]

        def fsz(ap):
            return ap.free_size()

        def mm(out, lhsT, rhs, start, stop, reads, writes, inc):
            c = max(64, fsz(rhs)) / 2000.0 * (4.0 if rhs.dtype == F32 else 1.0)
            S.op("pe", lambda E: E.matmul(out, lhsT, rhs, start=start, stop=stop), reads, writes, inc, cost=c)

        def tp(out, in_, reads, writes, inc):
            S.op("pe", lambda E: E.transpose(out, in_, cb[:, B_ID:B_ID + 128]), list(reads) + [r_cb], writes, inc, cost=0.11)

        ECOST = {"act": (0.2, 0.00085), "dve": (0.12, 0.0011), "pool": (0.15, 0.0022)}

        def ecost(e, out):
            a, b = ECOST[e]
            return a + b * fsz(out)

        def act(out, in_, func, reads, writes, bias=None, scale=None, accum=None):
            kw = {}
            if bias is not None:
                kw["bias"] = bias
            if scale is not None:
                kw["scale"] = scale
            if accum is not None:
                kw["accum_out"] = accum
            S.op("act", lambda E: E.activation(out, in_, func, **kw), reads, writes,
                 cost=ecost("act", in_) + (0.1 if accum is not None else 0.0))

        def tt(e, out, in0, in1, op, reads, writes):
            S.op(e, lambda E: E.tensor_tensor(out, in0, in1, op), reads, writes, cost=ecost(e, out))

        def ts(e, out, in0, s1, s2, op0, op1, reads, writes):
            if op1 is None:
                S.op(e, lambda E: E.tensor_scalar(out, in0, s1, None, op0), reads, writes, cost=ecost(e, out))
            else:
                S.op(e, lambda E: E.tensor_scalar(out, in0, s1, s2, op0, op1), reads, writes, cost=ecost(e, out))

        def stt(out, in0, sc, in1, op0, op1, reads, writes):
            S.op("dve", lambda E: E.scalar_tensor_tensor(out, in0, sc, in1, op0, op1), reads, writes, cost=ecost("dve", out))

        def cp(e, out, in_, reads, writes):
            if e == "act":
                S.op("act", lambda E: E.copy(out, in_), reads, writes, cost=ecost("act", out))
            else:
                S.op(e, lambda E: E.tensor_copy(out, in_), reads, writes, cost=ecost(e, out))

        def bc(ap, shape, axis):
            return ap.unsqueeze(axis).to_broadcast(shape)

        S.dma("sp", ptab[:], dptab[:, :], "ptab", writes=[r_ptab])
        S.dma("sp", cf[:], dcf[:, :], "cf", writes=[r_cf])
        S.dma("pool", cb[:], dcb[:, :], "cb", writes=[r_cb])
        S.dma("pool", wdt[:].rearrange("p a b -> p (a b)"), dwdt[:, :], "wdt", writes=[r_wdt])
        for kc in range(8):
            S.dma("pool", wmem[:, kc, :], dwmem[:, kc * 1024:(kc + 1) * 1024], "wmem", writes=[r_wmem])
        A_ap = ptab[:, P_ALOG:P_ALOG + 16]
        act(A_ap, A_ap, AF.Exp, [r_ptab], [r_ptab])
        ts("dve", A_ap, A_ap, -1.0, None, ALU.mult, None, [r_ptab], [r_ptab])
        act(ptab[:, P_SINK:P_SINK + 16], ptab[:, P_SINK:P_SINK + 16], AF.Exp, [r_ptab], [r_ptab])
        eps_ap = ptab[:, P_EPS:P_EPS + 1]
        ts("dve", ptab[:, P_CW:P_CW + 48], ptab[:, P_CW:P_CW + 48], 0.5, None, ALU.mult, None, [r_ptab], [r_ptab])
        S.op("pool", lambda E: E.memset(vaug[:], 1.0), [], r_v)
        S.op("pool", lambda E: E.memset(kTlo[:], 0.0), [], r_kT)
        S.op("pool", lambda E: E.memset(kThi[:], 0.0), [], r_kT)
        S.op("pool", lambda E: E.memset(mvaug[:], 1.0), [], [r_mv])

        def rstd_from_ss(ss_ap, n, scale, reads_writes):
            ts("dve", ss_ap, ss_ap, float(scale), 1e-6, ALU.mult, ALU.add, reads_writes, reads_writes)
            tt("pool", ss_ap, ss_ap, ptab[:, P_NEGH:P_NEGH + n], ALU.pow, reads_writes + [r_ptab], reads_writes)

        r_hnss = [r_sm, Reg()]
        junk1 = y1[:].bitcast(BF16)

        def norm_transpose(src_ap, xslot, r_x, wcol, dst_fn, r_dst, h=None):
            hh = 0 if h is None else h
            ss = sm[:, hh:hh + 1]
            jk, rjk = (junk, r_junk) if hh == 0 else (junk1, r_y1)
            act(jk[:], src_ap, AF.Square, [r_x], [rjk, r_hnss[hh]], accum=ss)
            rstd_from_ss(ss, 1, 1.0 / D, [r_hnss[hh]])
            i = nxt("xs", 2) if h is None else h
            act(xsb[i][:], src_ap, AF.Identity, [r_x, r_hnss[hh]], [r_xsb[i]], scale=ss)
            pbt, r_pbt = PB()
            for c in range(8):
                tp(pbt[:, c * 128:(c + 1) * 128], xsb[i][:, c * 128:(c + 1) * 128], [r_xsb[i]], [r_pbt], c == 7)
            tt("dve", dst_fn(), pbt[:].rearrange("p (c t) -> p c t", c=8),
               bc(ptab[:, wcol:wcol + 8], [128, 8, 128], 2), ALU.mult, [r_pbt, r_ptab], r_dst)

        for sq in range(NSEQ):
            for mt in range(2):
                i = nxt("x", 2)
                S.dma("sp", xin[i][:], dmem[sq, mt * 128:(mt + 1) * 128, :], "xin%d" % i, writes=[r_xin[i]])
                c0 = sq * 256 + mt * 128
                norm_transpose(xin[i][:], i, r_xin[i], P_NMEM, lambda c0=c0: memT[:, :, c0:c0 + 128], [r_memT])
        for sq in range(NSEQ):
            for hm in range(4):
                p, rp = PF()
                for kc in range(8):
                    mm(p[:, 0:256], wmem[:, kc, hm * 128:(hm + 1) * 128], memT[:, kc, sq * 256:(sq + 1) * 256],
                       kc == 0, kc == 7, [r_wmem, r_memT], [rp], kc == 7)
                cp("act", mkT[:, hm, sq * 256:(sq + 1) * 256], p[:, 0:256], [rp], [r_mkT])
            for mt in range(2):
                p, rp = PF()
                for kc in range(8):
                    mm(p[:, :], memT[:, kc, sq * 256 + mt * 128: sq * 256 + (mt + 1) * 128], wmem[:, kc, 512:1024],
                       kc == 0, kc == 7, [r_wmem, r_memT], [rp], kc == 7)
                cp("dve", mvaug[:, sq * 2 + mt, :, 0:128], p[:, :].rearrange("p (h d) -> p h d", h=4), [rp], [r_mv])
        for rr in r_hn:
            rr.w = r_memT.w; rr.r = list(r_memT.r)
        for c in range(16):
            for t in range(4):
                r_mix[c][t].w = r_wmem.w; r_mix[c][t].r = list(r_wmem.r)
        last = None
        for u in range(NU):
            i = u % NW
            S.dma("pool", wring[i][:].rearrange("p a b -> p (a b)"), dwin[u, :, :], "wst%d" % i, writes=[r_wring[i]])
            last = S.dma("pool", dwbf[u, :, :], wring[i][:].rearrange("p a b -> p (a b)"), "wsv%d" % i,
                         reads=[r_wring[i]], writes=[r_dwbf[u]])
        for kc in range(16):
            S.dma("pool", wout[:, kc, :], dwout[:, kc * 1024:(kc + 1) * 1024], "wout", writes=[r_wout])
        r_wout.w = (S.dsem["wout"][0], S.dsem["wout"][1], S.dsem["wout"][2])

        def wload(u, i):
            S.dma("sp", wring[i][:].rearrange("p a b -> p (a b)"), dwbf[u, :, :], "wld%d" % i,
                  reads=[r_dwbf[u]], writes=[r_wring[i]])
            return wring[i], r_wring[i]

        class WStream:
            def __init__(self, units, base):
                self.units = list(units)
                self.base = base
                self.k = 0
                self.loaded = []

            def prefetch(self):
                while len(self.loaded) < min(self.k + 2, len(self.units)):
                    n = len(self.loaded)
                    self.loaded.append(wload(self.units[n], self.base + n % 2))

            def get(self):
                self.prefetch()
                r = self.loaded[self.k]
                self.k += 1
                return r

        wpre = {}

        def mkws(key, units, base):
            if key in wpre:
                return wpre.pop(key)
            return WStream(units, base)

        def prefetch_ws(key, units, base):
            w = WStream(units, base)
            w.prefetch()
            wpre[key] = w

        ident_b = cb[:, B_ID:B_ID + 128]
        negf4 = cb[:, B_NF:B_NF + 512]
        negb4 = cb[:, B_NB:B_NB + 512]
        rperm = cb[:, B_RP:B_RP + 128]

        def hn_stage(sq, st, prev_st=None, h=None):
            reuse = {}
            if prev_st is not None and prev_st == st - 1:
                reuse = {0: 4, 1: 5}
            elif prev_st is not None and prev_st == st + 1:
                reuse = {4: 0, 5: 1}
            if h is None or h == "halo":
                for j in sorted(reuse, reverse=(4 in reuse)):
                    sj = reuse[j]
                    cp("dve", hnT[:, :, j * 128:(j + 1) * 128], hnT[:, :, sj * 128:(sj + 1) * 128], [r_hn[sj]], [r_hn[j]])
            if h == "halo":
                return
            todo = [j for j in range(6) if j not in reuse]
            for n, j in enumerate(todo):
                if h is not None and n % 2 != h:
                    continue
                g = st * 4 - 1 + j
                dst = hnT[:, :, j * 128:(j + 1) * 128]
                if g < 0 or g >= NT:
                    S.op("pool", lambda E, dst=dst: E.memset(dst, 0.0), [], [r_hn[j]])
                    continue
                i = nxt("x", 2) if h is None else h
                S.dma("sp", xin[i][:], dx[sq, g * 128:(g + 1) * 128, :], "xin%d" % i, writes=[r_xin[i]])
                norm_transpose(xin[i][:], i, r_xin[i], P_NIN, lambda dst=dst: dst, [r_hn[j]], h)

        def hn_regs(lo, hi):
            return [r_hn[j] for j in range(lo // 128, (hi - 1) // 128 + 1)]

        def proj_fm(w, rw, col0, lo, hi, pbank, rp, pcol0=0):
            for kc in range(8):
                mm(pbank[:, pcol0:pcol0 + hi - lo], w[:, kc, col0:col0 + 128], hnT[:, kc, lo:hi],
                   kc == 0, kc == 7, [rw] + hn_regs(lo, hi), [rp], kc == 7)

        def proj_tm(w, rw, ncol, tcol, pbank, rp, wcol0=0, pcol0=0):
            for kc in range(8):
                mm(pbank[:, pcol0:pcol0 + ncol], hnT[:, kc, tcol:tcol + 128], w[:, kc, wcol0:wcol0 + ncol],
                   kc == 0, kc == 7, [rw] + hn_regs(tcol, tcol + 128), [rp], kc == 7)

        def rope(p, rp, n, csl, dst, r_dst, sx=0):
            a0, ra0 = (rp0, r_rp0) if sx == 0 else (rq0, r_rq0)
            a1, ra1 = (rp1, r_rp1) if sx == 0 else (rq1, r_rq1)
            a2, ra2 = (rp2, r_rp2) if sx == 0 else (rq2, r_rq2)
            cp("act", a0[:, 0:n], p[:, 0:n], [rp], [ra0])
            p2, rp2_ = PF()
            mm(p2[:, 0:n], rperm, a0[:, 0:n], True, True, [r_cb, ra0], [rp2_], True)
            tt("dve", a1[:, 0:n], p2[:, 0:n], sinb[:, csl:csl + n], ALU.mult, [rp2_, r_cs], [ra1])
            tt("dve", a2[:, 0:n], p[:, 0:n], cosb[:, csl:csl + n], ALU.mult, [rp, r_cs], [ra2])
            tt("pool", dst, a1[:, 0:n], a2[:, 0:n], ALU.add, [ra1, ra2], r_dst)

        def rope_k(p, rp, n, csl, j, lo, hi):
            cp("act", rp0[:, 0:n], p[:, 0:n], [rp], [r_rp0])
            p2, rp2_ = PF()
            mm(p2[:, 0:n], rperm, rp0[:, 0:n], True, True, [r_cb, r_rp0], [rp2_], True)
            tt("dve", rp1[:, 0:n], p2[:, 0:n], sinb[:, csl:csl + n], ALU.mult, [rp2_, r_cs], [r_rp1])
            tt("dve", rp2[:, 0:n], p[:, 0:n], cosb[:, csl:csl + n], ALU.mult, [rp, r_cs], [r_rp2])
            tt("pool", kTlo[0:64, j, lo:hi], rp1[0:64, 0:n], rp2[0:64, 0:n], ALU.add, [r_rp1, r_rp2], [r_kT[j]])
            tt("pool", kThi[64:128, j, lo:hi], rp1[64:128, 0:n], rp2[64:128, 0:n], ALU.add, [r_rp1, r_rp2], [r_kT[j]])

        def att_cossin(sq, st):
            S0 = st * 512
            S.dma("sp", cosb[:], dcos[:, S0:S0 + 768], "cos", writes=[r_cs])
            S.dma("sp", sinb[:], dsin[:, S0:S0 + 768], "sin", writes=[r_cs])

        def att_prologue_k(sq, st):
            ws = mkws(("K", sq, st), [4, 5], 0)
            for u in range(2):
                w, rw = ws.get()
                for jj in range(2):
                    j = u * 2 + jj
                    for (lo, hi) in ((0, 512), (512, 768)):
                        p, rp = PF()
                        proj_fm(w, rw, jj * 128, lo, hi, p, rp)
                        rope_k(p, rp, hi - lo, lo, j, lo, hi)

        def att_prologue_v(sq, st):
            ws = mkws(("V", sq, st), [6], 2)
            w, rw = ws.get()
            for j6 in range(6):
                p, rp = PF()
                proj_tm(w, rw, 256, j6 * 128, p, rp)
                cp("act", vaug[:, j6, :, 0:64], p[:, 0:256].rearrange("p (k d) -> p k d", k=4), [rp], [r_v[j6]])

        kvflag = {}

        def att_stream(sq, st, sx):
            if sx == 0:
                att_prologue_k(sq, st)
                kvflag[(sq, st, 0)] = True
            else:
                att_prologue_v(sq, st)
                kvflag[(sq, st, 1)] = True
            ws = WStream([sx, 7 + sx, sx + 2, 9 + sx, 17 + sx, 19 + sx], 2 * sx)
            for j in (sx, sx + 2):
                wq, rwq = ws.get()
                for c in range(2):
                    p, rp = PF()
                    proj_fm(wq, rwq, c * 128, 128, 640, p, rp)
                    rope(p, rp, 512, 128, qT[sx][:, c, :], [r_qT[sx][c]], sx)
                wg, rwg = ws.get()
                for t in range(4):
                    p, rp = PF()
                    proj_tm(wg, rwg, 256, 128 + t * 128, p, rp)
                    act(gr[sx][:], p[:, 0:256], AF.Tanh, [rp], [r_gr[sx]], scale=0.5)
                    stt(gs[sx][:, t, :], gr[sx][:], 1.0, p[:, 0:256], ALU.add, ALU.mult, [r_gr[sx], rp], [r_gs[sx][t]])
                while not (kvflag.get((sq, st, 0)) and kvflag.get((sq, st, 1))):
                    S.sclock[S.cur] = max(S.sclock.values()) + 1e-3
                    S.cur.pause()
                for qb in range(4):
                    gq = st * 4 + qb
                    kbs = [kb for kb in (-1, 0, 1) if 0 <= gq + kb < NT]
                    for kb in kbs:
                        kcol = (qb + 1 + kb) * 128
                        p, rp = PF()
                        first = True
                        if kb != 0:
                            mm(p[:, :], ident_b, negb4 if kb < 0 else negf4, True, False, [r_cb], [rp], False)
                            first = False
                        for r in range(4):
                            c, half = r // 2, r % 2
                            kk = kTlo if half == 0 else kThi
                            mm(p[:, r * 128:(r + 1) * 128], kk[:, j, kcol:kcol + 128], qT[sx][:, c, qb * 128:(qb + 1) * 128],
                               first, r == 3, [r_kT[j], r_qT[sx][c]], [rp], r == 3)
                            first = False
                        act(PTb[sx][:, kb + 1, :], p[:, :], AF.Exp, [rp], [r_PT[sx]], scale=0.125)
                    pv, rpv = PF()
                    first = True
                    for r in range(4):
                        for kb in kbs:
                            lastmm = (r == 3 and kb == kbs[-1])
                            mm(pv[:, r * 128:r * 128 + 65], PTb[sx][:, kb + 1, r * 128:(r + 1) * 128],
                               vaug[:, qb + 1 + kb, j, 0:65], first, lastmm, [r_PT[sx], r_v[qb + 1 + kb]], [rpv], lastmm)
                            first = False
                    pv3 = pv[:, :].rearrange("p (r d) -> p r d", r=4)
                    dn = sm[:, 8 + 4 * sx:12 + 4 * sx]
                    tt("dve", dn, pv3[:, :, 64], ptab[:, P_SINK + 4 * j:P_SINK + 4 * j + 4], ALU.add,
                       [rpv, r_ptab], [r_dn[sx]])
                    S.op("dve", lambda E, dn=dn: E.reciprocal(dn, dn), [r_dn[sx]], [r_dn[sx]])
                    tt("pool", gr[sx][:].rearrange("p (r d) -> p r d", r=4), gs[sx][:, qb, :].rearrange("p (r d) -> p r d", r=4),
                       bc(dn, [128, 4, 64], 2), ALU.mult, [r_gs[sx][qb], r_dn[sx]], [r_gr[sx]])
                    stt(attg[sx][:].rearrange("p (r d) -> p r d", r=4), pv3[:, :, 0:64], 0.5,
                        gr[sx][:].rearrange("p (r d) -> p r d", r=4), ALU.mult, ALU.mult, [rpv, r_gr[sx]], [r_attg[sx]])
                    S.hold += 1
                    pbt, rpbt = PB()
                    for c in range(2):
                        tp(pbt[:, c * 128:(c + 1) * 128], attg[sx][:, c * 128:(c + 1) * 128], [r_attg[sx]], [rpbt], c == 1)
                    for c in range(2):
                        if c == 1:
                            S.hold -= 1
                        cp("act", mixT[:, 2 * j + c, qb * 128:(qb + 1) * 128], pbt[:, c * 128:(c + 1) * 128],
                           [rpbt], [r_mix[2 * j + c][qb]])
            w, rw = ws.get()
            for cc in range(2):
                p, rp = PF()
                proj_fm(w, rw, cc * 128, 128, 640, p, rp)
                cp("act", qT[sx][:, cc, :], p[:, :], [rp], [r_qT[sx][cc]])
            w, rw = ws.get()
            for t in range(4):
                p, rp = PF()
                proj_tm(w, rw, 256, 128 + t * 128, p, rp)
                act(gr[sx][:], p[:, 0:256], AF.Tanh, [rp], [r_gr[sx]], scale=0.5)
                stt(gs[sx][:, t, :], gr[sx][:], 1.0, p[:, 0:256], ALU.add, ALU.mult, [r_gr[sx], rp], [r_gs[sx][t]])
            sc = float(1.0 / np.sqrt(128.0))
            for cc in range(2):
                hm = 2 * sx + cc
                for mb in range(2):
                    p, rp = PF()
                    m0 = sq * 256 + mb * 128
                    mm(p[:, :], mkT[:, hm, m0:m0 + 128], qT[sx][:, cc, :], True, True, [r_mkT, r_qT[sx][cc]], [rp], True)
                    act(PTb[sx][:, mb, :], p[:, :], AF.Exp, [rp], [r_PT[sx]], scale=sc)
                for tp2 in range(2):
                    pv, rpv = PF()
                    first = True
                    for tl in range(2):
                        t = tp2 * 2 + tl
                        for mb in range(2):
                            lastmm = (tl == 1 and mb == 1)
                            mm(pv[:, tl * 256:tl * 256 + 129], PTb[sx][:, mb, t * 128:(t + 1) * 128],
                               mvaug[:, sq * 2 + mb, hm, 0:129], first, lastmm, [r_PT[sx], r_mv], [rpv], lastmm)
                            first = False
                    pv3 = pv[:, :].rearrange("p (t d) -> p t d", t=2)
                    dn = sm[:, 24 + 2 * sx:26 + 2 * sx]
                    S.op("dve", lambda E, dn=dn, pv3=pv3: E.reciprocal(dn, pv3[:, :, 128]), [rpv], [r_dn[sx]])
                    for tl in range(2):
                        t = tp2 * 2 + tl
                        ts("pool", gr[sx][:, 0:128], gs[sx][:, t, cc * 128:(cc + 1) * 128], dn[:, tl:tl + 1], 0.5, ALU.mult, ALU.mult,
                           [r_gs[sx][t], r_dn[sx]], [r_gr[sx]])
                        tt("dve", attg[sx][:, 0:128], pv[:, tl * 256:tl * 256 + 128], gr[sx][:, 0:128], ALU.mult,
                           [rpv, r_gr[sx]], [r_attg[sx]])
                        S.hold += 1
                        pbt, rpbt = PB()
                        tp(pbt[:, 0:128], attg[sx][:, 0:128], [r_attg[sx]], [rpbt], True)
                        S.hold -= 1
                        cp("act", mixT[:, 12 + hm, t * 128:(t + 1) * 128], pbt[:, 0:128], [rpbt], [r_mix[12 + hm][t]])
            yield

        def ssd_dt(sq, st):
            p, rp = PF()
            for t in range(4):
                for kc in range(8):
                    mm(p[:, t * 16:(t + 1) * 16], hnT[:, kc, 128 + t * 128:256 + t * 128], wdt[:, kc, :],
                       t == 0 and kc == 0, t == 3 and kc == 7, [r_wdt] + hn_regs(128, 640), [rp], t == 3 and kc == 7)
            d3 = dts[:]
            tt("dve", d3, p[:, 0:64].rearrange("p (t h) -> p t h", t=4), bc(ptab[:, P_DTB:P_DTB + 16], [128, 4, 16], 1),
               ALU.add, [rp, r_ptab], [r_dts])
            act(d3, d3, AF.Exp, [r_dts], [r_dts])
            act(d3, d3, AF.Ln, [r_dts], [r_dts], bias=1.0)
            tt("dve", adt[:], d3, bc(ptab[:, P_ALOG:P_ALOG + 16], [128, 4, 16], 1), ALU.mult, [r_dts, r_ptab], [r_adt])
            p, rp = PF()
            first = True
            for t in range(4):
                for (col, mat, a0, n) in ((0, C_SGT, 0, 8), (8, C_SLE, 0, 8), (16, C_SLT, 8, 8), (24, C_SGE, 8, 8),
                                          (32, C_ONES, 0, 16)):
                    lastmm = (t == 3 and col == 32)
                    mm(p[:, t * 48 + col:t * 48 + col + n], cf[:, mat:mat + 128], adt[:, t, a0:a0 + n], first, lastmm,
                       [r_cf, r_adt], [rp], lastmm)
                    first = False
            act(exs[:], p[:, 0:192].rearrange("p (t c) -> p t c", t=4), AF.Exp, [rp], [r_exs])
            tt("dve", wfb[:, :, 0:8], dts[:, :, 0:8], exs[:, :, 0:8], ALU.mult, [r_dts, r_exs], [r_wfb])
            tt("dve", wfb[:, :, 8:16], dts[:, :, 8:16], exs[:, :, 16:24], ALU.mult, [r_dts, r_exs], [r_wfb])

        def ssd_z(sq, st, ws):
            for u in range(2):
                w, rw = ws.get()
                for t in range(4):
                    p, rp = PF()
                    proj_tm(w, rw, 256, 128 + t * 128, p, rp)
                    act(cth[0][:], p[:, 0:256], AF.Tanh, [rp], [r_cth[0]], scale=0.5)
                    stt(zs[:, t, u * 256:(u + 1) * 256], cth[0][:], 1.0, p[:, 0:256], ALU.add, ALU.mult, [r_cth[0], rp], [r_zs[t]])
            yield

        def ssd_conv(sq, st, us, ws, ia):
            for u in us:
                w, rw = ws.get()
                for cc in range(2):
                    ch = u * 2 + cc
                    for hf in range(2):
                        lo = 126 + hf * 256
                        p, rp = PF()
                        proj_fm(w, rw, cc * 128, lo, lo + 260, p, rp)
                        acc = cacc[ia]
                        act(acc[:], p[:, 0:256], AF.Identity, [rp, r_ptab], [r_cacc[ia]],
                            bias=ptab[:, P_CB + ch:P_CB + ch + 1], scale=ptab[:, P_CW + ch * 5:P_CW + ch * 5 + 1])
                        for k in range(1, 5):
                            stt(acc[:], p[:, k:k + 256], ptab[:, P_CW + ch * 5 + k:P_CW + ch * 5 + k + 1], acc[:],
                                ALU.mult, ALU.add, [rp, r_ptab, r_cacc[ia]], [r_cacc[ia]])
                        if ch < 4:
                            act(cth[ia][:], acc[:], AF.Tanh, [r_cacc[ia]], [r_cth[ia]])
                            stt(xsTt[ia][:], cth[ia][:], 1.0, acc[:], ALU.add, ALU.mult, [r_cth[ia], r_cacc[ia]], [r_xsTt[ia]])
                            S.hold += 1
                            pbt, rpbt = PB()
                            for tl in range(2):
                                tp(pbt[:, tl * 128:(tl + 1) * 128], xsTt[ia][:, tl * 128:(tl + 1) * 128], [r_xsTt[ia]],
                                   [rpbt], tl == 1)
                            for tl in range(2):
                                if tl == 1:
                                    S.hold -= 1
                                t = hf * 2 + tl
                                cp("act", xs_tm[:, t, ch * 128:(ch + 1) * 128], pbt[:, tl * 128:(tl + 1) * 128],
                                   [rpbt], [r_xs[t]])
                        else:
                            q = ch - 4
                            act(cth[ia][:], acc[:], AF.Tanh, [r_cacc[ia]], [r_cth[ia]])
                            stt(bcT[:, q, hf * 256:(hf + 1) * 256], cth[ia][:], 1.0, acc[:], ALU.add, ALU.mult,
                                [r_cth[ia], r_cacc[ia]], [r_bcT[q][hf * 2], r_bcT[q][hf * 2 + 1]])
                            if q < 2:
                                S.hold += 1
                                pbt, rpbt = PB()
                                for tl in range(2):
                                    t = hf * 2 + tl
                                    tp(pbt[:, tl * 128:(tl + 1) * 128], bcT[:, q, t * 128:(t + 1) * 128], [r_bcT[q][t]],
                                       [rpbt], tl == 1)
                                for tl in range(2):
                                    if tl == 1:
                                        S.hold -= 1
                                    t = hf * 2 + tl
                                    cp("act", bm_tm[:, t, q * 128:(q + 1) * 128], pbt[:, tl * 128:(tl + 1) * 128],
                                       [rpbt], [r_bm[t]])
                        yield

        def xscale(dst_i, t, col_ap, r_col):
            tt("pool", xsc[dst_i][:].rearrange("p (h d) -> p h d", h=8), xs_tm[:, t, :].rearrange("p (h d) -> p h d", h=8),
               bc(col_ap, [128, 8, 64], 2), ALU.mult, [r_xs[t], r_col], [r_xsc[dst_i]])

        def state_update(t, d, xdd_i):
            p, rp = PF()
            for g in range(2):
                mm(p[:, g * 256:(g + 1) * 256], bm_tm[:, t, g * 128:(g + 1) * 128], xsc[xdd_i][:, g * 256:(g + 1) * 256],
                   g == 0, g == 1, [r_bm[t], r_xsc[xdd_i]], [rp], g == 1)
            cd = exs[:, t, 32 + d * 8:40 + d * 8]
            S3 = Sst[d][:].rearrange("p (h d) -> p h d", h=8)
            tt("dve", S3, S3, bc(cd, [128, 8, 64], 2), ALU.mult, [r_Sst[d], r_exs], [r_Sst[d]])
            tt("dve", Sst[d][:], Sst[d][:], p[:, :], ALU.add, [r_Sst[d], rp], [r_Sst[d]])

        def pass1_chunk(sq, st, t):
            g = st * 4 + t
            i = nxt("sbf", 2)
            cp("act", Sbf[i][:], Sst[1][:], [r_Sst[1]], [r_Sbf[i]])
            S.dma("pool", dprev[g, :, :], Sbf[i][:], "pst%d" % i, reads=[r_Sbf[i]], writes=[r_dprev[g]])
            xscale(3, t, wfb[:, t, 8:16], r_wfb)
            state_update(t, 1, 3)

        def ssd_chunk(sq, st, t):
            g = st * 4 + t
            tc0 = t * 128
            ip = nxt("pbuf", 2)
            S.dma("sp", pbuf[ip][:], dprev[g, :, :], "pld%d" % ip, reads=[r_dprev[g]], writes=[r_pbuf[ip]])
            pcb, rpcb = PF()
            for gI in range(2):
                mm(pcb[:, gI * 128:(gI + 1) * 128], bcT[:, gI, tc0:tc0 + 128], bcT[:, 2 + gI, tc0:tc0 + 128], gI == 0, gI == 1,
                   [r_bcT[gI][t], r_bcT[2 + gI][t]], [rpcb], gI == 1)
            cp("act", cbs[:], pcb[:, 0:256], [rpcb], [r_cbs])
            for d in range(2):
                for gI in range(2):
                    ir = nxt("rb", 2)
                    U = cf[:, C_SLE:C_SLE + 128] if d == 0 else cf[:, C_SGE:C_SGE + 128]
                    LT = cf[:, C_SGT:C_SGT + 128] if d == 0 else cf[:, C_SLT:C_SLT + 128]
                    tt("pool", Rb[ir][:], bc(U, [128, 4, 128], 1),
                       bc(adt[:, t, d * 8 + gI * 4:d * 8 + gI * 4 + 4], [128, 4, 128], 2), ALU.mult, [r_cf, r_adt], [r_Rb[ir]])
                    p, rp = PF()
                    mm(p[:, :], ident_b, negf4 if d == 0 else negb4, True, False, [r_cb], [rp], False)
                    mm(p[:, :], LT, Rb[ir][:].rearrange("p h l -> p (h l)"), False, True, [r_cf, r_Rb[ir]], [rp], True)
                    act(Lt[ir][:].rearrange("p h l -> p (h l)"), p[:, :], AF.Exp, [rp], [r_Lt[ir]])
                    mi = d * 2 + gI
                    tt("dve", Mt[mi][:], Lt[ir][:], bc(cbs[:, gI * 128:(gI + 1) * 128], [128, 4, 128], 1), ALU.mult,
                       [r_Lt[ir], r_cbs], [r_Mt[mi]])
                    yield
            xscale(0, t, dts[:, t, 0:8], r_dts)
            xscale(1, t, dts[:, t, 8:16], r_dts)
            xscale(2, t, wfb[:, t, 0:8], r_wfb)
            xscale(4, t, ptab[:, P_DSK:P_DSK + 8], r_ptab)
            isb = (cnt["sbf"] - 1) % 2 if cnt["sbf"] > 0 else 0
            e_f = exs[:, t, 8:16]
            e_b = exs[:, t, 24:32]
            v3 = lambda ap: ap.rearrange("p (h d) -> p h d", h=8)
            for d in range(2):
                po, rpo = PF()
                prev_ap, r_prev = (Sbf[isb], r_Sbf[isb]) if d == 0 else (pbuf[ip], r_pbuf[ip])
                for gI in range(2):
                    mm(po[:, gI * 256:(gI + 1) * 256], bcT[:, 2 + gI, tc0:tc0 + 128], prev_ap[:, gI * 256:(gI + 1) * 256],
                       gI == 0, gI == 1, [r_bcT[2 + gI][t], r_prev], [rpo], gI == 1)
                if d == 0:
                    tt("dve", v3(y1[:]), v3(po[:, :]), bc(e_f, [128, 8, 64], 2), ALU.mult, [rpo, r_exs], [r_y1])
                else:
                    tt("dve", v3(y2[:]), v3(po[:, :]), bc(e_b, [128, 8, 64], 2), ALU.mult, [rpo, r_exs], [r_y2])
            py, rpy = PF()
            mm(py[:, :], ident_b, xsc[4][:], True, False, [r_cb, r_xsc[4]], [rpy], False)
            for h in range(8):
                gI, r = h // 4, h % 4
                for d in range(2):
                    lastmm = (h == 7 and d == 1)
                    mm(py[:, h * 64:(h + 1) * 64], Mt[d * 2 + gI][:, r, :], xsc[d][:, h * 64:(h + 1) * 64], False, lastmm,
                       [r_Mt[d * 2 + gI], r_xsc[d]], [rpy], lastmm)
            tt("dve", y1[:], y1[:], y2[:], ALU.add, [r_y1, r_y2], [r_y1])
            tt("dve", y2[:], py[:, :], y1[:], ALU.add, [rpy, r_y1], [r_y2])
            stt(y1[:], y2[:], 0.5, zs[:, t, :], ALU.mult, ALU.mult, [r_y2, r_zs[t]], [r_y1])
            yield
            ss = sm[:, 16:18]
            for gI in range(2):
                act(junk[:, 0:256], y1[:, gI * 256:(gI + 1) * 256], AF.Square, [r_y1], [r_junk, r_ssd_ss], accum=ss[:, gI:gI + 1])
            rstd_from_ss(ss, 2, 1.0 / 256, [r_ssd_ss])
            for gI in range(2):
                stt(yn[:, gI * 256:(gI + 1) * 256], y1[:, gI * 256:(gI + 1) * 256], ss[:, gI:gI + 1],
                    ptab[:, P_SNW + gI * 256:P_SNW + (gI + 1) * 256], ALU.mult, ALU.mult, [r_y1, r_ssd_ss, r_ptab], [r_yn])
            pbt, rpbt = PB()
            for c in range(4):
                tp(pbt[:, c * 128:(c + 1) * 128], yn[:, c * 128:(c + 1) * 128], [r_yn], [rpbt], c == 3)
            for c in range(4):
                cp("act", mixT[:, 8 + c, tc0:tc0 + 128], pbt[:, c * 128:(c + 1) * 128], [rpbt], [r_mix[8 + c][t]])
            yield
            state_update(t, 0, 2)
            i = nxt("sbf", 2)
            cp("act", Sbf[i][:], Sst[0][:], [r_Sst[0]], [r_Sbf[i]])

        def mem_attn(sq, st, ws):
            for u in range(2):
                w, rw = ws.get()
                for cc in range(2):
                    hm = u * 2 + cc
                    p, rp = PF()
                    proj_fm(w, rw, cc * 128, 128, 640, p, rp)
                    cp("act", qmT[:, hm, :], p[:, :], [rp], [r_qm[hm]])
                    yield
            for u in range(2):
                w, rw = ws.get()
                for t in range(4):
                    p, rp = PF()
                    proj_tm(w, rw, 256, 128 + t * 128, p, rp)
                    act(gms[:, t, u * 256:(u + 1) * 256], p[:, 0:256], AF.Silu, [rp], [r_gms[t]])
                    yield
            sc = 1.0 / np.sqrt(128.0)
            for hm in range(4):
                ip = nxt("ptm", 2)
                for mb in range(2):
                    p, rp = PF()
                    m0 = sq * 256 + mb * 128
                    mm(p[:, :], mkT[:, hm, m0:m0 + 128], qmT[:, hm, :], True, True, [r_mkT, r_qm[hm]], [rp], True)
                    act(PTm[ip][:, mb, :], p[:, :], AF.Exp, [rp], [r_PTm[ip]], scale=float(sc))
                yield
                for tp2 in range(2):
                    pv, rpv = PF()
                    first = True
                    for tl in range(2):
                        t = tp2 * 2 + tl
                        for mb in range(2):
                            lastmm = (tl == 1 and mb == 1)
                            mm(pv[:, tl * 256:tl * 256 + 129], PTm[ip][:, mb, t * 128:(t + 1) * 128],
                               mvaug[:, sq * 2 + mb, hm, 0:129], first, lastmm, [r_PTm[ip], r_mv], [rpv], lastmm)
                            first = False
                    pv3 = pv[:, :].rearrange("p (t d) -> p t d", t=2)
                    dn = sm[:, 24:26]
                    S.op("dve", lambda E, dn=dn, pv3=pv3: E.reciprocal(dn, pv3[:, :, 128]), [rpv], [r_mem_dn])
                    for tl in range(2):
                        t = tp2 * 2 + tl
                        ts("pool", grm[tl][:, 0:128], gms[:, t, hm * 128:(hm + 1) * 128], dn[:, tl:tl + 1], None, ALU.mult, None,
                           [r_gms[t], r_mem_dn], [r_grm[tl]])
                        tt("dve", attgm[tl][:, 0:128], pv[:, tl * 256:tl * 256 + 128], grm[tl][:, 0:128], ALU.mult, [rpv, r_grm[tl]], [r_attgm[tl]])
                        pbt, rpbt = PB()
                        tp(pbt[:, 0:128], attgm[tl][:, 0:128], [r_attgm[tl]], [rpbt], True)
                        cp("act", mixT[:, 12 + hm, t * 128:(t + 1) * 128], pbt[:, 0:128], [rpbt], [r_mix[12 + hm][t]])
                    yield

        out_toks = []

        r_oss = [r_out_ss, Reg()]

        def out_proj(sq, st, h=None):
            hh = 0 if h is None else h
            for t in (range(4) if h is None else (2 * h, 2 * h + 1)):
                g = st * 4 + t
                i = nxt("xr", 2) if h is None else h
                S.dma("sp", xres[i][:], dx[sq, g * 128:(g + 1) * 128, :], "xr%d" % i, writes=[r_xres[i]])
                for nb in range(2):
                    p, rp = PF()
                    for c in range(16):
                        mm(p[:, :], mixT[:, c, t * 128:(t + 1) * 128], wout[:, c, nb * 512:(nb + 1) * 512], c == 0, c == 15,
                           [r_mix[c][t], r_wout], [rp], c == 15)
                    tt("dve", xres[i][:, nb * 512:(nb + 1) * 512], xres[i][:, nb * 512:(nb + 1) * 512], p[:, :], ALU.add,
                       [r_xres[i], rp], [r_xres[i]])
                ss = sm[:, 32 + hh:33 + hh]
                jk, rjk = (junk, r_junk) if hh == 0 else (junk1, r_y1)
                act(jk[:], xres[i][:], AF.Square, [r_xres[i]], [rjk, r_oss[hh]], accum=ss)
                rstd_from_ss(ss, 1, 1.0 / D, [r_oss[hh]])
                stt(xres[i][:], xres[i][:], ss, ptab[:, P_NOW:P_NOW + 1024], ALU.mult, ALU.mult,
                    [r_xres[i], r_oss[hh], r_ptab], [r_xres[i]])
                tok = S.dma("pool", dout[sq, g * 128:(g + 1) * 128, :], xres[i][:], "ost%d" % i, reads=[r_xres[i]])
                out_toks.append(tok)

        class _Stop(Exception):
            pass

        def stage(n):
            if STOP is not None and n >= STOP:
                raise _Stop()

        try:
          stage(0)
          def run(g):
              for _ in g:
                  pass

          def corun(*streams):
              cos = []
              base = min(S.efree[e] for e in ("pe", "act", "dve", "pool"))
              for g, pool in streams:
                  c = Co(lambda g=g: run(g))
                  c.pool = pool
                  S.sclock[c] = base
                  cos.append(c)
              alive = list(cos)
              while alive:
                  c = min(alive, key=lambda c: S.sclock[c])
                  S.cur = c
                  c.step()
                  S.cur = None
                  if c.done:
                      alive.remove(c)

          def gen(fn, *a):
              fn(*a)
              yield

          def ssd_stream(sq, st):
              ws = mkws(("B", sq, st), [11, 12, 13, 14, 15, 16], 4)
              ssd_dt(sq, st)
              yield from ssd_z(sq, st, ws)
              yield from ssd_conv(sq, st, [0, 1, 2, 3], ws, 0)
              for t in range(4):
                  yield from ssd_chunk(sq, st, t)

          for sq in range(NSEQ):
            S.op("pool", lambda E: E.memset(Sst[1][:], 0.0), [], [r_Sst[1]])
            def upd(sq, st):
                for t in reversed(range(4)):
                    pass1_chunk(sq, st, t)

            hn_stage(sq, NST - 1, None, "halo")
            corun((gen(hn_stage, sq, NST - 1, None, 0), "H1"), (gen(hn_stage, sq, NST - 1, None, 1), "H2"))
            for st in reversed(range(NST)):
                stage(1)
                corun((ssd_conv(sq, st, [0], mkws(("X", sq, st), [13], 0), 0), "X"),
                      (ssd_conv(sq, st, [1], mkws(("Y", sq, st), [14], 2), 1), "Y"),
                      (ssd_conv(sq, st, [2], mkws(("Z", sq, st), [15], 4), 2), "Z"),
                      (gen(ssd_dt, sq, st), "D"))
                stage(2)
                if st > 0:
                    prefetch_ws(("X", sq, st - 1), [13], 0)
                    prefetch_ws(("Y", sq, st - 1), [14], 2)
                    prefetch_ws(("Z", sq, st - 1), [15], 4)
                    hn_stage(sq, st - 1, st, "halo")
                    corun((gen(upd, sq, st), "X"),
                          (gen(hn_stage, sq, st - 1, st, 0), "H1"), (gen(hn_stage, sq, st - 1, st, 1), "H2"))
                else:
                    upd(sq, st)
                stage(3)
            S.op("pool", lambda E: E.memset(Sst[0][:], 0.0), [], [r_Sst[0]])
            i = nxt("sbf", 2)
            S.op("pool", lambda E, i=i: E.memset(Sbf[i][:], 0.0), [], [r_Sbf[i]])
            for st in range(NST):
                if st == 0:
                    att_cossin(sq, st)
                corun((att_stream(sq, st, 0), "A1"), (att_stream(sq, st, 1), "A2"), (ssd_stream(sq, st), "B"))
                if st + 1 < NST:
                    prefetch_ws(("K", sq, st + 1), [4, 5], 0)
                    prefetch_ws(("V", sq, st + 1), [6], 2)
                    prefetch_ws(("B", sq, st + 1), [11, 12, 13, 14, 15, 16], 4)
                    att_cossin(sq, st + 1)
                    hn_stage(sq, st + 1, st, "halo")
                    corun((gen(out_proj, sq, st, 0), "O1"), (gen(out_proj, sq, st, 1), "O2"),
                          (gen(hn_stage, sq, st + 1, st, 0), "H1"), (gen(hn_stage, sq, st + 1, st, 1), "H2"))
                else:
                    out_proj(sq, st)
        except _Stop:
            pass
        for e in ("pe", "act", "dve", "pool"):
            if S.cnt[e] > 0:
                nm, h = S.esem[e]
                S.finish_waits("pool", [(nm, h, S.cnt[e])])
        for k, (nm, h, v) in S.dsem.items():
            S.finish_waits("pool", [(nm, h, v)])
        fin = {}
        for nm, h, v in out_toks:
            if nm not in fin or fin[nm][2] < v:
                fin[nm] = (nm, h, v)
        S.finish_waits("pool", list(fin.values()))

        block = es.enter_context(nc.Block())

        @block.tensor
        def _(E):
            for f in S.prog["pe"]:
                f(E)

        @block.scalar
        def _(E):
            for f in S.prog["act"]:
                f(E)

        @block.vector
        def _(E):
            for f in S.prog["dve"]:
                f(E)

        @block.gpsimd
        def _(E):
            for f in S.prog["pool"]:
                f(E)

        @block.sync
        def _(E):
            for f in S.prog["sp"]:
                f(E)
    return nc


def _tables(L):
    j = np.arange(128)[:, None]
    s = np.arange(128)[None, :]
    cf = np.concatenate([(j > s), (j <= s), (j < s), (j >= s), np.ones((128, 128), bool)], axis=1).astype(np.float32)
    ident = np.eye(128, dtype=np.float32)
    negf = np.where(j > s, NEG, 0.0).astype(np.float32)
    negb = np.where(j < s, NEG, 0.0).astype(np.float32)
    rp = np.zeros((128, 128), np.float32)
    for m in range(128):
        if (m % 64) < 32:
            rp[m + 32, m] = -1.0
        else:
            rp[m - 32, m] = 1.0
    cb = np.concatenate([ident, np.tile(negf, (1, 4)), np.tile(negb, (1, 4)), rp], axis=1).astype(np.float32)
    inv_freq = (1.0 / (np.float32(10000.0) ** (np.arange(0, 64, 2, dtype=np.float32) / np.float32(64.0)))).astype(np.float32)
    pos = np.arange(L, dtype=np.float32)
    ang = (pos[None, :] * inv_freq[np.arange(128) % 32][:, None]).astype(np.float32)
    cosT = np.zeros((128, L + 256), np.float32)
    sinT = np.zeros((128, L + 256), np.float32)
    cosT[:, 128:128 + L] = np.cos(ang)
    sinT[:, 128:128 + L] = np.sin(ang)
    return cf, cb, cosT, sinT


def _prep_shared(inp, L):
    w_in = np.asarray(inp["w_in"], np.float32)[0]
    cols = []
    for j in range(4):
        cols.append(np.arange(256 * j, 256 * j + 256))
    kb = 1024
    for u in range(2):
        a = kb + (2 * u) * 64 + np.arange(64)
        b = kb + (2 * u + 1) * 64 + np.arange(64)
        cols.append(np.concatenate([a, a, b, b]))
    cols.append(np.arange(1280, 1536))
    for j in range(4):
        cols.append(np.arange(1536 + 256 * j, 1536 + 256 * j + 256))
    for j in range(2):
        cols.append(np.arange(2560 + 256 * j, 2560 + 256 * j + 256))
    for j in range(4):
        cols.append(np.arange(3072 + 256 * j, 3072 + 256 * j + 256))
    for j in range(2):
        cols.append(np.arange(4112 + 256 * j, 4112 + 256 * j + 256))
    for j in range(2):
        cols.append(np.arange(4624 + 256 * j, 4624 + 256 * j + 256))
    assert len(cols) == NU
    w3 = w_in.reshape(8, 128, -1)
    win_u = np.stack([w3[:, :, c].transpose(1, 0, 2).reshape(128, 2048) for c in cols], axis=0)
    wdt = w3[:, :, 4096:4112].transpose(1, 0, 2).reshape(128, 128)
    wout_p = np.asarray(inp["w_out"], np.float32)[0].reshape(16, 128, 1024).transpose(1, 0, 2).reshape(128, 16 * 1024)
    wmem_p = np.asarray(inp["w_mem_kv"], np.float32)[0].reshape(8, 128, 1024).transpose(1, 0, 2).reshape(128, 8 * 1024)
    pt = np.zeros((128, PT), np.float32)
    pt[:, P_NIN:P_NIN + 8] = np.asarray(inp["norm_in_w"], np.float32)[0].reshape(8, 128).T
    pt[:, P_NMEM:P_NMEM + 8] = np.asarray(inp["norm_mem_w"], np.float32).reshape(8, 128).T
    cw = np.asarray(inp["conv_w"], np.float32)[0]
    pt[:, P_CW:P_CW + 40] = cw.reshape(5, 8, 128).transpose(2, 1, 0).reshape(128, 40)
    pt[:, P_CB:P_CB + 8] = np.asarray(inp["conv_b"], np.float32)[0].reshape(8, 128).T
    pt[:, P_DTB:P_DTB + 16] = np.asarray(inp["dt_bias"], np.float32)[0].reshape(1, 16)
    pt[:, P_ALOG:P_ALOG + 16] = np.asarray(inp["a_log"], np.float32)[0].reshape(1, 16)
    pt[:, P_DSK:P_DSK + 8] = np.asarray(inp["d_skip"], np.float32)[0].reshape(1, 8)
    pt[:, P_SNW:P_SNW + 512] = np.asarray(inp["ssd_norm_w"], np.float32)[0].reshape(1, 512)
    pt[:, P_NOW:P_NOW + 1024] = np.asarray(inp["norm_out_w"], np.float32).reshape(1, 1024)
    pt[:, P_SINK:P_SINK + 16] = np.asarray(inp["attn_sink"], np.float32)[0].reshape(1, 16)
    pt[:, P_EPS] = 1e-6
    pt[:, P_NEGH:P_NEGH + 2] = -0.5
    cf, cb, cosT, sinT = _tables(L)
    return {"win_u": np.ascontiguousarray(win_u), "wdt": np.ascontiguousarray(wdt),
            "wout_p": np.ascontiguousarray(wout_p), "wmem_p": np.ascontiguousarray(wmem_p), "ptab": pt,
            "cf32": cf, "cb16": cb, "cosT": cosT, "sinT": sinT}


def run(inp, n_cores):
    x = np.asarray(inp["x"], np.float32)
    mem = np.asarray(inp["mem"], np.float32)
    B, L, _ = x.shape
    assert B % n_cores == 0
    nseq = B // n_cores
    shared = _prep_shared(inp, L)
    nc = build_nc(L, nseq)
    in_maps = []
    for i in range(n_cores):
        m = dict(shared)
        m["x"] = np.ascontiguousarray(x[i * nseq:(i + 1) * nseq])
        m["mem"] = np.ascontiguousarray(mem[i * nseq:(i + 1) * nseq])
        in_maps.append(m)
    res = run_bass_kernel_spmd(nc, in_maps, core_ids=list(range(n_cores)))
    return np.concatenate([np.asarray(r["out"], np.float32) for r in res.results], axis=0)


def kernel(**inputs):
    return run(inputs, 8)
```
